# Optimizing a Trainium2 kernel written in Bass

```python
import math
import jax, jax.numpy as jnp
from jax import lax
import numpy as np

D_MODEL = 1024
BATCH = 32
SEQ = 2048
DEPTH = 4
DEC_BATCH = 8
DEC_SEQ = 32
PAST_LEN = 1024

CHUNK = 64
Q_BLOCK = 128
D_MIX = D_MODEL
DA_HEADS = 4
DA_WIDTH = D_MIX // 2
DA_DV = DA_WIDTH // DA_HEADS
DA_DK = DA_DV // 2
ROT_DIM = DA_DK // 4
ROPE_THETA = 500000.0
GM_WIDTH = D_MIX - DA_WIDTH
GM_GROUPS = 4
GM_CH = GM_WIDTH // GM_GROUPS
GM_CHUNK = 128
MEM_LEN = 256
X_HEADS = 4
X_DH = D_MODEL // X_HEADS
D_FF = 2816
CONV_W = 3
EPS = 1e-6
DA_QK_COLS = DA_HEADS * 2 * DA_DK
IN_COLS = 2 * DA_QK_COLS + DA_WIDTH + 2 * GM_WIDTH
NEG = float(np.finfo(np.float32).min)

kernel_name = "hybrid_diffattn_gmlp_streaming_step"


def rmsnorm(x, g):
    xf = x.astype(jnp.float32)
    y = xf * lax.rsqrt(jnp.mean(xf * xf, axis=-1, keepdims=True) + EPS)
    return (y * g.astype(jnp.float32)).astype(x.dtype)


def partial_rope(x, pos):
    half = ROT_DIM // 2
    inv = ROPE_THETA ** (-jnp.arange(half, dtype=jnp.float32) / half)
    ang = pos.astype(jnp.float32)[:, None] * inv[None, :]
    cos = jnp.cos(ang)[:, None, None, :]
    sin = jnp.sin(ang)[:, None, None, :]
    x1 = x[..., :half].astype(jnp.float32)
    x2 = x[..., half:ROT_DIM].astype(jnp.float32)
    rot = jnp.concatenate([x1 * cos - x2 * sin, x2 * cos + x1 * sin], axis=-1)
    return jnp.concatenate([rot.astype(x.dtype), x[..., ROT_DIM:]], axis=-1)


def diff_attn_block(q, k, v, q_pos, k_pos, lam):
    s = jnp.einsum('bqhmd,bkhmd->bhmqk', q, k,
                   preferred_element_type=jnp.float32) * (DA_DK ** -0.5)
    mask = (k_pos[None, :] // CHUNK) <= (q_pos[:, None] // CHUNK)
    s = jnp.where(mask, s, NEG)
    p = jax.nn.softmax(s, axis=-1)
    a = p[:, :, 0] - lam * p[:, :, 1]
    return jnp.einsum('bhqk,bkhd->bqhd', a.astype(v.dtype), v)


def diff_attention(q, k, v, q_pos, k_pos, lam):
    B, Sq = q.shape[0], q.shape[1]
    if Sq > Q_BLOCK and Sq % Q_BLOCK == 0:
        nb = Sq // Q_BLOCK
        qb = q.reshape(B, nb, Q_BLOCK, DA_HEADS, 2, DA_DK).transpose(1, 0, 2, 3, 4, 5)
        pb = q_pos.reshape(nb, Q_BLOCK)
        out = lax.map(lambda a: diff_attn_block(a[0], k, v, a[1], k_pos, lam), (qb, pb))
        return out.transpose(1, 0, 2, 3, 4).reshape(B, Sq, DA_HEADS, DA_DV)
    return diff_attn_block(q, k, v, q_pos, k_pos, lam)


def spatial_gate(u, gv, w_s, b_s):
    B, S = u.shape[0], u.shape[1]
    L = min(S, GM_CHUNK)
    n = S // L
    tri = jnp.tril(jnp.ones((L, L), dtype=bool))
    w = jnp.where(tri[None], w_s[:, :L, :L], 0.0)
    vc = gv.reshape(B, n, L, GM_GROUPS, GM_CH)
    s = jnp.einsum('gts,bnsgc->bntgc', w, vc) + b_s[:, :L].T[None, None, :, :, None]
    return u * s.reshape(B, S, GM_WIDTH)


def mem_kv(mem, norm_g, wk, wv, kg):
    B, M, _ = mem.shape
    m = rmsnorm(mem, norm_g)
    k = rmsnorm((m @ wk).reshape(B, M, X_HEADS, X_DH), kg)
    v = (m @ wv).reshape(B, M, X_HEADS, X_DH)
    return k, v


def cross_attend(h, mk, mv, wq, wo, qg):
    B, S, _ = h.shape
    q = rmsnorm((h @ wq).reshape(B, S, X_HEADS, X_DH), qg)
    s = jnp.einsum('bqhd,bkhd->bhqk', q, mk,
                   preferred_element_type=jnp.float32) * (X_DH ** -0.5)
    p = jax.nn.softmax(s, axis=-1)
    o = jnp.einsum('bhqk,bkhd->bqhd', p.astype(mv.dtype), mv).reshape(B, S, D_MODEL)
    return o @ wo


def conv_ffn(h, hist, w_up, conv_w, conv_b, w_down):
    S = h.shape[1]
    g, up = jnp.split(h @ w_up, 2, axis=-1)
    gp = jnp.concatenate([hist, g], axis=1)
    c = conv_b + sum(conv_w[j] * gp[:, j:j + S] for j in range(CONV_W))
    a = jax.nn.silu(c) * up
    return a @ w_down, gp[:, -(CONV_W - 1):]


def trunk_layer(x, pos, past_k, past_v, mem_k, mem_v, conv_hist, w, lam_init):
    B, S, _ = x.shape
    h = rmsnorm(x, w['norm_mix_g'])
    z = h @ w['w_in']
    c1 = DA_QK_COLS
    c2 = 2 * DA_QK_COLS
    c3 = c2 + DA_WIDTH
    c4 = c3 + GM_WIDTH
    q, k, v, u, gv = jnp.split(z, [c1, c2, c3, c4], axis=-1)
    q = partial_rope(rmsnorm(q.reshape(B, S, DA_HEADS, 2, DA_DK), w['da_q_norm_g']), pos)
    k = partial_rope(rmsnorm(k.reshape(B, S, DA_HEADS, 2, DA_DK), w['da_k_norm_g']), pos)
    v = v.reshape(B, S, DA_HEADS, DA_DV)
    if past_k is None:
        k_all, v_all, k_pos = k, v, pos
    else:
        k_all = jnp.concatenate([past_k, k], axis=1)
        v_all = jnp.concatenate([past_v, v], axis=1)
        k_pos = jnp.arange(past_k.shape[1] + S)
    f32 = jnp.float32
    lam = (jnp.exp(jnp.sum(w['lambda_q1'].astype(f32) * w['lambda_k1'].astype(f32)))
           - jnp.exp(jnp.sum(w['lambda_q2'].astype(f32) * w['lambda_k2'].astype(f32)))
           + lam_init)
    o = diff_attention(q, k_all, v_all, pos, k_pos, lam)
    o = (rmsnorm(o, w['da_subln_g']) * (1.0 - lam_init)).reshape(B, S, DA_WIDTH)
    u = jax.nn.gelu(u)
    gv = rmsnorm(jax.nn.gelu(gv).reshape(B, S, GM_GROUPS, GM_CH), w['gm_norm_g'])
    g_out = spatial_gate(u, gv, w['gm_w_s'], w['gm_b'])
    x = x + jnp.concatenate([o, g_out], axis=-1) @ w['w_out']
    x = x + cross_attend(rmsnorm(x, w['norm_x_g']), mem_k, mem_v,
                         w['wq_c'], w['wo_c'], w['xq_norm_g'])
    f, conv_state = conv_ffn(rmsnorm(x, w['norm_ffn_g']), conv_hist,
                             w['w_up'], w['conv_w'], w['conv_b'], w['w_down'])
    return x + f, k, v, gv, conv_state


def setup_inputs(seed: int = 0) -> dict:
    key = jax.random.key(seed)
    ks = iter(jax.random.split(key, 40))
    f32 = jnp.float32

    def nrm(shape, scale):
        return scale * jax.random.normal(next(ks), shape, f32)

    def gain(shape):
        return 1.0 + 0.05 * jax.random.normal(next(ks), shape, f32)

    return {
        'x_prompt': nrm((BATCH, SEQ, D_MODEL), 1.0),
        'x_sample': nrm((DEC_BATCH, DEC_SEQ, D_MODEL), 1.0),
        'cache_da_k': nrm((DEPTH, DEC_BATCH, PAST_LEN, DA_HEADS, 2, DA_DK), 1.0),
        'cache_da_v': nrm((DEPTH, DEC_BATCH, PAST_LEN, DA_HEADS, DA_DV), 1.0),
        'cache_mem_k': nrm((DEPTH, DEC_BATCH, MEM_LEN, X_HEADS, X_DH), 1.0),
        'cache_mem_v': nrm((DEPTH, DEC_BATCH, MEM_LEN, X_HEADS, X_DH), 1.0),
        'state_ffn_conv': nrm((DEPTH, DEC_BATCH, CONV_W - 1, D_FF), 1.0),
        'mem_prompt': nrm((BATCH, MEM_LEN, D_MODEL), 1.0),
        'norm_mix_g': gain((DEPTH, D_MODEL)),
        'w_in': nrm((DEPTH, D_MODEL, IN_COLS), D_MODEL ** -0.5),
        'da_q_norm_g': gain((DEPTH, DA_DK)),
        'da_k_norm_g': gain((DEPTH, DA_DK)),
        'lambda_q1': nrm((DEPTH, DA_DK), 0.1),
        'lambda_k1': nrm((DEPTH, DA_DK), 0.1),
        'lambda_q2': nrm((DEPTH, DA_DK), 0.1),
        'lambda_k2': nrm((DEPTH, DA_DK), 0.1),
        'da_subln_g': gain((DEPTH, DA_DV)),
        'gm_norm_g': gain((DEPTH, GM_CH)),
        'gm_w_s': nrm((DEPTH, GM_GROUPS, GM_CHUNK, GM_CHUNK), GM_CHUNK ** -0.5),
        'gm_b': nrm((DEPTH, GM_GROUPS, GM_CHUNK), 0.02),
        'w_out': nrm((DEPTH, D_MIX, D_MODEL), D_MIX ** -0.5),
        'norm_x_g': gain((DEPTH, D_MODEL)),
        'norm_mem_g': gain((DEPTH, D_MODEL)),
        'wq_c': nrm((DEPTH, D_MODEL, D_MODEL), D_MODEL ** -0.5),
        'wk_c': nrm((DEPTH, D_MODEL, D_MODEL), D_MODEL ** -0.5),
        'wv_c': nrm((DEPTH, D_MODEL, D_MODEL), D_MODEL ** -0.5),
        'wo_c': nrm((DEPTH, D_MODEL, D_MODEL), D_MODEL ** -0.5),
        'xq_norm_g': gain((DEPTH, X_DH)),
        'xk_norm_g': gain((DEPTH, X_DH)),
        'norm_ffn_g': gain((DEPTH, D_MODEL)),
        'w_up': nrm((DEPTH, D_MODEL, 2 * D_FF), D_MODEL ** -0.5),
        'conv_w': nrm((DEPTH, CONV_W, D_FF), CONV_W ** -0.5),
        'conv_b': nrm((DEPTH, D_FF), 0.02),
        'w_down': nrm((DEPTH, D_FF, D_MODEL), D_FF ** -0.5),
    }


def reference(x_prompt, x_sample, cache_da_k, cache_da_v, cache_mem_k, cache_mem_v,
              state_ffn_conv, mem_prompt, norm_mix_g, w_in, da_q_norm_g, da_k_norm_g,
              lambda_q1, lambda_k1, lambda_q2, lambda_k2, da_subln_g, gm_norm_g,
              gm_w_s, gm_b, w_out, norm_x_g, norm_mem_g, wq_c, wk_c, wv_c, wo_c,
              xq_norm_g, xk_norm_g, norm_ffn_g, w_up, conv_w, conv_b, w_down):
    S_p = x_prompt.shape[1]
    S_s = x_sample.shape[1]
    past = cache_da_k.shape[2]
    pos_p = jnp.arange(S_p)
    pos_s = past + jnp.arange(S_s)
    hist_p = jnp.zeros((x_prompt.shape[0], CONV_W - 1, D_FF), x_prompt.dtype)

    xp, xs = x_prompt, x_sample
    dk_p, dv_p, mk_p, mv_p, fc_p = [], [], [], [], []
    dk_s, dv_s, gv_s, fc_s = [], [], [], []
    for l in range(DEPTH):
        lam_init = 0.8 - 0.6 * math.exp(-0.3 * l)
        w = dict(norm_mix_g=norm_mix_g[l], w_in=w_in[l], da_q_norm_g=da_q_norm_g[l],
                 da_k_norm_g=da_k_norm_g[l], lambda_q1=lambda_q1[l], lambda_k1=lambda_k1[l],
                 lambda_q2=lambda_q2[l], lambda_k2=lambda_k2[l], da_subln_g=da_subln_g[l],
                 gm_norm_g=gm_norm_g[l], gm_w_s=gm_w_s[l], gm_b=gm_b[l], w_out=w_out[l],
                 norm_x_g=norm_x_g[l], wq_c=wq_c[l], wo_c=wo_c[l], xq_norm_g=xq_norm_g[l],
                 norm_ffn_g=norm_ffn_g[l], w_up=w_up[l], conv_w=conv_w[l],
                 conv_b=conv_b[l], w_down=w_down[l])
        mk, mv = mem_kv(mem_prompt, norm_mem_g[l], wk_c[l], wv_c[l], xk_norm_g[l])
        xp, k_new, v_new, _, conv_new = trunk_layer(
            xp, pos_p, None, None, mk, mv, hist_p, w, lam_init)
        dk_p.append(k_new); dv_p.append(v_new); mk_p.append(mk); mv_p.append(mv)
        fc_p.append(conv_new)
        xs, k_new, v_new, gv_new, conv_new = trunk_layer(
            xs, pos_s, cache_da_k[l], cache_da_v[l], cache_mem_k[l], cache_mem_v[l],
            state_ffn_conv[l], w, lam_init)
        dk_s.append(k_new); dv_s.append(v_new); gv_s.append(gv_new); fc_s.append(conv_new)

    return (xp, xs, jnp.stack(dk_p), jnp.stack(dv_p), jnp.stack(mk_p), jnp.stack(mv_p),
            jnp.stack(fc_p), jnp.stack(dk_s), jnp.stack(dv_s), jnp.stack(gv_s),
            jnp.stack(fc_s))
```

```python
import os
import numpy as np
from contextlib import ExitStack
import concourse.bass as bass
import concourse.mybir as mybir
from concourse.bass_utils import run_bass_kernel_spmd

F32 = mybir.dt.float32
BF16 = mybir.dt.bfloat16
AF = mybir.ActivationFunctionType
ALU = mybir.AluOpType
AX = mybir.AxisListType
CELL = 256
SAME_ENGINE_SYNC = True
N_DMA_SEMS = 12

D = 1024
L = 4
NB = 4
S = 2048
DS = 32
PAST = 1024
MEM = 256
DFF = 2816
NFC = 22
EPS = 1e-6
NREP = 1152
NFM = 124


def _esize(dt):
    return 2 if dt == BF16 else 4


class KB:
    def __init__(self):
        self.nc = bass.Bass("TRN2", target_bir_lowering=False)
        nc = self.nc
        self.es = ExitStack()
        self.eng = {"pe": nc.tensor, "act": nc.scalar, "dve": nc.vector, "pool": nc.gpsimd, "sp": nc.sync}
        self.sems = {}
        self.sem_id = {}
        self.cnt = {}
        self._nsem = 0
        for e in ["pe", "act", "dve", "pool"]:
            self.sems[e] = self._newsem("c_" + e)
            self.cnt[e] = 0
        self.dq = {}
        for q in ["sp", "pool"]:
            pool = [self._newsem(f"d_{q}{i}") for i in range(N_DMA_SEMS)]
            self.dq[q] = {"sems": pool, "vals": [0] * N_DMA_SEMS, "next": 0}
        self.seen = {e: {} for e in ["pe", "act", "dve", "pool", "sp"]}
        self.cells = {}
        self.n_inst = 0
        self.n_wait = 0

    def _newsem(self, name):
        h = self.es.enter_context(self.nc.semaphore(name))
        k = self._nsem
        self._nsem += 1
        self.sem_id[k] = h
        return k

    def _cells(self, ap):
        sp = str(ap.space).upper()
        if "SB" not in sp and "PSUM" not in sp:
            return None
        es = _esize(ap.dtype)
        a = ap.ap
        pstep = a[0][0]
        off = ap.offset
        base = (off % pstep) * es if pstep > 0 else off * es
        region = ap.tensor.name
        cell = CELL if "SB" in sp else 2048
        dims = [(s * es, c) for (s, c) in a[1:]]
        if not dims:
            dims = [(es, 1)]
        ls, lc = dims[-1]
        run = (lc - 1) * ls + es
        starts = [base]
        for (s, c) in dims[:-1]:
            if s == 0 or c == 1:
                continue
            starts = [st + i * s for st in starts for i in range(c)]
        out = set()
        for st in starts:
            for c in range(st // cell, (st + run - 1) // cell + 1):
                out.add((region, c))
        return out

    def _deps(self, reads, writes):
        deps = {}
        rc = set()
        wc = set()
        for ap in reads:
            c = self._cells(ap)
            if c:
                if "PSUM" in str(ap.space).upper():
                    wc |= c
                else:
                    rc |= c
        for ap in writes:
            c = self._cells(ap)
            if c:
                wc |= c
        cells = self.cells
        for c in rc:
            st = cells.get(c)
            if st is not None and st[0] is not None:
                k, v = st[0]
                if deps.get(k, 0) < v:
                    deps[k] = v
        for c in wc:
            st = cells.get(c)
            if st is not None:
                if st[0] is not None:
                    k, v = st[0]
                    if deps.get(k, 0) < v:
                        deps[k] = v
                for k, v in st[1].items():
                    if deps.get(k, 0) < v:
                        deps[k] = v
        return deps, rc, wc

    def _commit(self, tok, rc, wc):
        k, v = tok
        cells = self.cells
        for c in rc:
            if c in wc:
                continue
            st = cells.get(c)
            if st is None:
                st = [None, {}]
                cells[c] = st
            if st[1].get(k, 0) < v:
                st[1][k] = v
        for c in wc:
            cells[c] = [tok, {}]

    def _waits(self, ename, deps):
        e = self.eng[ename]
        seen = self.seen[ename]
        own = self.sems.get(ename)
        for k, v in deps.items():
            if k == own and (ename == "pe" or not SAME_ENGINE_SYNC):
                continue
            if seen.get(k, 0) >= v:
                continue
            e.wait_ge(self.sem_id[k], v)
            seen[k] = v
            self.n_wait += 1
            if os.environ.get("K_TRACE"):
                print("   WAIT", ename, "sem", k, ">=", v)

    def I(self, ename, fn, reads, writes):
        deps, rc, wc = self._deps(reads, writes)
        self._waits(ename, deps)
        ins = fn()
        self.cnt[ename] += 1
        k = self.sems[ename]
        if os.environ.get("K_TRACE"):
            print("INS", ename, self.cnt[ename], str(ins)[:150])
        ins.then_inc(self.sem_id[k], 1)
        self._commit((k, self.cnt[ename]), rc, wc)
        self.n_inst += 1
        return ins

    def dma(self, q, out, in_, extra_wait=None):
        deps, rc, wc = self._deps([in_], [out])
        if extra_wait:
            for k, v in extra_wait:
                if deps.get(k, 0) < v:
                    deps[k] = v
        d = self.dq[q]
        i = d["next"]
        d["next"] = (i + 1) % N_DMA_SEMS
        k = d["sems"][i]
        if d["vals"][i] > 0:
            deps[k] = max(deps.get(k, 0), d["vals"][i])
        self._waits(q, deps)
        ins = self.eng[q].dma_start(out=out, in_=in_)
        d["vals"][i] += 16
        ins.then_inc(self.sem_id[k], 16)
        tok = (k, d["vals"][i])
        if os.environ.get("K_TRACE"):
            print("DMA", q, tok, str(ins)[:150])
        self._commit(tok, rc, wc)
        self.n_inst += 1
        return tok

    def finish(self):
        last = {}
        for q, d in self.dq.items():
            for k, v in zip(d["sems"], d["vals"]):
                if v > 0:
                    last[k] = v
        for e in ["pe", "act", "dve", "pool"]:
            if self.cnt[e] > 0:
                last[self.sems[e]] = self.cnt[e]
        self._waits("sp", last)

    def mm(self, out, lhsT, rhs, start=True, stop=True):
        return self.I("pe", lambda: self.nc.tensor.matmul(out, lhsT=lhsT, rhs=rhs, start=start, stop=stop),
                      [lhsT, rhs], [out])

    def tr(self, out, in_, ident):
        return self.I("pe", lambda: self.nc.tensor.transpose(out, in_, ident), [in_, ident], [out])

    def act(self, out, in_, func, bias=None, scale=1.0, accum_out=None):
        reads = [in_]
        kw = {}
        if bias is not None:
            kw["bias"] = bias
            if not isinstance(bias, (int, float)):
                reads.append(bias)
        if not isinstance(scale, (int, float)):
            reads.append(scale)
        writes = [out]
        if accum_out is not None:
            kw["accum_out"] = accum_out
            writes.append(accum_out)
        return self.I("act", lambda: self.nc.scalar.activation(out=out, in_=in_, func=func, scale=scale, **kw),
                      reads, writes)

    def tt(self, e, out, in0, in1, op):
        return self.I(e, lambda: self.eng[e].tensor_tensor(out=out, in0=in0, in1=in1, op=op), [in0, in1], [out])

    def ts(self, e, out, in0, s1, op0, s2=None, op1=None):
        reads = [in0]
        if not isinstance(s1, (int, float)):
            reads.append(s1)
        if s2 is not None and not isinstance(s2, (int, float)):
            reads.append(s2)
        kw = {}
        if op1 is not None:
            kw["op1"] = op1
        return self.I(e, lambda: self.eng[e].tensor_scalar(out=out, in0=in0, scalar1=s1, scalar2=s2, op0=op0, **kw),
                      reads, [out])

    def stt(self, e, out, in0, scalar, in1, op0, op1):
        reads = [in0, in1]
        if not isinstance(scalar, (int, float)):
            reads.append(scalar)
        return self.I(e, lambda: self.eng[e].scalar_tensor_tensor(out=out, in0=in0, scalar=scalar, in1=in1,
                                                                  op0=op0, op1=op1), reads, [out])

    def cp(self, e, out, in_):
        if e == "act":
            return self.I("act", lambda: self.nc.scalar.copy(out=out, in_=in_), [in_], [out])
        return self.I(e, lambda: self.eng[e].tensor_copy(out=out, in_=in_), [in_], [out])

    def memset(self, e, out, val):
        return self.I(e, lambda: self.eng[e].memset(out, val), [], [out])

    def recip(self, out, in_):
        return self.I("dve", lambda: self.nc.vector.reciprocal(out=out, in_=in_), [in_], [out])

    def reduce_add(self, out, in_):
        return self.I("dve", lambda: self.nc.vector.tensor_reduce(out=out, in_=in_, axis=AX.X, op=ALU.add),
                      [in_], [out])


class Seq:
    def __init__(self, kind, b):
        self.kind = kind
        self.b = b
        if kind == "p":
            self.S, self.TP, self.NT, self.QB, self.NBLK, self.TPB = S, 128, 16, 512, 4, 4
            self.SK, self.NKT = S, 16
        else:
            self.S, self.TP, self.NT, self.QB, self.NBLK, self.TPB = DS, DS, 1, DS, 1, 1
            self.SK, self.NKT = PAST + DS, 9


class Prog:
    def __init__(self):
        self.kb = KB()
        kb = self.kb
        nc = kb.nc
        self.nc = nc

        def din(name, shape, dt=F32):
            return nc.dram_tensor(name, list(shape), dt, kind="ExternalInput").ap()

        def dout(name, shape):
            return nc.dram_tensor(name, list(shape), F32, kind="ExternalOutput").ap()

        def dint(name, shape):
            return nc.dram_tensor(name, list(shape), BF16, kind="Internal").ap()

        self.xp = din("xp", [NB, S, D])
        self.xs = din("xs", [DS, D])
        self.ck = din("ck", [L, PAST, 512])
        self.cv = din("cv", [L, PAST, 512])
        self.cmk = din("cmk", [L, MEM, D])
        self.cmv = din("cmv", [L, MEM, D])
        self.memp = din("memp", [NB, MEM, D])
        self.wf = {
            "w_in": din("w_in", [L, D, 2560]), "w_out": din("w_out", [L, D, D]),
            "wq": din("wq", [L, D, D]), "wk": din("wk", [L, D, D]), "wv": din("wv", [L, D, D]),
            "wo": din("wo", [L, D, D]), "w_up": din("w_up", [L, D, 2 * DFF]), "w_down": din("w_down", [L, DFF, D]),
        }
        self.wb = {k: dint("b_" + k, v.shape) for k, v in self.wf.items()}
        self.rep = din("rep", [L, 128, NREP])
        self.fm = din("fm", [L, 128, NFM])
        self.wst = din("wst", [L, 128, 512])
        self.ident_d = din("ident", [128, 128])
        self.tril_d = din("tril", [128, 128])
        self.csp_d = din("csp", [128, 16, 16])
        self.css_d = din("css", [DS, 16])
        self.convst_d = din("convst", [128, L, NFC, 2])
        self.yp = dout("yp", [NB, S, D])
        self.ys = dout("ys", [DS, D])
        self.dkp = dout("dkp", [L, NB, S, 512])
        self.dvp = dout("dvp", [L, NB, S, 512])
        self.mkp = dout("mkp", [L, NB, MEM, D])
        self.mvp = dout("mvp", [L, NB, MEM, D])
        self.fcp = dout("fcp", [L, NB, 2, DFF])
        self.dks = dout("dks", [L, DS, 512])
        self.dvs = dout("dvs", [L, DS, 512])
        self.gvs = dout("gvs", [L, DS, 512])
        self.fcs = dout("fcs", [L, 2, DFF])

        ARENA_BYTES = 212800
        self.arena = nc.alloc_sbuf_tensor("arena", [128, ARENA_BYTES // 2], BF16)
        self.ps = nc.alloc_psum_tensor("ps", [128, 8, 512], F32)
        self.X_OFF = 0
        self.HB_OFF = 65536
        self.KT_OFF = 98304
        self.VA_OFF = 114688
        self.W_OFF = 131328
        self.C_OFF = 188672
        self.cast_tok = {}
        self.ps_rr = {"mm": 0, "pair": 0, "acc": 0}
        self.lctr = 0
        self.build()

    def V(self, off, shape, dt, P=128):
        n = int(np.prod(shape)) * _esize(dt)
        assert off % 4 == 0
        v = self.arena[0:P, off // 2:(off + n) // 2]
        if dt != BF16:
            v = v.bitcast(dt)
        if len(shape) == 2:
            v = v.rearrange("p (a b) -> p a b", a=shape[0])
        elif len(shape) == 3:
            v = v.rearrange("p (a b c) -> p a b c", a=shape[0], b=shape[1])
        return v

    def wslot(self, i, shape):
        return self.V(self.W_OFF + i * 8192, shape, BF16)

    def ps_mm(self):
        i = self.ps_rr["mm"]
        self.ps_rr["mm"] = (i + 1) % 4
        return self.ps[:, i, :]

    def ps_pair(self):
        i = self.ps_rr["pair"]
        self.ps_rr["pair"] = (i + 1) % 2
        return self.ps[:, 2 * i:2 * i + 2, :]

    def ps_tr(self, n, w):
        return self.ps[:, 7, 0:(n * w) // 2].bitcast(BF16).rearrange("p (a b) -> p a b", a=n)

    def rstd(self, out, ss, dim, P):
        kb = self.kb
        kb.act(out, ss, AF.Ln, scale=1.0 / dim, bias=EPS)
        kb.act(out, out, AF.Exp, scale=-0.5)

    def build(self):
        kb = self.kb
        c = self.C_OFF
        self.identf = self.V(c, [128], F32); c += 512
        self.identb = self.V(c, [128], BF16); c += 256
        self.tril = self.V(c, [128], F32); c += 512
        self.csp = self.V(c, [16, 16], F32); c += 1024
        self.css = self.V(c, [16], F32); c += 64
        self.lslot = []
        for i in range(2):
            d = {}
            d["rep"] = self.V(c, [NREP], F32); c += NREP * 4
            d["fm"] = self.V(c, [NFM], F32); c += NFM * 4
            d["wsT"] = self.V(c, [4, 128], BF16); c += 1024
            d["lam"] = self.V(c, [16], F32); c += 64
            self.lslot.append(d)
        self.mkT = self.V(c, [8, MEM], BF16); c += 4096
        self.MVA = self.V(c, [2, 4, 258], BF16); c += 4128
        self.stat = self.V(c, [128], F32); c += 512
        assert c <= 212800, c

        kb.dma("sp", self.identf, self.ident_d[:, :])
        kb.dma("sp", self.tril, self.tril_d[:, :])
        kb.dma("sp", self.csp, self.csp_d[:, :, :])
        kb.dma("sp", self.css[0:DS], self.css_d[:, :])
        kb.cp("dve", self.identb, self.identf)
        kb.memset("dve", self.MVA[:, :, :, 256:258], 1.0)

        for l in range(L):
            for name in ["wk", "wv", "w_in", "w_out", "wq", "wo", "w_up", "w_down"]:
                src = self.wf[name]
                dst = self.wb[name]
                rows = src.shape[1]
                toks = []
                for r0 in range(0, rows, 128):
                    toks.append(kb.dma("pool", dst[l, r0:r0 + 128, :], src[l, r0:r0 + 128, :]))
                self.cast_tok[(name, l)] = toks

        import os
        self.dbg = int(os.environ.get("K_DBG", "9"))
        self.dbg_nl = int(os.environ.get("K_NL", str(L)))
        seqs = [Seq("p", b) for b in range(NB)] + [Seq("s", 0)]
        sel = os.environ.get("K_SEQS")
        if sel is not None:
            seqs = [seqs[int(i)] for i in sel.split(",")]
        if os.environ.get("K_NOCAST"):
            pass
        self.va_ones_done = None
        for sq in seqs:
            self.run_seq(sq)
        kb.finish()

    def load_panel(self, slot, name, l, rows, cols, shape):
        src = self.wb[name][l, rows[0]:rows[1], cols[0]:cols[1]].rearrange("(k p) n -> p k n", p=128)
        dst = self.wslot(slot, shape)
        self.kb.dma("sp", dst, src, extra_wait=self.cast_tok[(name, l)])
        return dst

    def run_seq(self, sq):
        kb = self.kb
        TP, NT = sq.TP, sq.NT
        self.X = self.V(self.X_OFF, [16, D], F32)
        for t in range(NT):
            if sq.kind == "p":
                kb.dma("sp", self.X[:, t, :], self.xp[sq.b, t * 128:(t + 1) * 128, :])
            else:
                kb.dma("sp", self.X[0:TP, 0, :], self.xs[:, :])
        self.KT = self.V(self.KT_OFF, [4, sq.SK], BF16)
        self.VA = self.V(self.VA_OFF, [sq.NKT, 4, 130], BF16)
        for l in range(self.dbg_nl):
            self.run_layer(sq, l)
        for t in range(NT):
            if sq.kind == "p":
                kb.dma("sp", self.yp[sq.b, t * 128:(t + 1) * 128, :], self.X[:, t, :])
            else:
                kb.dma("sp", self.ys[:, :], self.X[0:TP, 0, :])

    def norm_transpose(self, sq, xrow, gcol, outT, xn, sqr, P):
        kb = self.kb
        ss = self.stat[0:P, 0:1]
        rs = self.stat[0:P, 1:2]
        kb.act(sqr, xrow, AF.Square, accum_out=ss)
        self.rstd(rs, ss, D, P)
        kb.ts("dve", xn, xrow, rs, ALU.mult)
        pt = self.ps_tr(8, P)
        for kc in range(8):
            kb.tr(pt[:, kc, :], xn[:, kc * 128:(kc + 1) * 128], self.identb[0:P, 0:P])
        kb.tt("dve", outT, pt, gcol.unsqueeze(2).broadcast_to([128, 8, P]), ALU.mult)

    def run_layer(self, sq, l):
        kb = self.kb
        TP, NT = sq.TP, sq.NT
        isp = sq.kind == "p"
        slot = self.lslot[self.lctr % 2]
        self.lctr += 1
        lam_init = 0.8 - 0.6 * float(np.exp(-0.3 * l))
        HB = self.HB_OFF

        kb.dma("sp", slot["rep"], self.rep[l, :, :])
        kb.dma("sp", slot["fm"], self.fm[l, :, :])
        wst_f = self.V(HB, [4, 128], F32)
        kb.dma("sp", wst_f, self.wst[l, :, :].rearrange("p (g t) -> p g t", g=4))
        rep = slot["rep"]
        g_qk = rep[:, 0:128].rearrange("p (a d) -> p a d", a=2)
        g_sub = rep[:, 128:256]
        g_gmn = rep[:, 256:384]
        g_xq = rep[:, 384:640]
        g_xk = rep[:, 640:896]
        lamv = rep[:, 896:1152].rearrange("p (a d) -> p a d", a=4)
        fmv = slot["fm"]
        gfm = fmv[:, 0:32].rearrange("p (a k) -> p a k", a=4)
        gm_b = fmv[:, 32:36]
        conv_w = fmv[:, 36:102].rearrange("p (f j) -> p f j", f=NFC)
        conv_b = fmv[:, 102:124]
        lam = slot["lam"]
        kb.tt("dve", slot["wsT"], wst_f, self.tril.unsqueeze(1).broadcast_to([128, 4, 128]), ALU.mult)
        prod = self.V(HB + 2048, [2, 64], F32)
        kb.tt("dve", prod[:, 0, :], lamv[:, 0, :], lamv[:, 1, :], ALU.mult)
        kb.tt("dve", prod[:, 1, :], lamv[:, 2, :], lamv[:, 3, :], ALU.mult)
        kb.reduce_add(lam[:, 0:2], prod)
        kb.act(lam[:, 0:2], lam[:, 0:2], AF.Exp)
        kb.tt("dve", lam[:, 2:3], lam[:, 1:2], lam[:, 0:1], ALU.subtract)
        kb.ts("dve", lam[:, 3:4], lam[:, 2:3], -lam_init, ALU.add)
        kb.ts("dve", g_sub, g_sub, 1.0 - lam_init, ALU.mult)

        if isp:
            wk_p = [self.load_panel(3 + i, "wk", l, (0, D), (i * 512, (i + 1) * 512), [8, 512]) for i in range(2)]
            wv_p = [self.load_panel(5 + i, "wv", l, (0, D), (i * 512, (i + 1) * 512), [8, 512]) for i in range(2)]
        win = [None] * 5
        for i in range(3):
            win[i] = self.load_panel(i, "w_in", l, (0, D), (i * 512, (i + 1) * 512), [8, 512])

        if self.dbg < 2:
            return
        self.stage_mem(sq, l, slot, gfm, g_xk, wk_p if isp else None, wv_p if isp else None)

        if self.dbg < 3:
            return
        for i in range(3, 5):
            win[i] = self.load_panel(i, "w_in", l, (0, D), (i * 512, (i + 1) * 512), [8, 512])
        woA = self.load_panel(5, "w_out", l, (0, 512), (0, D), [4, D])
        woB = self.load_panel(6, "w_out", l, (512, D), (0, D), [4, D])

        if not os.environ.get("K_NOMEMSET"):
            kb.memset("dve", self.VA[:, :, :, 128:130], 1.0)
        if not isp:
            self.load_past(sq, l)

        QT = self.V(HB + 18432, [4, 512], BF16)
        for blk in range(sq.NBLK):
            for ti in range(sq.TPB):
                t = blk * sq.TPB + ti
                if t >= int(os.environ.get("K_NT", "99")):
                    continue
                self.stage_a_tile(sq, l, t, ti, slot, gfm, g_qk, g_gmn, gm_b, win, woB, QT)
            if blk == sq.NBLK - 1:
                wq_p = [self.load_panel(i, "wq", l, (0, D), (i * 512, (i + 1) * 512), [8, 512]) for i in range(2)]
                wo_p = [self.load_panel(2 + i, "wo", l, (0, D), (i * 512, (i + 1) * 512), [8, 512]) for i in range(2)]
            if not os.environ.get("K_SKIPB"):
                self.stage_b_block(sq, l, blk, slot, g_sub, lam, woA, QT)

        if self.dbg < 4:
            return
        fpan = {}
        fpan[0] = self.load_ffn_group(l, 0, 4)

        for t in range(NT):
            self.stage_c_tile(sq, l, t, gfm, g_xq, wq_p, wo_p)

        if self.dbg < 5:
            return
        fpan[1] = self.load_ffn_group(l, 1, 0)

        self.stage_ffn(sq, l, conv_w, conv_b, fpan)

    def load_ffn_group(self, l, g, s0):
        nf = 4 if g < 5 else 2
        c0 = g * 512
        gate = self.load_panel(s0, "w_up", l, (0, D), (c0, c0 + nf * 128), [8, nf * 128])
        up = self.load_panel(s0 + 1, "w_up", l, (0, D), (DFF + c0, DFF + c0 + nf * 128), [8, nf * 128])
        down = self.load_panel(s0 + 2, "w_down", l, (c0, c0 + nf * 128), (0, D), [nf, D])
        return gate, up, down, nf

    def stage_mem(self, sq, l, slot, gfm, g_xk, wk_p, wv_p):
        kb = self.kb
        HB = self.HB_OFF
        isp = sq.kind == "p"
        memt = self.V(HB + 4096, [D], F32)
        xn = self.V(HB + 8192, [D], BF16)
        mT = self.V(HB + 10240, [8, 128], BF16)
        kst = self.V(HB + 12288, [D], F32)
        kbf = self.V(HB + 16384, [D], BF16)
        vst = self.V(HB + 18432, [D], F32)
        sqr = self.V(HB + 22528, [D], F32)
        ss4 = self.stat[:, 4:8]
        rs4 = self.stat[:, 8:12]
        for mt in range(2):
            if isp:
                kb.dma("sp", memt, self.memp[sq.b, mt * 128:(mt + 1) * 128, :])
                self.norm_transpose(sq, memt, gfm[:, 3, :], mT, xn, sqr, 128)
                pk = self.ps_pair()
                for cb in range(2):
                    for kc in range(8):
                        kb.mm(pk[:, cb, :], lhsT=mT[:, kc, :], rhs=wk_p[cb][:, kc, :], start=(kc == 0), stop=(kc == 7))
                pk2 = pk.rearrange("p a b -> p (a b)")
                kb.act(sqr, pk2, AF.Square)
                kb.reduce_add(ss4, sqr.rearrange("p (h d) -> p h d", h=4))
                self.rstd(rs4, ss4, 256, 128)
                k3 = kst.rearrange("p (h d) -> p h d", h=4)
                kb.tt("dve", k3, pk2.rearrange("p (h d) -> p h d", h=4),
                      rs4.unsqueeze(2).broadcast_to([128, 4, 256]), ALU.mult)
                kb.tt("dve", k3, k3, g_xk.unsqueeze(1).broadcast_to([128, 4, 256]), ALU.mult)
                kb.dma("sp", self.mkp[l, sq.b, mt * 128:(mt + 1) * 128, :], kst)
                kb.cp("act", kbf, kst)
                pv = self.ps_pair()
                for cb in range(2):
                    for kc in range(8):
                        kb.mm(pv[:, cb, :], lhsT=mT[:, kc, :], rhs=wv_p[cb][:, kc, :], start=(kc == 0), stop=(kc == 7))
                pv2 = pv.rearrange("p a b -> p (a b)")
                kb.cp("act", vst, pv2)
                kb.dma("sp", self.mvp[l, sq.b, mt * 128:(mt + 1) * 128, :], vst)
                kb.cp("dve", self.MVA[:, mt, :, 0:256], vst.rearrange("p (h d) -> p h d", h=4))
            else:
                kb.dma("sp", kst, self.cmk[l, mt * 128:(mt + 1) * 128, :])
                kb.cp("act", kbf, kst)
                kb.dma("sp", vst, self.cmv[l, mt * 128:(mt + 1) * 128, :])
                kb.cp("dve", self.MVA[:, mt, :, 0:256], vst.rearrange("p (h d) -> p h d", h=4))
            pt = self.ps_tr(8, 128)
            for kc in range(8):
                kb.tr(pt[:, kc, :], kbf[:, kc * 128:(kc + 1) * 128], self.identb)
            kb.cp("dve", self.mkT[:, :, mt * 128:(mt + 1) * 128], pt)

    def load_past(self, sq, l):
        kb = self.kb
        HB = self.HB_OFF
        st = self.V(HB + 4096, [512], F32)
        sb = self.V(HB + 6144, [512], BF16)
        for kt in range(8):
            kb.dma("sp", st, self.ck[l, kt * 128:(kt + 1) * 128, :])
            kb.cp("act", sb, st)
            pt = self.ps_tr(4, 128)
            for h in range(4):
                kb.tr(pt[:, h, :], sb[:, h * 128:(h + 1) * 128], self.identb)
            kb.cp("dve", self.KT[:, :, kt * 128:(kt + 1) * 128], pt)
            st2 = self.V(HB + 8192, [512], F32)
            kb.dma("sp", st2, self.cv[l, kt * 128:(kt + 1) * 128, :])
            kb.cp("dve", self.VA[:, kt, :, 0:128], st2.rearrange("p (h d) -> p h d", h=4))

    def stage_a_tile(self, sq, l, t, ti, slot, gfm, g_qk, g_gmn, gm_b, win, woB, QT):
        kb = self.kb
        P = sq.TP
        HB = self.HB_OFF
        isp = sq.kind == "p"
        o = HB
        xn = self.V(o, [D], BF16, P); o += 2048
        hT = self.V(o, [8, P], BF16); o += 2048
        sqr = self.V(o, [D], F32, P); o += 4096
        zq = self.V(o, [512], F32, P); o += 2048
        zk = self.V(o, [512], F32, P); o += 2048
        zv = self.V(o, [512], F32, P); o += 2048
        qkb = self.V(o, [D], BF16, P); o += 2048
        gvb = self.V(HB, [512], BF16, P)
        gob = self.V(HB + 1024, [512], BF16, P)
        goT = self.V(o, [4, P], BF16); o += 1024
        rope = self.V(o, [4, 64], F32, P); o += 1024
        assert o <= HB + 18432
        cs = self.csp[:, t, :] if isp else self.css[0:P, :]
        cosv = cs[:, 0:8]
        sinv = cs[:, 8:16]
        xrow = self.X[0:P, t, :]

        astep = float(os.environ.get("K_ASTEP", "99"))
        self.norm_transpose(sq, xrow, gfm[:, 0, :], hT, xn, sqr, P)
        if astep < 1:
            return

        def proj(cg):
            pz = self.ps_mm()[0:P, :]
            for kc in range(8):
                kb.mm(pz, lhsT=hT[:, kc, :], rhs=win[cg][:, kc, :], start=(kc == 0), stop=(kc == 7))
            return pz

        ss8 = self.stat[0:P, 16:24]
        rs8 = self.stat[0:P, 24:32]
        sq512 = sqr[:, 0:512]
        for which, dst in ((0, zq), (1, zk)):
            pz = proj(which)
            if astep < 2:
                kb.cp("act", dst, pz)
                continue
            kb.act(sq512, pz, AF.Square)
            kb.reduce_add(ss8, sq512.rearrange("p (g d) -> p g d", g=8))
            self.rstd(rs8, ss8, 64, P)
            d3 = dst.rearrange("p (g d) -> p g d", g=8)
            kb.tt("dve", d3, pz.rearrange("p (g d) -> p g d", g=8), rs8.unsqueeze(2).broadcast_to([P, 8, 64]), ALU.mult)
            kb.tt("dve", d3, d3, g_qk[0:P, which, :].unsqueeze(1).broadcast_to([P, 8, 64]), ALU.mult)
            if astep < 3:
                continue
            x1 = d3[:, :, 0:8]
            x2 = d3[:, :, 8:16]
            cb_ = cosv.unsqueeze(1).broadcast_to([P, 8, 8])
            sb_ = sinv.unsqueeze(1).broadcast_to([P, 8, 8])
            r3 = rope.rearrange("p a (g d) -> p a g d", g=8)
            kb.tt("dve", r3[:, 0], x1, cb_, ALU.mult)
            kb.tt("dve", r3[:, 1], x2, sb_, ALU.mult)
            kb.tt("dve", r3[:, 2], x2, cb_, ALU.mult)
            kb.tt("dve", r3[:, 3], x1, sb_, ALU.mult)
            kb.tt("dve", x1, r3[:, 0], r3[:, 1], ALU.subtract)
            kb.tt("dve", x2, r3[:, 2], r3[:, 3], ALU.add)
            kb.cp("act", qkb[:, which * 512:(which + 1) * 512], dst)
        if isp:
            kb.dma("sp", self.dkp[l, sq.b, t * 128:(t + 1) * 128, :], zk)
        else:
            kb.dma("sp", self.dks[l, :, :], zk)
        if astep < 3.2:
            return
        pt = self.ps_tr(8, P)
        for j in range(8):
            kb.tr(pt[:, j, :], qkb[:, j * 128:(j + 1) * 128], self.identb[0:P, 0:P])
        kb.cp("dve", QT[:, :, ti * 128:ti * 128 + P], pt[:, 0:4, :])
        kpos = t * 128 if isp else PAST
        kb.cp("act", self.KT[:, :, kpos:kpos + P], pt[:, 4:8, :])
        if astep < 3.5:
            return
        pz = proj(2)
        kb.cp("act", zv, pz)
        if isp:
            kb.dma("sp", self.dvp[l, sq.b, t * 128:(t + 1) * 128, :], zv)
        else:
            kb.dma("sp", self.dvs[l, :, :], zv)
        vt = t if isp else 8
        if astep < 4:
            return
        kb.cp("dve", self.VA[0:P, vt, :, 0:128], zv.rearrange("p (h d) -> p h d", h=4))
        if astep < 5:
            return
        ug = zq
        pu = proj(3)
        kb.act(ug, pu, AF.Gelu_apprx_tanh)
        pg = proj(4)
        gg = sqr[:, 0:512]
        kb.act(gg, pg, AF.Gelu_apprx_tanh)
        sq2 = sqr[:, 512:1024]
        kb.act(sq2, gg, AF.Square)
        ss4 = self.stat[0:P, 4:8]
        rs4 = self.stat[0:P, 8:12]
        kb.reduce_add(ss4, sq2.rearrange("p (g d) -> p g d", g=4))
        self.rstd(rs4, ss4, 128, P)
        gg3 = gg.rearrange("p (g d) -> p g d", g=4)
        kb.tt("dve", gg3, gg3, rs4.unsqueeze(2).broadcast_to([P, 4, 128]), ALU.mult)
        kb.tt("dve", gg3, gg3, g_gmn[0:P, :].unsqueeze(1).broadcast_to([P, 4, 128]), ALU.mult)
        if not isp:
            kb.dma("sp", self.gvs[l, :, :], gg)
        kb.cp("act", gvb, gg)
        if astep < 6:
            return
        pgate = self.ps_mm()[0:P, :]
        for g in range(4):
            kb.mm(pgate[:, g * 128:(g + 1) * 128], lhsT=slot["wsT"][0:P, g, 0:P], rhs=gvb[:, g * 128:(g + 1) * 128])
        go3 = sq2.rearrange("p (g d) -> p g d", g=4)
        kb.tt("dve", go3, pgate.rearrange("p (g d) -> p g d", g=4),
              gm_b[0:P, :].unsqueeze(2).broadcast_to([P, 4, 128]), ALU.add)
        kb.tt("dve", gob, sq2, ug, ALU.mult)
        if astep < 7:
            return
        pt2 = self.ps_tr(4, P)
        for j in range(4):
            kb.tr(pt2[:, j, :], gob[:, j * 128:(j + 1) * 128], self.identb[0:P, 0:P])
        kb.cp("act", goT, pt2)
        py = self.ps_pair()
        for cb in range(2):
            for kc in range(4):
                kb.mm(py[0:P, cb, :], lhsT=goT[:, kc, :], rhs=woB[:, kc, cb * 512:(cb + 1) * 512],
                      start=(kc == 0), stop=(kc == 3))
        kb.tt("dve", xrow, py[0:P].rearrange("p a b -> p (a b)"), xrow, ALU.add)

    def stage_b_block(self, sq, l, blk, slot, g_sub, lam, woA, QT):
        kb = self.kb
        HB = self.HB_OFF
        isp = sq.kind == "p"
        P = sq.TP
        QB = sq.QB
        o = HB + 18432 + 4096
        eTs = [self.V(o + i * 2048, [2, 512], BF16) for i in range(2)]; o += 4096
        otmp = self.V(o, [128], F32); o += 512
        osq = self.V(o, [128], F32); o += 512
        OB = self.V(o, [4, 512], BF16); o += 4096
        oT = self.V(o, [4, 128], BF16); o += 1024
        assert o <= HB + 32768, o
        nqs = sq.TPB
        if isp:
            nkt = 4 * blk + 4
        else:
            nkt = 9
        rz = self.stat[0:P, 32:48]
        ssq = self.stat[0:P, 48:49]
        rso = self.stat[0:P, 49:50]
        nl = self.stat[0:P, 50:51]
        ei = 0
        for h in range(4):
            def acc(qs, m):
                r = qs * 2 + m
                return self.ps[0:P, 4 + r // 3, (r % 3) * 160:(r % 3) * 160 + 129]
            for kt in range(nkt):
                KP = 128 if (isp or kt < 8) else DS
                j = kt - 4 * blk if isp else -1
                q0 = max(0, j) * 128
                kpos = kt * 128
                psS = self.ps_pair()
                for m in range(2):
                    kb.mm(psS[0:KP, m, q0:QB], lhsT=self.KT[m * 64:(m + 1) * 64, h, kpos:kpos + KP],
                          rhs=QT[m * 64:(m + 1) * 64, h, q0:QB])
                eT = eTs[ei % 2]
                ei += 1
                kb.act(eT[0:KP, :, q0:QB], psS[0:KP, :, q0:QB], AF.Exp, scale=0.125)
                if j >= 0:
                    kb.memset("dve", eT[64:128, :, q0:q0 + 64], 0.0)
                for qs in range(max(0, j), nqs):
                    last = (kt == 4 * blk + qs) if isp else (kt == nkt - 1)
                    for m in range(2):
                        kb.mm(acc(qs, m), lhsT=eT[0:KP, m, qs * 128:qs * 128 + P], rhs=self.VA[0:KP, kt, h, 0:129],
                              start=(kt == 0 and (qs * 2 + m) % 3 == 0), stop=last)
            for qs in range(nqs):
                a0 = acc(qs, 0)
                a1 = acc(qs, 1)
                kb.recip(rz[:, 0:1], a0[:, 128:129])
                kb.recip(rz[:, 1:2], a1[:, 128:129])
                kb.tt("dve", nl, rz[:, 1:2], lam[0:P, 3:4], ALU.mult)
                ot = otmp[0:P, :]
                kb.ts("dve", ot, a0[:, 0:128], rz[:, 0:1], ALU.mult)
                kb.stt("dve", ot, a1[:, 0:128], nl, ot, ALU.mult, ALU.add)
                kb.act(osq[0:P, :], ot, AF.Square, accum_out=ssq)
                self.rstd(rso, ssq, 128, P)
                kb.stt("dve", OB[0:P, qs, h * 128:(h + 1) * 128], ot, rso, g_sub[0:P, :], ALU.mult, ALU.mult)
        for qs in range(nqs):
            t = blk * sq.TPB + qs
            pt = self.ps_tr(4, P)
            for j in range(4):
                kb.tr(pt[:, j, :], OB[0:P, qs, j * 128:(j + 1) * 128], self.identb[0:P, 0:P])
            kb.cp("act", oT[:, :, 0:P], pt)
            py = self.ps_pair()
            for cb in range(2):
                for kc in range(4):
                    kb.mm(py[0:P, cb, :], lhsT=oT[:, kc, 0:P], rhs=woA[:, kc, cb * 512:(cb + 1) * 512],
                          start=(kc == 0), stop=(kc == 3))
            xrow = self.X[0:P, t, :]
            kb.tt("dve", xrow, py[0:P].rearrange("p a b -> p (a b)"), xrow, ALU.add)

    def stage_c_tile(self, sq, l, t, gfm, g_xq, wq_p, wo_p):
        kb = self.kb
        P = sq.TP
        o = self.KT_OFF
        xn = self.V(o, [D], BF16, P); o += 2048
        h2T = self.V(o, [8, P], BF16); o += 2048
        sqr = self.V(o, [D], F32, P); o += 4096
        qcn = self.V(o, [D], BF16, P); o += 2048
        qcT = self.V(o, [8, P], BF16); o += 2048
        eT = self.V(o, [4, 2, P], BF16); o += 2048
        ocb = self.V(o, [D], BF16, P); o += 2048
        ocT = self.V(o, [8, P], BF16); o += 2048
        xrow = self.X[0:P, t, :]
        ss4 = self.stat[0:P, 4:8]
        rs4 = self.stat[0:P, 8:12]
        rz4 = self.stat[0:P, 12:16]
        self.norm_transpose(sq, xrow, gfm[:, 1, :], h2T, xn, sqr, P)
        pq = self.ps_pair()
        for cb in range(2):
            for kc in range(8):
                kb.mm(pq[0:P, cb, :], lhsT=h2T[:, kc, :], rhs=wq_p[cb][:, kc, :], start=(kc == 0), stop=(kc == 7))
        pq2 = pq[0:P].rearrange("p a b -> p (a b)")
        kb.act(sqr, pq2, AF.Square)
        kb.reduce_add(ss4, sqr.rearrange("p (h d) -> p h d", h=4))
        self.rstd(rs4, ss4, 256, P)
        s3 = sqr.rearrange("p (h d) -> p h d", h=4)
        kb.tt("dve", s3, pq2.rearrange("p (h d) -> p h d", h=4), rs4.unsqueeze(2).broadcast_to([P, 4, 256]), ALU.mult)
        kb.tt("dve", qcn.rearrange("p (h d) -> p h d", h=4), s3, g_xq[0:P, :].unsqueeze(1).broadcast_to([P, 4, 256]),
              ALU.mult)
        pt = self.ps_tr(8, P)
        for j in range(8):
            kb.tr(pt[:, j, :], qcn[:, j * 128:(j + 1) * 128], self.identb[0:P, 0:P])
        kb.cp("act", qcT, pt)
        psS = self.ps_pair()
        for h in range(4):
            for mt in range(2):
                c0 = (h * 2 + mt) * 128
                dst = psS[:, c0 // 512, (c0 % 512):(c0 % 512) + P]
                for dc in range(2):
                    kb.mm(dst, lhsT=self.mkT[:, h * 2 + dc, mt * 128:(mt + 1) * 128], rhs=qcT[:, h * 2 + dc, :],
                          start=(dc == 0), stop=(dc == 1))
        for half in range(2):
            kb.act(eT[:, half * 2:half * 2 + 2, :, :],
                   psS[:, half, :].rearrange("p (h m q) -> p h m q", h=2, m=2)[:, :, :, 0:P], AF.Exp, scale=1.0 / 16.0)
        for h in range(4):
            po = self.ps[0:P, 4 + (h % 3), 0:257]
            for mt in range(2):
                kb.mm(po, lhsT=eT[:, h, mt, :], rhs=self.MVA[:, mt, h, 0:257], start=(mt == 0), stop=(mt == 1))
            kb.recip(rz4[:, h:h + 1], po[:, 256:257])
            kb.ts("dve", ocb[:, h * 256:(h + 1) * 256], po[:, 0:256], rz4[:, h:h + 1], ALU.mult)
        pt2 = self.ps_tr(8, P)
        for j in range(8):
            kb.tr(pt2[:, j, :], ocb[:, j * 128:(j + 1) * 128], self.identb[0:P, 0:P])
        kb.cp("act", ocT, pt2)
        py = self.ps_pair()
        for cb in range(2):
            for kc in range(8):
                kb.mm(py[0:P, cb, :], lhsT=ocT[:, kc, :], rhs=wo_p[cb][:, kc, :], start=(kc == 0), stop=(kc == 7))
        kb.tt("dve", xrow, py[0:P].rearrange("p a b -> p (a b)"), xrow, ALU.add)
        HBv = self.V(self.HB_OFF, [8, sq.S], BF16)
        self.norm_transpose(sq, xrow, gfm[:, 2, :], HBv[:, :, t * 128:t * 128 + P], xn, sqr, P)

    def stage_ffn(self, sq, l, conv_w, conv_b, fpan):
        kb = self.kb
        isp = sq.kind == "p"
        P = sq.TP
        NTOK = sq.QB
        HBv = self.V(self.HB_OFF, [8, sq.S], BF16)
        o = self.KT_OFF
        gss = [self.V(o + i * 2064, [516], F32) for i in range(2)]; o += 4128
        ccs = [self.V(o + i * 2048, [512], F32) for i in range(2)]; o += 4096
        scs = [self.V(o + i * 2048, [512], F32) for i in range(2)]; o += 4096
        aTs = [self.V(o + i * 4096, [4, 512], BF16) for i in range(2)]; o += 8192
        carry = self.V(o, [NFC, 2], F32); o += 176
        cst = self.V(o, [512], F32, 2); o += 2048
        assert o <= self.KT_OFF + 33024
        if isp:
            kb.memset("dve", carry, 0.0)
        else:
            kb.dma("sp", carry, self.convst_d[:, l, :, :])
        ri = 0
        ai = 0
        for g in range(6):
            gate, up, down, nf = fpan[g]
            for blk in range(sq.NBLK):
                aT = aTs[ai % 2]
                ai += 1
                for fl in range(nf):
                    fc = g * 4 + fl
                    pg = self.ps_mm()
                    for kc in range(8):
                        kb.mm(pg[:, 0:NTOK], lhsT=gate[:, kc, fl * 128:(fl + 1) * 128],
                              rhs=HBv[:, kc, blk * 512:blk * 512 + NTOK], start=(kc == 0), stop=(kc == 7))
                    pu = self.ps_mm()
                    for kc in range(8):
                        kb.mm(pu[:, 0:NTOK], lhsT=up[:, kc, fl * 128:(fl + 1) * 128],
                              rhs=HBv[:, kc, blk * 512:blk * 512 + NTOK], start=(kc == 0), stop=(kc == 7))
                    gs = gss[ri % 2]
                    cc = ccs[ri % 2]
                    sc = scs[ri % 2]
                    ri += 1
                    kb.cp("act", gs[:, 2:2 + NTOK], pg[:, 0:NTOK])
                    kb.cp("dve", gs[:, 0:2], carry[:, fc, :])
                    kb.cp("dve", carry[:, fc, :], gs[:, NTOK:NTOK + 2])
                    kb.ts("dve", cc[:, 0:NTOK], gs[:, 2:2 + NTOK], conv_w[:, fc, 2:3], ALU.mult, conv_b[:, fc:fc + 1], ALU.add)
                    kb.stt("dve", cc[:, 0:NTOK], gs[:, 1:1 + NTOK], conv_w[:, fc, 1:2], cc[:, 0:NTOK], ALU.mult, ALU.add)
                    kb.stt("dve", cc[:, 0:NTOK], gs[:, 0:NTOK], conv_w[:, fc, 0:1], cc[:, 0:NTOK], ALU.mult, ALU.add)
                    kb.act(sc[:, 0:NTOK], cc[:, 0:NTOK], AF.Silu)
                    kb.tt("dve", aT[:, fl, 0:NTOK], sc[:, 0:NTOK], pu[:, 0:NTOK], ALU.mult)
                for ti in range(sq.TPB):
                    t = blk * sq.TPB + ti
                    py = self.ps[0:P, 4:6, :]
                    for cb in range(2):
                        for fl in range(nf):
                            kb.mm(py[:, cb, :], lhsT=aT[:, fl, ti * 128:ti * 128 + P], rhs=down[:, fl, cb * 512:(cb + 1) * 512],
                                  start=(fl == 0), stop=(fl == nf - 1))
                    xrow = self.X[0:P, t, :]
                    kb.tt("dve", xrow, py.rearrange("p a b -> p (a b)"), xrow, ALU.add)
            pc = self.ps[0:2, 6, :]
            for fl in range(nf):
                kb.tr(pc[:, fl * 128:(fl + 1) * 128], carry[:, g * 4 + fl, :], self.identf)
            kb.cp("act", cst[:, 0:nf * 128], pc[:, 0:nf * 128])
            dst = self.fcp[l, sq.b, :, g * 512:g * 512 + nf * 128] if isp else self.fcs[l, :, g * 512:g * 512 + nf * 128]
            kb.dma("sp", dst, cst[:, 0:nf * 128])
            if g + 2 < 6:
                fpan[g + 2] = self.load_ffn_group(l, g + 2, 4 if (g % 2 == 0) else 0)


_PROG = None


def _get_prog():
    global _PROG
    if _PROG is None:
        _PROG = Prog()
    return _PROG


def _host_consts(inp):
    f32 = np.float32
    rep = np.zeros((L, 128, NREP), f32)
    fm = np.zeros((L, 128, NFM), f32)
    for l in range(L):
        v = np.concatenate([inp["da_q_norm_g"][l], inp["da_k_norm_g"][l], inp["da_subln_g"][l], inp["gm_norm_g"][l],
                            inp["xq_norm_g"][l], inp["xk_norm_g"][l], inp["lambda_q1"][l], inp["lambda_k1"][l],
                            inp["lambda_q2"][l], inp["lambda_k2"][l]]).astype(f32)
        rep[l] = np.broadcast_to(v[None, :], (128, NREP))
        for i, nm in enumerate(["norm_mix_g", "norm_x_g", "norm_ffn_g", "norm_mem_g"]):
            fm[l, :, i * 8:(i + 1) * 8] = inp[nm][l].reshape(8, 128).T
        fm[l, :, 32:36] = inp["gm_b"][l].T
        cw = inp["conv_w"][l].reshape(3, NFC, 128)
        fm[l, :, 36:102] = cw.transpose(2, 1, 0).reshape(128, NFC * 3)
        fm[l, :, 102:124] = inp["conv_b"][l].reshape(NFC, 128).T
    wst = np.ascontiguousarray(inp["gm_w_s"].transpose(0, 3, 1, 2)).reshape(L, 128, 512).astype(f32)
    ident = np.eye(128, dtype=f32)
    tril = np.triu(np.ones((128, 128), f32))
    half = 8
    inv = (np.float32(500000.0) ** (-np.arange(half, dtype=f32) / np.float32(half))).astype(f32)

    def cs(pos):
        ang = pos.astype(f32)[:, None] * inv[None, :]
        return np.concatenate([np.cos(ang), np.sin(ang)], axis=1).astype(f32)

    csp = cs(np.arange(S)).reshape(16, 128, 16).transpose(1, 0, 2)
    css = cs(PAST + np.arange(DS))
    return dict(rep=rep, fm=fm, wst=wst, ident=ident, tril=tril, csp=np.ascontiguousarray(csp), css=css)


def _in_maps(inp, cores):
    hc = _host_consts(inp)
    in_maps = []
    for c in cores:
        m = dict(hc)
        m["xp"] = np.ascontiguousarray(inp["x_prompt"][c * NB:(c + 1) * NB])
        m["xs"] = np.ascontiguousarray(inp["x_sample"][c])
        m["ck"] = np.ascontiguousarray(inp["cache_da_k"][:, c].reshape(L, PAST, 512))
        m["cv"] = np.ascontiguousarray(inp["cache_da_v"][:, c].reshape(L, PAST, 512))
        m["cmk"] = np.ascontiguousarray(inp["cache_mem_k"][:, c].reshape(L, MEM, D))
        m["cmv"] = np.ascontiguousarray(inp["cache_mem_v"][:, c].reshape(L, MEM, D))
        m["memp"] = np.ascontiguousarray(inp["mem_prompt"][c * NB:(c + 1) * NB])
        st = inp["state_ffn_conv"][:, c]
        m["convst"] = np.ascontiguousarray(st.reshape(L, 2, NFC, 128).transpose(3, 0, 2, 1))
        m["w_in"] = inp["w_in"]; m["w_out"] = inp["w_out"]
        m["wq"] = inp["wq_c"]; m["wk"] = inp["wk_c"]; m["wv"] = inp["wv_c"]; m["wo"] = inp["wo_c"]
        m["w_up"] = inp["w_up"]; m["w_down"] = inp["w_down"]
        in_maps.append(m)
    return in_maps


def kernel(**inp):
    inp = {k: np.asarray(v) for k, v in inp.items()}
    prog = _get_prog()
    n = 8
    in_maps = _in_maps(inp, range(n))
    res = run_bass_kernel_spmd(prog.nc, in_maps, core_ids=list(range(n))).results
    cat = lambda k, ax: np.concatenate([r[k] for r in res], axis=ax)
    y_p = cat("yp", 0)
    y_s = np.stack([r["ys"] for r in res], 0)
    dk_p = cat("dkp", 1).reshape(L, 32, S, 4, 2, 64)
    dv_p = cat("dvp", 1).reshape(L, 32, S, 4, 128)
    mk_p = cat("mkp", 1).reshape(L, 32, MEM, 4, 256)
    mv_p = cat("mvp", 1).reshape(L, 32, MEM, 4, 256)
    fc_p = cat("fcp", 1)
    dk_s = np.stack([r["dks"] for r in res], 1).reshape(L, 8, DS, 4, 2, 64)
    dv_s = np.stack([r["dvs"] for r in res], 1).reshape(L, 8, DS, 4, 128)
    gv_s = np.stack([r["gvs"] for r in res], 1).reshape(L, 8, DS, 4, 128)
    fc_s = np.stack([r["fcs"] for r in res], 1)
    return (y_p, y_s, dk_p, dv_p, mk_p, mv_p, fc_p, dk_s, dv_s, gv_s, fc_s)
```

```python
import os
import numpy as np
from contextlib import ExitStack
import concourse.bass as bass
import concourse.mybir as mybir
from concourse.bass_utils import run_bass_kernel_spmd

F32 = mybir.dt.float32
BF16 = mybir.dt.bfloat16
AF = mybir.ActivationFunctionType
ALU = mybir.AluOpType
AX = mybir.AxisListType
CELL = 256
SAME_ENGINE_SYNC = bool(int(os.environ.get("K_SES", "0")))
SMALL_T = int(os.environ.get("K_SMALL", "256"))
N_DMA_SEMS = 12

D = 1024
L = 4
NB = 4
S = 2048
DS = 32
PAST = 1024
MEM = 256
DFF = 2816
NFC = 22
EPS = 1e-6
NREP = 1152
NFM = 124


def _esize(dt):
    return 2 if dt == BF16 else 4


class KB:
    def __init__(self):
        self.nc = bass.Bass("TRN2", target_bir_lowering=False)
        nc = self.nc
        self.es = ExitStack()
        self.eng = {"pe": nc.tensor, "act": nc.scalar, "dve": nc.vector, "pool": nc.gpsimd, "sp": nc.sync}
        self.sems = {}
        self.sem_id = {}
        self.cnt = {}
        self._nsem = 0
        for e in ["pe", "act", "dve", "pool"]:
            self.sems[e] = self._newsem("c_" + e)
            self.cnt[e] = 0
        self.dq = {}
        for q in ["sp", "pool"]:
            pool = [self._newsem(f"d_{q}{i}") for i in range(N_DMA_SEMS)]
            self.dq[q] = {"sems": pool, "vals": [0] * N_DMA_SEMS, "next": 0}
        self.seen = {e: {} for e in ["pe", "act", "dve", "pool", "sp"]}
        self.cells = {}
        self.n_inst = 0
        self.n_wait = 0

    def _newsem(self, name):
        h = self.es.enter_context(self.nc.semaphore(name))
        k = self._nsem
        self._nsem += 1
        self.sem_id[k] = h
        return k

    def _cells(self, ap):
        sp = str(ap.space).upper()
        if "SB" not in sp and "PSUM" not in sp:
            return None
        es = _esize(ap.dtype)
        a = ap.ap
        pstep = a[0][0]
        off = ap.offset
        base = (off % pstep) * es if pstep > 0 else off * es
        region = ap.tensor.name
        cell = CELL if "SB" in sp else 2048
        dims = [(s * es, c) for (s, c) in a[1:]]
        if not dims:
            dims = [(es, 1)]
        ls, lc = dims[-1]
        run = (lc - 1) * ls + es
        starts = [base]
        for (s, c) in dims[:-1]:
            if s == 0 or c == 1:
                continue
            starts = [st + i * s for st in starts for i in range(c)]
        out = set()
        for st in starts:
            for c in range(st // cell, (st + run - 1) // cell + 1):
                out.add((region, c))
        fsz = 1
        for (s_, c_) in a[1:]:
            if s_ != 0:
                fsz *= c_
        self._last_fsz = fsz
        return out

    def _deps(self, reads, writes, own=None):
        deps = {}
        rc = set()
        wc = set()
        msize = 1 << 30
        for ap in reads:
            c = self._cells(ap)
            if c:
                msize = min(msize, self._last_fsz)
                if "PSUM" in str(ap.space).upper():
                    wc |= c
                else:
                    rc |= c
        for ap in writes:
            c = self._cells(ap)
            if c:
                msize = min(msize, self._last_fsz)
                wc |= c
        self._msize = msize
        cells = self.cells
        small = msize <= SMALL_T

        def add(tok):
            k, v, sz = tok
            if k == own and not SAME_ENGINE_SYNC and not (small or sz <= SMALL_T):
                return
            if deps.get(k, 0) < v:
                deps[k] = v
        for c in rc:
            st = cells.get(c)
            if st is not None and st[0] is not None:
                add(st[0])
        for c in wc:
            st = cells.get(c)
            if st is not None:
                if st[0] is not None:
                    add(st[0])
                for k, (v, sz) in st[1].items():
                    add((k, v, sz))
        return deps, rc, wc

    def _commit(self, tok, rc, wc):
        k, v = tok
        sz = self._msize
        cells = self.cells
        for c in rc:
            if c in wc:
                continue
            st = cells.get(c)
            if st is None:
                st = [None, {}]
                cells[c] = st
            old = st[1].get(k)
            if old is None or old[0] < v:
                st[1][k] = (v, sz)
        t3 = (k, v, sz)
        for c in wc:
            cells[c] = [t3, {}]

    def _waits(self, ename, deps):
        e = self.eng[ename]
        seen = self.seen[ename]
        own = self.sems.get(ename)
        for k, v in deps.items():
            if k == own and ename == "pe":
                continue
            if seen.get(k, 0) >= v:
                continue
            e.wait_ge(self.sem_id[k], v)
            seen[k] = v
            self.n_wait += 1
            if os.environ.get("K_TRACE"):
                print("   WAIT", ename, "sem", k, ">=", v)

    def I(self, ename, fn, reads, writes):
        deps, rc, wc = self._deps(reads, writes, self.sems.get(ename))
        self._waits(ename, deps)
        ins = fn()
        self.cnt[ename] += 1
        k = self.sems[ename]
        if os.environ.get("K_TRACE"):
            print("INS", ename, self.cnt[ename], str(ins)[:150])
        ins.then_inc(self.sem_id[k], 1)
        self._commit((k, self.cnt[ename]), rc, wc)
        self.n_inst += 1
        return ins

    def dma(self, q, out, in_, extra_wait=None):
        deps, rc, wc = self._deps([in_], [out])
        if extra_wait:
            for k, v in extra_wait:
                if deps.get(k, 0) < v:
                    deps[k] = v
        d = self.dq[q]
        i = d["next"]
        d["next"] = (i + 1) % N_DMA_SEMS
        k = d["sems"][i]
        if d["vals"][i] > 0:
            deps[k] = max(deps.get(k, 0), d["vals"][i])
        self._waits(q, deps)
        ins = self.eng[q].dma_start(out=out, in_=in_)
        d["vals"][i] += 16
        ins.then_inc(self.sem_id[k], 16)
        tok = (k, d["vals"][i])
        if os.environ.get("K_TRACE"):
            print("DMA", q, tok, str(ins)[:150])
        self._commit(tok, rc, wc)
        self.n_inst += 1
        return tok

    def finish(self):
        last = {}
        for q, d in self.dq.items():
            for k, v in zip(d["sems"], d["vals"]):
                if v > 0:
                    last[k] = v
        for e in ["pe", "act", "dve", "pool"]:
            if self.cnt[e] > 0:
                last[self.sems[e]] = self.cnt[e]
        self._waits("sp", last)

    def mm(self, out, lhsT, rhs, start=True, stop=True):
        return self.I("pe", lambda: self.nc.tensor.matmul(out, lhsT=lhsT, rhs=rhs, start=start, stop=stop),
                      [lhsT, rhs], [out])

    def tr(self, out, in_, ident):
        return self.I("pe", lambda: self.nc.tensor.transpose(out, in_, ident), [in_, ident], [out])

    def act(self, out, in_, func, bias=None, scale=1.0, accum_out=None):
        reads = [in_]
        kw = {}
        if bias is not None:
            kw["bias"] = bias
            if not isinstance(bias, (int, float)):
                reads.append(bias)
        if not isinstance(scale, (int, float)):
            reads.append(scale)
        writes = [out]
        if accum_out is not None:
            kw["accum_out"] = accum_out
            writes.append(accum_out)
        return self.I("act", lambda: self.nc.scalar.activation(out=out, in_=in_, func=func, scale=scale, **kw),
                      reads, writes)

    def tt(self, e, out, in0, in1, op):
        return self.I(e, lambda: self.eng[e].tensor_tensor(out=out, in0=in0, in1=in1, op=op), [in0, in1], [out])

    def ts(self, e, out, in0, s1, op0, s2=None, op1=None):
        reads = [in0]
        if not isinstance(s1, (int, float)):
            reads.append(s1)
        if s2 is not None and not isinstance(s2, (int, float)):
            reads.append(s2)
        kw = {}
        if op1 is not None:
            kw["op1"] = op1
        return self.I(e, lambda: self.eng[e].tensor_scalar(out=out, in0=in0, scalar1=s1, scalar2=s2, op0=op0, **kw),
                      reads, [out])

    def stt(self, e, out, in0, scalar, in1, op0, op1):
        reads = [in0, in1]
        if not isinstance(scalar, (int, float)):
            reads.append(scalar)
        return self.I(e, lambda: self.eng[e].scalar_tensor_tensor(out=out, in0=in0, scalar=scalar, in1=in1,
                                                                  op0=op0, op1=op1), reads, [out])

    def cp(self, e, out, in_):
        if e == "act":
            return self.I("act", lambda: self.nc.scalar.copy(out=out, in_=in_), [in_], [out])
        return self.I(e, lambda: self.eng[e].tensor_copy(out=out, in_=in_), [in_], [out])

    def memset(self, e, out, val):
        return self.I(e, lambda: self.eng[e].memset(out, val), [], [out])

    def recip(self, out, in_):
        return self.I("dve", lambda: self.nc.vector.reciprocal(out=out, in_=in_), [in_], [out])

    def reduce_add(self, out, in_):
        return self.I("dve", lambda: self.nc.vector.tensor_reduce(out=out, in_=in_, axis=AX.X, op=ALU.add),
                      [in_], [out])


class Seq:
    def __init__(self, kind, b):
        self.kind = kind
        self.b = b
        if kind == "p":
            self.S, self.TP, self.NT, self.QB, self.NBLK, self.TPB = S, 128, 16, 512, 4, 4
            self.SK, self.NKT = S, 16
        else:
            self.S, self.TP, self.NT, self.QB, self.NBLK, self.TPB = DS, DS, 1, DS, 1, 1
            self.SK, self.NKT = PAST + DS, 9


class Prog:
    def __init__(self):
        self.kb = KB()
        kb = self.kb
        nc = kb.nc
        self.nc = nc

        def din(name, shape, dt=F32):
            return nc.dram_tensor(name, list(shape), dt, kind="ExternalInput").ap()

        def dout(name, shape):
            return nc.dram_tensor(name, list(shape), F32, kind="ExternalOutput").ap()

        def dint(name, shape):
            return nc.dram_tensor(name, list(shape), BF16, kind="Internal").ap()

        self.xp = din("xp", [NB, S, D])
        self.xs = din("xs", [DS, D])
        self.ck = din("ck", [L, PAST, 512])
        self.cv = din("cv", [L, PAST, 512])
        self.cmk = din("cmk", [L, MEM, D])
        self.cmv = din("cmv", [L, MEM, D])
        self.memp = din("memp", [NB, MEM, D])
        self.wf = {
            "w_in": din("w_in", [L, D, 2560]), "w_out": din("w_out", [L, D, D]),
            "wq": din("wq", [L, D, D]), "wk": din("wk", [L, D, D]), "wv": din("wv", [L, D, D]),
            "wo": din("wo", [L, D, D]), "w_up": din("w_up", [L, D, 2 * DFF]), "w_down": din("w_down", [L, DFF, D]),
        }
        self.wb = {k: dint("b_" + k, v.shape) for k, v in self.wf.items()}
        self.rep = din("rep", [L, 128, NREP])
        self.fm = din("fm", [L, 128, NFM])
        self.wst = din("wst", [L, 128, 512])
        self.ident_d = din("ident", [128, 128])
        self.tril_d = din("tril", [128, 128])
        self.csp_d = din("csp", [128, 16, 16])
        self.css_d = din("css", [DS, 16])
        self.convst_d = din("convst", [128, L, NFC, 2])
        self.yp = dout("yp", [NB, S, D])
        self.ys = dout("ys", [DS, D])
        self.dkp = dout("dkp", [L, NB, S, 512])
        self.dvp = dout("dvp", [L, NB, S, 512])
        self.mkp = dout("mkp", [L, NB, MEM, D])
        self.mvp = dout("mvp", [L, NB, MEM, D])
        self.fcp = dout("fcp", [L, NB, 2, DFF])
        self.dks = dout("dks", [L, DS, 512])
        self.dvs = dout("dvs", [L, DS, 512])
        self.gvs = dout("gvs", [L, DS, 512])
        self.fcs = dout("fcs", [L, 2, DFF])

        ARENA_BYTES = 212800
        self.arena = nc.alloc_sbuf_tensor("arena", [128, ARENA_BYTES // 2], BF16)
        self.ps = nc.alloc_psum_tensor("ps", [128, 8, 512], F32)
        self.X_OFF = 0
        self.HB_OFF = 65536
        self.KT_OFF = 98304
        self.VA_OFF = 114688
        self.W_OFF = 131328
        self.C_OFF = 188672
        self.cast_tok = {}
        self.ps_rr = {"mm": 0, "pair": 0, "acc": 0}
        self.lctr = 0
        self.build()

    def V(self, off, shape, dt, P=128):
        n = int(np.prod(shape)) * _esize(dt)
        assert off % 4 == 0
        v = self.arena[0:P, off // 2:(off + n) // 2]
        if dt != BF16:
            v = v.bitcast(dt)
        if len(shape) == 2:
            v = v.rearrange("p (a b) -> p a b", a=shape[0])
        elif len(shape) == 3:
            v = v.rearrange("p (a b c) -> p a b c", a=shape[0], b=shape[1])
        return v

    def wslot(self, i, shape):
        return self.V(self.W_OFF + i * 8192, shape, BF16)

    def ps_mm(self):
        i = self.ps_rr["mm"]
        self.ps_rr["mm"] = (i + 1) % 4
        return self.ps[:, i, :]

    def ps_pair(self):
        i = self.ps_rr["pair"]
        self.ps_rr["pair"] = (i + 1) % 2
        return self.ps[:, 2 * i:2 * i + 2, :]

    def ps_tr(self, n, w):
        return self.ps[:, 7, 0:(n * w) // 2].bitcast(BF16).rearrange("p (a b) -> p a b", a=n)

    def rstd(self, out, ss, dim, P):
        kb = self.kb
        kb.act(out, ss, AF.Ln, scale=1.0 / dim, bias=EPS)
        kb.act(out, out, AF.Exp, scale=-0.5)

    def build(self):
        kb = self.kb
        c = self.C_OFF
        self.identf = self.V(c, [128], F32); c += 512
        self.identb = self.V(c, [128], BF16); c += 256
        self.tril = self.V(c, [128], F32); c += 512
        self.csp = self.V(c, [16, 16], F32); c += 1024
        self.css = self.V(c, [16], F32); c += 64
        self.lslot = []
        for i in range(2):
            d = {}
            d["rep"] = self.V(c, [NREP], F32); c += NREP * 4
            d["fm"] = self.V(c, [NFM], F32); c += NFM * 4
            d["wsT"] = self.V(c, [4, 128], BF16); c += 1024
            d["lam"] = self.V(c, [16], F32); c += 64
            self.lslot.append(d)
        self.mkT = self.V(c, [8, MEM], BF16); c += 4096
        self.MVA = self.V(c, [2, 4, 258], BF16); c += 4128
        self.stat = self.V(c, [128], F32); c += 512
        assert c <= 212800, c

        kb.dma("sp", self.identf, self.ident_d[:, :])
        kb.dma("sp", self.tril, self.tril_d[:, :])
        kb.dma("sp", self.csp, self.csp_d[:, :, :])
        kb.dma("sp", self.css[0:DS], self.css_d[:, :])
        kb.cp("dve", self.identb, self.identf)
        kb.memset("dve", self.MVA[:, :, :, 256:258], 1.0)

        for l in range(L):
            for name in ["wk", "wv", "w_in", "w_out", "wq", "wo", "w_up", "w_down"]:
                src = self.wf[name]
                dst = self.wb[name]
                rows = src.shape[1]
                toks = []
                for r0 in range(0, rows, 128):
                    toks.append(kb.dma("pool", dst[l, r0:r0 + 128, :], src[l, r0:r0 + 128, :]))
                self.cast_tok[(name, l)] = toks

        import os
        self.dbg = int(os.environ.get("K_DBG", "9"))
        self.dbg_nl = int(os.environ.get("K_NL", str(L)))
        seqs = [Seq("p", b) for b in range(NB)] + [Seq("s", 0)]
        sel = os.environ.get("K_SEQS")
        if sel is not None:
            seqs = [seqs[int(i)] for i in sel.split(",")]
        if os.environ.get("K_NOCAST"):
            pass
        self.va_ones_done = None
        for sq in seqs:
            self.run_seq(sq)
        kb.finish()

    def load_panel(self, slot, name, l, rows, cols, shape):
        src = self.wb[name][l, rows[0]:rows[1], cols[0]:cols[1]].rearrange("(k p) n -> p k n", p=128)
        dst = self.wslot(slot, shape)
        self.kb.dma("sp", dst, src, extra_wait=self.cast_tok[(name, l)])
        return dst

    def run_seq(self, sq):
        kb = self.kb
        TP, NT = sq.TP, sq.NT
        self.X = self.V(self.X_OFF, [16, D], F32)
        for t in range(NT):
            if sq.kind == "p":
                kb.dma("sp", self.X[:, t, :], self.xp[sq.b, t * 128:(t + 1) * 128, :])
            else:
                kb.dma("sp", self.X[0:TP, 0, :], self.xs[:, :])
        self.KT = self.V(self.KT_OFF, [4, sq.SK], BF16)
        self.VA = self.V(self.VA_OFF, [sq.NKT, 4, 130], BF16)
        for l in range(self.dbg_nl):
            self.run_layer(sq, l)
        for t in range(NT):
            if sq.kind == "p":
                kb.dma("sp", self.yp[sq.b, t * 128:(t + 1) * 128, :], self.X[:, t, :])
            else:
                kb.dma("sp", self.ys[:, :], self.X[0:TP, 0, :])

    def norm_transpose(self, sq, xrow, gcol, outT, xn, sqr, P):
        kb = self.kb
        ss = self.stat[0:P, 0:1]
        rs = self.stat[0:P, 1:2]
        kb.act(sqr, xrow, AF.Square, accum_out=ss)
        self.rstd(rs, ss, D, P)
        kb.ts("dve", xn, xrow, rs, ALU.mult)
        pt = self.ps_tr(8, P)
        for kc in range(8):
            kb.tr(pt[:, kc, :], xn[:, kc * 128:(kc + 1) * 128], self.identb[0:P, 0:P])
        kb.tt("dve", outT, pt, gcol.unsqueeze(2).broadcast_to([128, 8, P]), ALU.mult)

    def run_layer(self, sq, l):
        kb = self.kb
        TP, NT = sq.TP, sq.NT
        isp = sq.kind == "p"
        slot = self.lslot[self.lctr % 2]
        self.lctr += 1
        lam_init = 0.8 - 0.6 * float(np.exp(-0.3 * l))
        HB = self.HB_OFF

        kb.dma("sp", slot["rep"], self.rep[l, :, :])
        kb.dma("sp", slot["fm"], self.fm[l, :, :])
        wst_f = self.V(HB, [4, 128], F32)
        kb.dma("sp", wst_f, self.wst[l, :, :].rearrange("p (g t) -> p g t", g=4))
        rep = slot["rep"]
        g_qk = rep[:, 0:128].rearrange("p (a d) -> p a d", a=2)
        g_sub = rep[:, 128:256]
        g_gmn = rep[:, 256:384]
        g_xq = rep[:, 384:640]
        g_xk = rep[:, 640:896]
        lamv = rep[:, 896:1152].rearrange("p (a d) -> p a d", a=4)
        fmv = slot["fm"]
        gfm = fmv[:, 0:32].rearrange("p (a k) -> p a k", a=4)
        gm_b = fmv[:, 32:36]
        conv_w = fmv[:, 36:102].rearrange("p (f j) -> p f j", f=NFC)
        conv_b = fmv[:, 102:124]
        lam = slot["lam"]
        kb.tt("dve", slot["wsT"], wst_f, self.tril.unsqueeze(1).broadcast_to([128, 4, 128]), ALU.mult)
        prod = self.V(HB + 2048, [2, 64], F32)
        kb.tt("dve", prod[:, 0, :], lamv[:, 0, :], lamv[:, 1, :], ALU.mult)
        kb.tt("dve", prod[:, 1, :], lamv[:, 2, :], lamv[:, 3, :], ALU.mult)
        kb.reduce_add(lam[:, 0:2], prod)
        kb.act(lam[:, 0:2], lam[:, 0:2], AF.Exp)
        kb.tt("dve", lam[:, 2:3], lam[:, 1:2], lam[:, 0:1], ALU.subtract)
        kb.ts("dve", lam[:, 3:4], lam[:, 2:3], -lam_init, ALU.add)
        kb.ts("dve", g_sub, g_sub, 1.0 - lam_init, ALU.mult)

        if isp:
            wk_p = [self.load_panel(3 + i, "wk", l, (0, D), (i * 512, (i + 1) * 512), [8, 512]) for i in range(2)]
            wv_p = [self.load_panel(5 + i, "wv", l, (0, D), (i * 512, (i + 1) * 512), [8, 512]) for i in range(2)]
        win = [None] * 5
        for i in range(3):
            win[i] = self.load_panel(i, "w_in", l, (0, D), (i * 512, (i + 1) * 512), [8, 512])

        if self.dbg < 2:
            return
        self.stage_mem(sq, l, slot, gfm, g_xk, wk_p if isp else None, wv_p if isp else None)

        if self.dbg < 3:
            return
        for i in range(3, 5):
            win[i] = self.load_panel(i, "w_in", l, (0, D), (i * 512, (i + 1) * 512), [8, 512])
        woA = self.load_panel(5, "w_out", l, (0, 512), (0, D), [4, D])
        woB = self.load_panel(6, "w_out", l, (512, D), (0, D), [4, D])

        if not os.environ.get("K_NOMEMSET"):
            kb.memset("dve", self.VA[:, :, :, 128:130], 1.0)
        if not isp:
            self.load_past(sq, l)

        QT = self.V(HB + 18432, [4, 512], BF16)
        for blk in range(sq.NBLK):
            for ti in range(sq.TPB):
                t = blk * sq.TPB + ti
                if t >= int(os.environ.get("K_NT", "99")):
                    continue
                self.stage_a_tile(sq, l, t, ti, slot, gfm, g_qk, g_gmn, gm_b, win, woB, QT)
            if blk == sq.NBLK - 1:
                wq_p = [self.load_panel(i, "wq", l, (0, D), (i * 512, (i + 1) * 512), [8, 512]) for i in range(2)]
                wo_p = [self.load_panel(2 + i, "wo", l, (0, D), (i * 512, (i + 1) * 512), [8, 512]) for i in range(2)]
            if not os.environ.get("K_SKIPB"):
                self.stage_b_block(sq, l, blk, slot, g_sub, lam, woA, QT)

        if self.dbg < 4:
            return
        fpan = {}
        fpan[0] = self.load_ffn_group(l, 0, 4)

        for t in range(NT):
            self.stage_c_tile(sq, l, t, gfm, g_xq, wq_p, wo_p)

        if self.dbg < 5:
            return
        fpan[1] = self.load_ffn_group(l, 1, 0)

        self.stage_ffn(sq, l, conv_w, conv_b, fpan)

    def load_ffn_group(self, l, g, s0):
        nf = 4 if g < 5 else 2
        c0 = g * 512
        gate = self.load_panel(s0, "w_up", l, (0, D), (c0, c0 + nf * 128), [8, nf * 128])
        up = self.load_panel(s0 + 1, "w_up", l, (0, D), (DFF + c0, DFF + c0 + nf * 128), [8, nf * 128])
        down = self.load_panel(s0 + 2, "w_down", l, (c0, c0 + nf * 128), (0, D), [nf, D])
        return gate, up, down, nf

    def stage_mem(self, sq, l, slot, gfm, g_xk, wk_p, wv_p):
        kb = self.kb
        HB = self.HB_OFF
        isp = sq.kind == "p"
        memt = self.V(HB + 4096, [D], F32)
        xn = self.V(HB + 8192, [D], BF16)
        mT = self.V(HB + 10240, [8, 128], BF16)
        kst = self.V(HB + 12288, [D], F32)
        kbf = self.V(HB + 16384, [D], BF16)
        vst = self.V(HB + 18432, [D], F32)
        sqr = self.V(HB + 22528, [D], F32)
        ss4 = self.stat[:, 4:8]
        rs4 = self.stat[:, 8:12]
        for mt in range(2):
            if isp:
                kb.dma("sp", memt, self.memp[sq.b, mt * 128:(mt + 1) * 128, :])
                self.norm_transpose(sq, memt, gfm[:, 3, :], mT, xn, sqr, 128)
                pk = self.ps_pair()
                for cb in range(2):
                    for kc in range(8):
                        kb.mm(pk[:, cb, :], lhsT=mT[:, kc, :], rhs=wk_p[cb][:, kc, :], start=(kc == 0), stop=(kc == 7))
                pk2 = pk.rearrange("p a b -> p (a b)")
                kb.act(sqr, pk2, AF.Square)
                kb.reduce_add(ss4, sqr.rearrange("p (h d) -> p h d", h=4))
                self.rstd(rs4, ss4, 256, 128)
                k3 = kst.rearrange("p (h d) -> p h d", h=4)
                kb.tt("dve", k3, pk2.rearrange("p (h d) -> p h d", h=4),
                      rs4.unsqueeze(2).broadcast_to([128, 4, 256]), ALU.mult)
                kb.tt("dve", k3, k3, g_xk.unsqueeze(1).broadcast_to([128, 4, 256]), ALU.mult)
                kb.dma("sp", self.mkp[l, sq.b, mt * 128:(mt + 1) * 128, :], kst)
                kb.cp("act", kbf, kst)
                pv = self.ps_pair()
                for cb in range(2):
                    for kc in range(8):
                        kb.mm(pv[:, cb, :], lhsT=mT[:, kc, :], rhs=wv_p[cb][:, kc, :], start=(kc == 0), stop=(kc == 7))
                pv2 = pv.rearrange("p a b -> p (a b)")
                kb.cp("act", vst, pv2)
                kb.dma("sp", self.mvp[l, sq.b, mt * 128:(mt + 1) * 128, :], vst)
                kb.cp("dve", self.MVA[:, mt, :, 0:256], vst.rearrange("p (h d) -> p h d", h=4))
            else:
                kb.dma("sp", kst, self.cmk[l, mt * 128:(mt + 1) * 128, :])
                kb.cp("act", kbf, kst)
                kb.dma("sp", vst, self.cmv[l, mt * 128:(mt + 1) * 128, :])
                kb.cp("dve", self.MVA[:, mt, :, 0:256], vst.rearrange("p (h d) -> p h d", h=4))
            pt = self.ps_tr(8, 128)
            for kc in range(8):
                kb.tr(pt[:, kc, :], kbf[:, kc * 128:(kc + 1) * 128], self.identb)
            kb.cp("dve", self.mkT[:, :, mt * 128:(mt + 1) * 128], pt)

    def load_past(self, sq, l):
        kb = self.kb
        HB = self.HB_OFF
        st = self.V(HB + 4096, [512], F32)
        sb = self.V(HB + 6144, [512], BF16)
        for kt in range(8):
            kb.dma("sp", st, self.ck[l, kt * 128:(kt + 1) * 128, :])
            kb.cp("act", sb, st)
            pt = self.ps_tr(4, 128)
            for h in range(4):
                kb.tr(pt[:, h, :], sb[:, h * 128:(h + 1) * 128], self.identb)
            kb.cp("dve", self.KT[:, :, kt * 128:(kt + 1) * 128], pt)
            st2 = self.V(HB + 8192, [512], F32)
            kb.dma("sp", st2, self.cv[l, kt * 128:(kt + 1) * 128, :])
            kb.cp("dve", self.VA[:, kt, :, 0:128], st2.rearrange("p (h d) -> p h d", h=4))

    def stage_a_tile(self, sq, l, t, ti, slot, gfm, g_qk, g_gmn, gm_b, win, woB, QT):
        kb = self.kb
        P = sq.TP
        HB = self.HB_OFF
        isp = sq.kind == "p"
        o = HB
        xn = self.V(o, [D], BF16, P); o += 2048
        hT = self.V(o, [8, P], BF16); o += 2048
        sqr = self.V(o, [D], F32, P); o += 4096
        zq = self.V(o, [512], F32, P); o += 2048
        zk = self.V(o, [512], F32, P); o += 2048
        zv = self.V(o, [512], F32, P); o += 2048
        qkb = self.V(o, [D], BF16, P); o += 2048
        gvb = self.V(HB, [512], BF16, P)
        gob = self.V(HB + 1024, [512], BF16, P)
        goT = self.V(o, [4, P], BF16); o += 1024
        rope = self.V(o, [4, 64], F32, P); o += 1024
        assert o <= HB + 18432
        cs = self.csp[:, t, :] if isp else self.css[0:P, :]
        cosv = cs[:, 0:8]
        sinv = cs[:, 8:16]
        xrow = self.X[0:P, t, :]

        astep = float(os.environ.get("K_ASTEP", "99"))
        self.norm_transpose(sq, xrow, gfm[:, 0, :], hT, xn, sqr, P)
        if astep < 1:
            return

        def proj(cg):
            pz = self.ps_mm()[0:P, :]
            for kc in range(8):
                kb.mm(pz, lhsT=hT[:, kc, :], rhs=win[cg][:, kc, :], start=(kc == 0), stop=(kc == 7))
            return pz

        ss8 = self.stat[0:P, 16:24]
        rs8 = self.stat[0:P, 24:32]
        sq512 = sqr[:, 0:512]
        for which, dst in ((0, zq), (1, zk)):
            pz = proj(which)
            if astep < 2:
                kb.cp("act", dst, pz)
                continue
            kb.act(sq512, pz, AF.Square)
            kb.reduce_add(ss8, sq512.rearrange("p (g d) -> p g d", g=8))
            self.rstd(rs8, ss8, 64, P)
            d3 = dst.rearrange("p (g d) -> p g d", g=8)
            kb.tt("dve", d3, pz.rearrange("p (g d) -> p g d", g=8), rs8.unsqueeze(2).broadcast_to([P, 8, 64]), ALU.mult)
            kb.tt("dve", d3, d3, g_qk[0:P, which, :].unsqueeze(1).broadcast_to([P, 8, 64]), ALU.mult)
            if astep < 3:
                continue
            x1 = d3[:, :, 0:8]
            x2 = d3[:, :, 8:16]
            cb_ = cosv.unsqueeze(1).broadcast_to([P, 8, 8])
            sb_ = sinv.unsqueeze(1).broadcast_to([P, 8, 8])
            r3 = rope.rearrange("p a (g d) -> p a g d", g=8)
            kb.tt("dve", r3[:, 0], x1, cb_, ALU.mult)
            kb.tt("dve", r3[:, 1], x2, sb_, ALU.mult)
            kb.tt("dve", r3[:, 2], x2, cb_, ALU.mult)
            kb.tt("dve", r3[:, 3], x1, sb_, ALU.mult)
            kb.tt("dve", x1, r3[:, 0], r3[:, 1], ALU.subtract)
            kb.tt("dve", x2, r3[:, 2], r3[:, 3], ALU.add)
            kb.cp("act", qkb[:, which * 512:(which + 1) * 512], dst)
        if isp:
            kb.dma("sp", self.dkp[l, sq.b, t * 128:(t + 1) * 128, :], zk)
        else:
            kb.dma("sp", self.dks[l, :, :], zk)
        if astep < 3.2:
            return
        pt = self.ps_tr(8, P)
        for j in range(8):
            kb.tr(pt[:, j, :], qkb[:, j * 128:(j + 1) * 128], self.identb[0:P, 0:P])
        kb.cp("dve", QT[:, :, ti * 128:ti * 128 + P], pt[:, 0:4, :])
        kpos = t * 128 if isp else PAST
        kb.cp("act", self.KT[:, :, kpos:kpos + P], pt[:, 4:8, :])
        if astep < 3.5:
            return
        pz = proj(2)
        kb.cp("act", zv, pz)
        if isp:
            kb.dma("sp", self.dvp[l, sq.b, t * 128:(t + 1) * 128, :], zv)
        else:
            kb.dma("sp", self.dvs[l, :, :], zv)
        vt = t if isp else 8
        if astep < 4:
            return
        kb.cp("dve", self.VA[0:P, vt, :, 0:128], zv.rearrange("p (h d) -> p h d", h=4))
        if astep < 5:
            return
        ug = zq
        pu = proj(3)
        kb.act(ug, pu, AF.Gelu_apprx_tanh)
        pg = proj(4)
        gg = sqr[:, 0:512]
        kb.act(gg, pg, AF.Gelu_apprx_tanh)
        sq2 = sqr[:, 512:1024]
        kb.act(sq2, gg, AF.Square)
        ss4 = self.stat[0:P, 4:8]
        rs4 = self.stat[0:P, 8:12]
        kb.reduce_add(ss4, sq2.rearrange("p (g d) -> p g d", g=4))
        self.rstd(rs4, ss4, 128, P)
        gg3 = gg.rearrange("p (g d) -> p g d", g=4)
        kb.tt("dve", gg3, gg3, rs4.unsqueeze(2).broadcast_to([P, 4, 128]), ALU.mult)
        kb.tt("dve", gg3, gg3, g_gmn[0:P, :].unsqueeze(1).broadcast_to([P, 4, 128]), ALU.mult)
        if not isp:
            kb.dma("sp", self.gvs[l, :, :], gg)
        kb.cp("act", gvb, gg)
        if astep < 6:
            return
        pgate = self.ps_mm()[0:P, :]
        for g in range(4):
            kb.mm(pgate[:, g * 128:(g + 1) * 128], lhsT=slot["wsT"][0:P, g, 0:P], rhs=gvb[:, g * 128:(g + 1) * 128])
        go3 = sq2.rearrange("p (g d) -> p g d", g=4)
        kb.tt("dve", go3, pgate.rearrange("p (g d) -> p g d", g=4),
              gm_b[0:P, :].unsqueeze(2).broadcast_to([P, 4, 128]), ALU.add)
        kb.tt("dve", gob, sq2, ug, ALU.mult)
        if astep < 7:
            return
        pt2 = self.ps_tr(4, P)
        for j in range(4):
            kb.tr(pt2[:, j, :], gob[:, j * 128:(j + 1) * 128], self.identb[0:P, 0:P])
        kb.cp("act", goT, pt2)
        py = self.ps_pair()
        for cb in range(2):
            for kc in range(4):
                kb.mm(py[0:P, cb, :], lhsT=goT[:, kc, :], rhs=woB[:, kc, cb * 512:(cb + 1) * 512],
                      start=(kc == 0), stop=(kc == 3))
        kb.tt("dve", xrow, py[0:P].rearrange("p a b -> p (a b)"), xrow, ALU.add)

    def stage_b_block(self, sq, l, blk, slot, g_sub, lam, woA, QT):
        kb = self.kb
        HB = self.HB_OFF
        isp = sq.kind == "p"
        P = sq.TP
        QB = sq.QB
        o = HB + 18432 + 4096
        eTs = [self.V(o + i * 2048, [2, 512], BF16) for i in range(2)]; o += 4096
        otmp = self.V(o, [128], F32); o += 512
        o += 512
        OF = self.V(HB, [4, 512], F32)
        osq = self.V(HB + 8192, [512], F32)
        OB = self.V(o, [4, 512], BF16); o += 4096
        oT = self.V(o, [4, 128], BF16); o += 1024
        assert o <= HB + 32768, o
        nqs = sq.TPB
        if isp:
            nkt = 4 * blk + 4
        else:
            nkt = 9
        rz = self.stat[0:P, 32:48]
        ssq = self.stat[0:P, 48:49]
        rso = self.stat[0:P, 49:50]
        nl = self.stat[0:P, 50:51]
        ei = 0
        for h in range(4):
            def acc(qs, m):
                r = qs * 2 + m
                return self.ps[0:P, 4 + r // 3, (r % 3) * 160:(r % 3) * 160 + 129]
            for kt in range(nkt):
                KP = 128 if (isp or kt < 8) else DS
                j = kt - 4 * blk if isp else -1
                q0 = max(0, j) * 128
                kpos = kt * 128
                psS = self.ps_pair()
                for m in range(2):
                    kb.mm(psS[0:KP, m, q0:QB], lhsT=self.KT[m * 64:(m + 1) * 64, h, kpos:kpos + KP],
                          rhs=QT[m * 64:(m + 1) * 64, h, q0:QB])
                eT = eTs[ei % 2]
                ei += 1
                kb.act(eT[0:KP, :, q0:QB], psS[0:KP, :, q0:QB], AF.Exp, scale=0.125)
                if j >= 0:
                    kb.memset("dve", eT[64:128, :, q0:q0 + 64], 0.0)
                for qs in range(max(0, j), nqs):
                    last = (kt == 4 * blk + qs) if isp else (kt == nkt - 1)
                    for m in range(2):
                        kb.mm(acc(qs, m), lhsT=eT[0:KP, m, qs * 128:qs * 128 + P], rhs=self.VA[0:KP, kt, h, 0:129],
                              start=(kt == 0 and (qs * 2 + m) % 3 == 0), stop=last)
            for qs in range(nqs):
                a0 = acc(qs, 0)
                a1 = acc(qs, 1)
                kb.recip(rz[:, 0:1], a0[:, 128:129])
                kb.recip(rz[:, 1:2], a1[:, 128:129])
                kb.tt("dve", nl, rz[:, 1:2], lam[0:P, 3:4], ALU.mult)
                ot = otmp[0:P, :]
                kb.ts("dve", ot, a0[:, 0:128], rz[:, 0:1], ALU.mult)
                kb.stt("dve", OF[0:P, qs, h * 128:(h + 1) * 128], a1[:, 0:128], nl, ot, ALU.mult, ALU.add)
        ss4 = self.stat[0:P, 4:8]
        rs4 = self.stat[0:P, 8:12]
        for qs in range(nqs):
            t = blk * sq.TPB + qs
            of = OF[0:P, qs, :]
            kb.act(osq[0:P, :], of, AF.Square)
            kb.reduce_add(ss4, osq[0:P, :].rearrange("p (h d) -> p h d", h=4))
            self.rstd(rs4, ss4, 128, P)
            of3 = of.rearrange("p (h d) -> p h d", h=4)
            kb.tt("dve", of3, of3, rs4.unsqueeze(2).broadcast_to([P, 4, 128]), ALU.mult)
            kb.tt("dve", OB[0:P, qs, :].rearrange("p (h d) -> p h d", h=4), of3,
                  g_sub[0:P, :].unsqueeze(1).broadcast_to([P, 4, 128]), ALU.mult)
            pt = self.ps_tr(4, P)
            for j in range(4):
                kb.tr(pt[:, j, :], OB[0:P, qs, j * 128:(j + 1) * 128], self.identb[0:P, 0:P])
            kb.cp("act", oT[:, :, 0:P], pt)
            py = self.ps_pair()
            for cb in range(2):
                for kc in range(4):
                    kb.mm(py[0:P, cb, :], lhsT=oT[:, kc, 0:P], rhs=woA[:, kc, cb * 512:(cb + 1) * 512],
                          start=(kc == 0), stop=(kc == 3))
            xrow = self.X[0:P, t, :]
            kb.tt("dve", xrow, py[0:P].rearrange("p a b -> p (a b)"), xrow, ALU.add)

    def stage_c_tile(self, sq, l, t, gfm, g_xq, wq_p, wo_p):
        kb = self.kb
        P = sq.TP
        o = self.KT_OFF
        xn = self.V(o, [D], BF16, P); o += 2048
        h2T = self.V(o, [8, P], BF16); o += 2048
        sqr = self.V(o, [D], F32, P); o += 4096
        qcn = self.V(o, [D], BF16, P); o += 2048
        qcT = self.V(o, [8, P], BF16); o += 2048
        eT = self.V(o, [4, 2, P], BF16); o += 2048
        ocb = self.V(o, [D], BF16, P); o += 2048
        ocT = self.V(o, [8, P], BF16); o += 2048
        xrow = self.X[0:P, t, :]
        ss4 = self.stat[0:P, 4:8]
        rs4 = self.stat[0:P, 8:12]
        rz4 = self.stat[0:P, 12:16]
        self.norm_transpose(sq, xrow, gfm[:, 1, :], h2T, xn, sqr, P)
        pq = self.ps_pair()
        for cb in range(2):
            for kc in range(8):
                kb.mm(pq[0:P, cb, :], lhsT=h2T[:, kc, :], rhs=wq_p[cb][:, kc, :], start=(kc == 0), stop=(kc == 7))
        pq2 = pq[0:P].rearrange("p a b -> p (a b)")
        kb.act(sqr, pq2, AF.Square)
        kb.reduce_add(ss4, sqr.rearrange("p (h d) -> p h d", h=4))
        self.rstd(rs4, ss4, 256, P)
        s3 = sqr.rearrange("p (h d) -> p h d", h=4)
        kb.tt("dve", s3, pq2.rearrange("p (h d) -> p h d", h=4), rs4.unsqueeze(2).broadcast_to([P, 4, 256]), ALU.mult)
        kb.tt("dve", qcn.rearrange("p (h d) -> p h d", h=4), s3, g_xq[0:P, :].unsqueeze(1).broadcast_to([P, 4, 256]),
              ALU.mult)
        pt = self.ps_tr(8, P)
        for j in range(8):
            kb.tr(pt[:, j, :], qcn[:, j * 128:(j + 1) * 128], self.identb[0:P, 0:P])
        kb.cp("act", qcT, pt)
        psS = self.ps_pair()
        for h in range(4):
            for mt in range(2):
                c0 = (h * 2 + mt) * 128
                dst = psS[:, c0 // 512, (c0 % 512):(c0 % 512) + P]
                for dc in range(2):
                    kb.mm(dst, lhsT=self.mkT[:, h * 2 + dc, mt * 128:(mt + 1) * 128], rhs=qcT[:, h * 2 + dc, :],
                          start=(dc == 0), stop=(dc == 1))
        for half in range(2):
            kb.act(eT[:, half * 2:half * 2 + 2, :, :],
                   psS[:, half, :].rearrange("p (h m q) -> p h m q", h=2, m=2)[:, :, :, 0:P], AF.Exp, scale=1.0 / 16.0)
        for h in range(4):
            po = self.ps[0:P, 4 + (h % 3), 0:257]
            for mt in range(2):
                kb.mm(po, lhsT=eT[:, h, mt, :], rhs=self.MVA[:, mt, h, 0:257], start=(mt == 0), stop=(mt == 1))
            kb.recip(rz4[:, h:h + 1], po[:, 256:257])
            kb.ts("dve", ocb[:, h * 256:(h + 1) * 256], po[:, 0:256], rz4[:, h:h + 1], ALU.mult)
        pt2 = self.ps_tr(8, P)
        for j in range(8):
            kb.tr(pt2[:, j, :], ocb[:, j * 128:(j + 1) * 128], self.identb[0:P, 0:P])
        kb.cp("act", ocT, pt2)
        py = self.ps_pair()
        for cb in range(2):
            for kc in range(8):
                kb.mm(py[0:P, cb, :], lhsT=ocT[:, kc, :], rhs=wo_p[cb][:, kc, :], start=(kc == 0), stop=(kc == 7))
        kb.tt("dve", xrow, py[0:P].rearrange("p a b -> p (a b)"), xrow, ALU.add)
        HBv = self.V(self.HB_OFF, [8, sq.S], BF16)
        self.norm_transpose(sq, xrow, gfm[:, 2, :], HBv[:, :, t * 128:t * 128 + P], xn, sqr, P)

    def stage_ffn(self, sq, l, conv_w, conv_b, fpan):
        kb = self.kb
        isp = sq.kind == "p"
        P = sq.TP
        NTOK = sq.QB
        HBv = self.V(self.HB_OFF, [8, sq.S], BF16)
        o = self.KT_OFF
        gss = [self.V(o + i * 2064, [516], F32) for i in range(2)]; o += 4128
        ccs = [self.V(o + i * 2048, [512], F32) for i in range(2)]; o += 4096
        scs = [self.V(o + i * 2048, [512], F32) for i in range(2)]; o += 4096
        aTs = [self.V(o + i * 4096, [4, 512], BF16) for i in range(2)]; o += 8192
        carry = self.V(o, [NFC, 2], F32); o += 176
        cst = self.V(o, [512], F32, 2); o += 2048
        assert o <= self.KT_OFF + 33024
        if isp:
            kb.memset("dve", carry, 0.0)
        else:
            kb.dma("sp", carry, self.convst_d[:, l, :, :])
        ri = 0
        ai = 0
        for g in range(6):
            gate, up, down, nf = fpan[g]
            for blk in range(sq.NBLK):
                aT = aTs[ai % 2]
                ai += 1
                for fl in range(nf):
                    fc = g * 4 + fl
                    pg = self.ps_mm()
                    for kc in range(8):
                        kb.mm(pg[:, 0:NTOK], lhsT=gate[:, kc, fl * 128:(fl + 1) * 128],
                              rhs=HBv[:, kc, blk * 512:blk * 512 + NTOK], start=(kc == 0), stop=(kc == 7))
                    pu = self.ps_mm()
                    for kc in range(8):
                        kb.mm(pu[:, 0:NTOK], lhsT=up[:, kc, fl * 128:(fl + 1) * 128],
                              rhs=HBv[:, kc, blk * 512:blk * 512 + NTOK], start=(kc == 0), stop=(kc == 7))
                    gs = gss[ri % 2]
                    cc = ccs[ri % 2]
                    sc = scs[ri % 2]
                    ri += 1
                    kb.cp("act", gs[:, 2:2 + NTOK], pg[:, 0:NTOK])
                    kb.cp("dve", gs[:, 0:2], carry[:, fc, :])
                    kb.cp("dve", carry[:, fc, :], gs[:, NTOK:NTOK + 2])
                    kb.act(cc[:, 0:NTOK], pg[:, 0:NTOK], AF.Identity, bias=conv_b[:, fc:fc + 1], scale=conv_w[:, fc, 2:3])
                    kb.stt("dve", cc[:, 0:NTOK], gs[:, 1:1 + NTOK], conv_w[:, fc, 1:2], cc[:, 0:NTOK], ALU.mult, ALU.add)
                    kb.stt("dve", cc[:, 0:NTOK], gs[:, 0:NTOK], conv_w[:, fc, 0:1], cc[:, 0:NTOK], ALU.mult, ALU.add)
                    kb.act(sc[:, 0:NTOK], cc[:, 0:NTOK], AF.Silu)
                    kb.tt("dve", aT[:, fl, 0:NTOK], sc[:, 0:NTOK], pu[:, 0:NTOK], ALU.mult)
                for ti in range(sq.TPB):
                    t = blk * sq.TPB + ti
                    py = self.ps[0:P, 4:6, :]
                    for cb in range(2):
                        for fl in range(nf):
                            kb.mm(py[:, cb, :], lhsT=aT[:, fl, ti * 128:ti * 128 + P], rhs=down[:, fl, cb * 512:(cb + 1) * 512],
                                  start=(fl == 0), stop=(fl == nf - 1))
                    xrow = self.X[0:P, t, :]
                    kb.tt("dve", xrow, py.rearrange("p a b -> p (a b)"), xrow, ALU.add)
            pc = self.ps[0:2, 6, :]
            for fl in range(nf):
                kb.tr(pc[:, fl * 128:(fl + 1) * 128], carry[:, g * 4 + fl, :], self.identf)
            kb.cp("act", cst[:, 0:nf * 128], pc[:, 0:nf * 128])
            dst = self.fcp[l, sq.b, :, g * 512:g * 512 + nf * 128] if isp else self.fcs[l, :, g * 512:g * 512 + nf * 128]
            kb.dma("sp", dst, cst[:, 0:nf * 128])
            if g + 2 < 6:
                fpan[g + 2] = self.load_ffn_group(l, g + 2, 4 if (g % 2 == 0) else 0)


_PROG = None


def _get_prog():
    global _PROG
    if _PROG is None:
        _PROG = Prog()
    return _PROG


def _host_consts(inp):
    f32 = np.float32
    rep = np.zeros((L, 128, NREP), f32)
    fm = np.zeros((L, 128, NFM), f32)
    for l in range(L):
        v = np.concatenate([inp["da_q_norm_g"][l], inp["da_k_norm_g"][l], inp["da_subln_g"][l], inp["gm_norm_g"][l],
                            inp["xq_norm_g"][l], inp["xk_norm_g"][l], inp["lambda_q1"][l], inp["lambda_k1"][l],
                            inp["lambda_q2"][l], inp["lambda_k2"][l]]).astype(f32)
        rep[l] = np.broadcast_to(v[None, :], (128, NREP))
        for i, nm in enumerate(["norm_mix_g", "norm_x_g", "norm_ffn_g", "norm_mem_g"]):
            fm[l, :, i * 8:(i + 1) * 8] = inp[nm][l].reshape(8, 128).T
        fm[l, :, 32:36] = inp["gm_b"][l].T
        cw = inp["conv_w"][l].reshape(3, NFC, 128)
        fm[l, :, 36:102] = cw.transpose(2, 1, 0).reshape(128, NFC * 3)
        fm[l, :, 102:124] = inp["conv_b"][l].reshape(NFC, 128).T
    wst = np.ascontiguousarray(inp["gm_w_s"].transpose(0, 3, 1, 2)).reshape(L, 128, 512).astype(f32)
    ident = np.eye(128, dtype=f32)
    tril = np.triu(np.ones((128, 128), f32))
    half = 8
    inv = (np.float32(500000.0) ** (-np.arange(half, dtype=f32) / np.float32(half))).astype(f32)

    def cs(pos):
        ang = pos.astype(f32)[:, None] * inv[None, :]
        return np.concatenate([np.cos(ang), np.sin(ang)], axis=1).astype(f32)

    csp = cs(np.arange(S)).reshape(16, 128, 16).transpose(1, 0, 2)
    css = cs(PAST + np.arange(DS))
    return dict(rep=rep, fm=fm, wst=wst, ident=ident, tril=tril, csp=np.ascontiguousarray(csp), css=css)


def _in_maps(inp, cores):
    hc = _host_consts(inp)
    in_maps = []
    for c in cores:
        m = dict(hc)
        m["xp"] = np.ascontiguousarray(inp["x_prompt"][c * NB:(c + 1) * NB])
        m["xs"] = np.ascontiguousarray(inp["x_sample"][c])
        m["ck"] = np.ascontiguousarray(inp["cache_da_k"][:, c].reshape(L, PAST, 512))
        m["cv"] = np.ascontiguousarray(inp["cache_da_v"][:, c].reshape(L, PAST, 512))
        m["cmk"] = np.ascontiguousarray(inp["cache_mem_k"][:, c].reshape(L, MEM, D))
        m["cmv"] = np.ascontiguousarray(inp["cache_mem_v"][:, c].reshape(L, MEM, D))
        m["memp"] = np.ascontiguousarray(inp["mem_prompt"][c * NB:(c + 1) * NB])
        st = inp["state_ffn_conv"][:, c]
        m["convst"] = np.ascontiguousarray(st.reshape(L, 2, NFC, 128).transpose(3, 0, 2, 1))
        m["w_in"] = inp["w_in"]; m["w_out"] = inp["w_out"]
        m["wq"] = inp["wq_c"]; m["wk"] = inp["wk_c"]; m["wv"] = inp["wv_c"]; m["wo"] = inp["wo_c"]
        m["w_up"] = inp["w_up"]; m["w_down"] = inp["w_down"]
        in_maps.append(m)
    return in_maps


def kernel(**inp):
    inp = {k: np.asarray(v) for k, v in inp.items()}
    prog = _get_prog()
    n = 8
    in_maps = _in_maps(inp, range(n))
    res = run_bass_kernel_spmd(prog.nc, in_maps, core_ids=list(range(n))).results
    cat = lambda k, ax: np.concatenate([r[k] for r in res], axis=ax)
    y_p = cat("yp", 0)
    y_s = np.stack([r["ys"] for r in res], 0)
    dk_p = cat("dkp", 1).reshape(L, 32, S, 4, 2, 64)
    dv_p = cat("dvp", 1).reshape(L, 32, S, 4, 128)
    mk_p = cat("mkp", 1).reshape(L, 32, MEM, 4, 256)
    mv_p = cat("mvp", 1).reshape(L, 32, MEM, 4, 256)
    fc_p = cat("fcp", 1)
    dk_s = np.stack([r["dks"] for r in res], 1).reshape(L, 8, DS, 4, 2, 64)
    dv_s = np.stack([r["dvs"] for r in res], 1).reshape(L, 8, DS, 4, 128)
    gv_s = np.stack([r["gvs"] for r in res], 1).reshape(L, 8, DS, 4, 128)
    fc_s = np.stack([r["fcs"] for r in res], 1)
    return (y_p, y_s, dk_p, dv_p, mk_p, mv_p, fc_p, dk_s, dv_s, gv_s, fc_s)
```

```python
import os
import numpy as np
from contextlib import ExitStack
import concourse.bass as bass
import concourse.mybir as mybir
from concourse.bass_utils import run_bass_kernel_spmd

F32 = mybir.dt.float32
BF16 = mybir.dt.bfloat16
AF = mybir.ActivationFunctionType
ALU = mybir.AluOpType
AX = mybir.AxisListType
CELL = 256
SAME_ENGINE_SYNC = bool(int(os.environ.get("K_SES", "0")))
SMALL_T = int(os.environ.get("K_SMALL", "256"))
N_DMA_SEMS = 12

D = 1024
L = 4
NB = 4
S = 2048
DS = 32
PAST = 1024
MEM = 256
DFF = 2816
NFC = 22
EPS = 1e-6
NREP = 1152
NFM = 124


def _esize(dt):
    return 2 if dt == BF16 else 4


class KB:
    def __init__(self):
        self.nc = bass.Bass("TRN2", target_bir_lowering=False)
        nc = self.nc
        self.es = ExitStack()
        self.eng = {"pe": nc.tensor, "act": nc.scalar, "dve": nc.vector, "pool": nc.gpsimd, "sp": nc.sync}
        self.sems = {}
        self.sem_id = {}
        self.cnt = {}
        self._nsem = 0
        for e in ["pe", "act", "dve", "pool"]:
            self.sems[e] = self._newsem("c_" + e)
            self.cnt[e] = 0
        self.dq = {}
        for q in ["sp", "pool"]:
            pool = [self._newsem(f"d_{q}{i}") for i in range(N_DMA_SEMS)]
            self.dq[q] = {"sems": pool, "vals": [0] * N_DMA_SEMS, "next": 0}
        self.seen = {e: {} for e in ["pe", "act", "dve", "pool", "sp"]}
        self.cells = {}
        self.n_inst = 0
        self.n_wait = 0

    def _newsem(self, name):
        h = self.es.enter_context(self.nc.semaphore(name))
        k = self._nsem
        self._nsem += 1
        self.sem_id[k] = h
        return k

    def _cells(self, ap):
        sp = str(ap.space).upper()
        if "SB" not in sp and "PSUM" not in sp:
            return None
        es = _esize(ap.dtype)
        a = ap.ap
        pstep = a[0][0]
        off = ap.offset
        base = (off % pstep) * es if pstep > 0 else off * es
        region = ap.tensor.name
        cell = CELL if "SB" in sp else 2048
        dims = [(s * es, c) for (s, c) in a[1:]]
        if not dims:
            dims = [(es, 1)]
        ls, lc = dims[-1]
        run = (lc - 1) * ls + es
        starts = [base]
        for (s, c) in dims[:-1]:
            if s == 0 or c == 1:
                continue
            starts = [st + i * s for st in starts for i in range(c)]
        out = set()
        for st in starts:
            for c in range(st // cell, (st + run - 1) // cell + 1):
                out.add((region, c))
        fsz = 1
        for (s_, c_) in a[1:]:
            if s_ != 0:
                fsz *= c_
        self._last_fsz = fsz
        return out

    def _deps(self, reads, writes, own=None):
        deps = {}
        rc = set()
        wc = set()
        msize = 1 << 30
        for ap in reads:
            c = self._cells(ap)
            if c:
                msize = min(msize, self._last_fsz)
                if "PSUM" in str(ap.space).upper():
                    wc |= c
                else:
                    rc |= c
        for ap in writes:
            c = self._cells(ap)
            if c:
                msize = min(msize, self._last_fsz)
                wc |= c
        self._msize = msize
        cells = self.cells
        small = msize <= SMALL_T

        def add(tok):
            k, v, sz = tok
            if k == own and not SAME_ENGINE_SYNC and not (small or sz <= SMALL_T):
                return
            if deps.get(k, 0) < v:
                deps[k] = v
        for c in rc:
            st = cells.get(c)
            if st is not None and st[0] is not None:
                add(st[0])
        for c in wc:
            st = cells.get(c)
            if st is not None:
                if st[0] is not None:
                    add(st[0])
                for k, (v, sz) in st[1].items():
                    add((k, v, sz))
        return deps, rc, wc

    def _commit(self, tok, rc, wc):
        k, v = tok
        sz = self._msize
        cells = self.cells
        for c in rc:
            if c in wc:
                continue
            st = cells.get(c)
            if st is None:
                st = [None, {}]
                cells[c] = st
            old = st[1].get(k)
            if old is None or old[0] < v:
                st[1][k] = (v, sz)
        t3 = (k, v, sz)
        for c in wc:
            cells[c] = [t3, {}]

    def _waits(self, ename, deps):
        e = self.eng[ename]
        seen = self.seen[ename]
        own = self.sems.get(ename)
        for k, v in deps.items():
            if k == own and ename == "pe":
                continue
            if seen.get(k, 0) >= v:
                continue
            e.wait_ge(self.sem_id[k], v)
            seen[k] = v
            self.n_wait += 1
            if os.environ.get("K_TRACE"):
                print("   WAIT", ename, "sem", k, ">=", v)

    def I(self, ename, fn, reads, writes):
        deps, rc, wc = self._deps(reads, writes, self.sems.get(ename))
        self._waits(ename, deps)
        ins = fn()
        self.cnt[ename] += 1
        k = self.sems[ename]
        if os.environ.get("K_TRACE"):
            print("INS", ename, self.cnt[ename], str(ins)[:150])
        ins.then_inc(self.sem_id[k], 1)
        self._commit((k, self.cnt[ename]), rc, wc)
        self.n_inst += 1
        return ins

    def dma(self, q, out, in_, extra_wait=None):
        deps, rc, wc = self._deps([in_], [out])
        if extra_wait:
            for k, v in extra_wait:
                if deps.get(k, 0) < v:
                    deps[k] = v
        d = self.dq[q]
        i = d["next"]
        d["next"] = (i + 1) % N_DMA_SEMS
        k = d["sems"][i]
        if d["vals"][i] > 0:
            deps[k] = max(deps.get(k, 0), d["vals"][i])
        self._waits(q, deps)
        ins = self.eng[q].dma_start(out=out, in_=in_)
        d["vals"][i] += 16
        ins.then_inc(self.sem_id[k], 16)
        tok = (k, d["vals"][i])
        if os.environ.get("K_TRACE"):
            print("DMA", q, tok, str(ins)[:150])
        self._commit(tok, rc, wc)
        self.n_inst += 1
        return tok

    def finish(self):
        last = {}
        for q, d in self.dq.items():
            for k, v in zip(d["sems"], d["vals"]):
                if v > 0:
                    last[k] = v
        for e in ["pe", "act", "dve", "pool"]:
            if self.cnt[e] > 0:
                last[self.sems[e]] = self.cnt[e]
        self._waits("sp", last)

    def mm(self, out, lhsT, rhs, start=True, stop=True):
        return self.I("pe", lambda: self.nc.tensor.matmul(out, lhsT=lhsT, rhs=rhs, start=start, stop=stop),
                      [lhsT, rhs], [out])

    def tr(self, out, in_, ident):
        return self.I("pe", lambda: self.nc.tensor.transpose(out, in_, ident), [in_, ident], [out])

    def act(self, out, in_, func, bias=None, scale=1.0, accum_out=None):
        reads = [in_]
        kw = {}
        if bias is not None:
            kw["bias"] = bias
            if not isinstance(bias, (int, float)):
                reads.append(bias)
        if not isinstance(scale, (int, float)):
            reads.append(scale)
        writes = [out]
        if accum_out is not None:
            kw["accum_out"] = accum_out
            writes.append(accum_out)
        return self.I("act", lambda: self.nc.scalar.activation(out=out, in_=in_, func=func, scale=scale, **kw),
                      reads, writes)

    def tt(self, e, out, in0, in1, op):
        return self.I(e, lambda: self.eng[e].tensor_tensor(out=out, in0=in0, in1=in1, op=op), [in0, in1], [out])

    def ts(self, e, out, in0, s1, op0, s2=None, op1=None):
        reads = [in0]
        if not isinstance(s1, (int, float)):
            reads.append(s1)
        if s2 is not None and not isinstance(s2, (int, float)):
            reads.append(s2)
        kw = {}
        if op1 is not None:
            kw["op1"] = op1
        return self.I(e, lambda: self.eng[e].tensor_scalar(out=out, in0=in0, scalar1=s1, scalar2=s2, op0=op0, **kw),
                      reads, [out])

    def stt(self, e, out, in0, scalar, in1, op0, op1):
        reads = [in0, in1]
        if not isinstance(scalar, (int, float)):
            reads.append(scalar)
        return self.I(e, lambda: self.eng[e].scalar_tensor_tensor(out=out, in0=in0, scalar=scalar, in1=in1,
                                                                  op0=op0, op1=op1), reads, [out])

    def cp(self, e, out, in_):
        if e == "act":
            return self.I("act", lambda: self.nc.scalar.copy(out=out, in_=in_), [in_], [out])
        return self.I(e, lambda: self.eng[e].tensor_copy(out=out, in_=in_), [in_], [out])

    def memset(self, e, out, val):
        return self.I(e, lambda: self.eng[e].memset(out, val), [], [out])

    def recip(self, out, in_):
        return self.I("dve", lambda: self.nc.vector.reciprocal(out=out, in_=in_), [in_], [out])

    def reduce_add(self, out, in_):
        return self.I("dve", lambda: self.nc.vector.tensor_reduce(out=out, in_=in_, axis=AX.X, op=ALU.add),
                      [in_], [out])


class Seq:
    def __init__(self, kind, b):
        self.kind = kind
        self.b = b
        if kind == "p":
            self.S, self.TP, self.NT, self.QB, self.NBLK, self.TPB = S, 128, 16, 512, 4, 4
            self.SK, self.NKT = S, 16
        else:
            self.S, self.TP, self.NT, self.QB, self.NBLK, self.TPB = DS, DS, 1, DS, 1, 1
            self.SK, self.NKT = PAST + DS, 9


class Prog:
    def __init__(self):
        self.kb = KB()
        kb = self.kb
        nc = kb.nc
        self.nc = nc

        def din(name, shape, dt=F32):
            return nc.dram_tensor(name, list(shape), dt, kind="ExternalInput").ap()

        def dout(name, shape):
            return nc.dram_tensor(name, list(shape), F32, kind="ExternalOutput").ap()

        def dint(name, shape):
            return nc.dram_tensor(name, list(shape), BF16, kind="Internal").ap()

        self.xp = din("xp", [NB, S, D])
        self.xs = din("xs", [DS, D])
        self.ck = din("ck", [L, PAST, 512])
        self.cv = din("cv", [L, PAST, 512])
        self.cmk = din("cmk", [L, MEM, D])
        self.cmv = din("cmv", [L, MEM, D])
        self.memp = din("memp", [NB, MEM, D])
        self.wf = {
            "w_in": din("w_in", [L, D, 2560]), "w_out": din("w_out", [L, D, D]),
            "wq": din("wq", [L, D, D]), "wk": din("wk", [L, D, D]), "wv": din("wv", [L, D, D]),
            "wo": din("wo", [L, D, D]), "w_up": din("w_up", [L, D, 2 * DFF]), "w_down": din("w_down", [L, DFF, D]),
        }
        self.wb = {k: dint("b_" + k, v.shape) for k, v in self.wf.items()}
        self.rep = din("rep", [L, 128, NREP])
        self.fm = din("fm", [L, 128, NFM])
        self.wst = din("wst", [L, 128, 512])
        self.ident_d = din("ident", [128, 128])
        self.tril_d = din("tril", [128, 128])
        self.csp_d = din("csp", [128, 16, 16])
        self.css_d = din("css", [DS, 16])
        self.convst_d = din("convst", [128, L, NFC, 2])
        self.yp = dout("yp", [NB, S, D])
        self.ys = dout("ys", [DS, D])
        self.dkp = dout("dkp", [L, NB, S, 512])
        self.dvp = dout("dvp", [L, NB, S, 512])
        self.mkp = dout("mkp", [L, NB, MEM, D])
        self.mvp = dout("mvp", [L, NB, MEM, D])
        self.fcp = dout("fcp", [L, NB, 2, DFF])
        self.dks = dout("dks", [L, DS, 512])
        self.dvs = dout("dvs", [L, DS, 512])
        self.gvs = dout("gvs", [L, DS, 512])
        self.fcs = dout("fcs", [L, 2, DFF])

        ARENA_BYTES = 212800
        self.arena = nc.alloc_sbuf_tensor("arena", [128, ARENA_BYTES // 2], BF16)
        self.ps = nc.alloc_psum_tensor("ps", [128, 8, 512], F32)
        self.X_OFF = 0
        self.HB_OFF = 65536
        self.KT_OFF = 98304
        self.VA_OFF = 114688
        self.W_OFF = 131328
        self.C_OFF = 188672
        self.cast_tok = {}
        self.ps_rr = {"mm": 0, "pair": 0, "acc": 0}
        self.lctr = 0
        self.build()

    def V(self, off, shape, dt, P=128):
        n = int(np.prod(shape)) * _esize(dt)
        assert off % 4 == 0
        v = self.arena[0:P, off // 2:(off + n) // 2]
        if dt != BF16:
            v = v.bitcast(dt)
        if len(shape) == 2:
            v = v.rearrange("p (a b) -> p a b", a=shape[0])
        elif len(shape) == 3:
            v = v.rearrange("p (a b c) -> p a b c", a=shape[0], b=shape[1])
        return v

    def wslot(self, i, shape):
        return self.V(self.W_OFF + i * 8192, shape, BF16)

    def ps_mm(self):
        i = self.ps_rr["mm"]
        self.ps_rr["mm"] = (i + 1) % 4
        return self.ps[:, i, :]

    def ps_pair(self):
        i = self.ps_rr["pair"]
        self.ps_rr["pair"] = (i + 1) % 2
        return self.ps[:, 2 * i:2 * i + 2, :]

    def ps_tr(self, n, w):
        return self.ps[:, 7, 0:(n * w) // 2].bitcast(BF16).rearrange("p (a b) -> p a b", a=n)

    def rstd(self, out, ss, dim, P):
        kb = self.kb
        kb.act(out, ss, AF.Ln, scale=1.0 / dim, bias=EPS)
        kb.act(out, out, AF.Exp, scale=-0.5)

    def build(self):
        kb = self.kb
        c = self.C_OFF
        self.identf = self.V(c, [128], F32); c += 512
        self.identb = self.V(c, [128], BF16); c += 256
        self.tril = self.V(c, [128], F32); c += 512
        self.csp = self.V(c, [16, 16], F32); c += 1024
        self.css = self.V(c, [16], F32); c += 64
        self.lslot = []
        for i in range(2):
            d = {}
            d["rep"] = self.V(c, [NREP], F32); c += NREP * 4
            d["fm"] = self.V(c, [NFM], F32); c += NFM * 4
            d["wsT"] = self.V(c, [4, 128], BF16); c += 1024
            d["lam"] = self.V(c, [16], F32); c += 64
            self.lslot.append(d)
        self.mkT = self.V(c, [8, MEM], BF16); c += 4096
        self.MVA = self.V(c, [2, 4, 258], BF16); c += 4128
        self.stat = self.V(c, [128], F32); c += 512
        assert c <= 212800, c

        kb.dma("sp", self.identf, self.ident_d[:, :])
        kb.dma("sp", self.tril, self.tril_d[:, :])
        kb.dma("sp", self.csp, self.csp_d[:, :, :])
        kb.dma("sp", self.css[0:DS], self.css_d[:, :])
        kb.cp("dve", self.identb, self.identf)
        kb.memset("dve", self.MVA[:, :, :, 256:258], 1.0)

        for l in range(L):
            for name in ["wk", "wv", "w_in", "w_out", "wq", "wo", "w_up", "w_down"]:
                src = self.wf[name]
                dst = self.wb[name]
                rows = src.shape[1]
                toks = []
                for r0 in range(0, rows, 128):
                    toks.append(kb.dma("pool", dst[l, r0:r0 + 128, :], src[l, r0:r0 + 128, :]))
                self.cast_tok[(name, l)] = toks

        import os
        self.dbg = int(os.environ.get("K_DBG", "9"))
        self.dbg_nl = int(os.environ.get("K_NL", str(L)))
        seqs = [Seq("p", b) for b in range(NB)] + [Seq("s", 0)]
        sel = os.environ.get("K_SEQS")
        if sel is not None:
            seqs = [seqs[int(i)] for i in sel.split(",")]
        if os.environ.get("K_NOCAST"):
            pass
        self.va_ones_done = None
        for sq in seqs:
            self.run_seq(sq)
        kb.finish()

    def load_panel(self, slot, name, l, rows, cols, shape):
        src = self.wb[name][l, rows[0]:rows[1], cols[0]:cols[1]].rearrange("(k p) n -> p k n", p=128)
        dst = self.wslot(slot, shape)
        self.kb.dma("sp", dst, src, extra_wait=self.cast_tok[(name, l)])
        return dst

    def run_seq(self, sq):
        kb = self.kb
        TP, NT = sq.TP, sq.NT
        self.X = self.V(self.X_OFF, [16, D], F32)
        for t in range(NT):
            if sq.kind == "p":
                kb.dma("sp", self.X[:, t, :], self.xp[sq.b, t * 128:(t + 1) * 128, :])
            else:
                kb.dma("sp", self.X[0:TP, 0, :], self.xs[:, :])
        self.KT = self.V(self.KT_OFF, [4, sq.SK], BF16)
        self.VA = self.V(self.VA_OFF, [sq.NKT, 4, 130], BF16)
        for l in range(self.dbg_nl):
            self.run_layer(sq, l)
        for t in range(NT):
            if sq.kind == "p":
                kb.dma("sp", self.yp[sq.b, t * 128:(t + 1) * 128, :], self.X[:, t, :])
            else:
                kb.dma("sp", self.ys[:, :], self.X[0:TP, 0, :])

    def norm_transpose(self, sq, xrow, gcol, outT, xn, sqr, P, sc=0):
        kb = self.kb
        ss = self.stat[0:P, sc:sc + 1]
        rs = self.stat[0:P, sc + 1:sc + 2]
        kb.act(sqr, xrow, AF.Square, accum_out=ss)
        self.rstd(rs, ss, D, P)
        kb.ts("dve", xn, xrow, rs, ALU.mult)
        pt = self.ps_tr(8, P)
        for kc in range(8):
            kb.tr(pt[:, kc, :], xn[:, kc * 128:(kc + 1) * 128], self.identb[0:P, 0:P])
        kb.tt("dve", outT, pt, gcol.unsqueeze(2).broadcast_to([128, 8, P]), ALU.mult)

    def run_layer(self, sq, l):
        kb = self.kb
        TP, NT = sq.TP, sq.NT
        isp = sq.kind == "p"
        slot = self.lslot[self.lctr % 2]
        self.lctr += 1
        lam_init = 0.8 - 0.6 * float(np.exp(-0.3 * l))
        HB = self.HB_OFF

        kb.dma("sp", slot["rep"], self.rep[l, :, :])
        kb.dma("sp", slot["fm"], self.fm[l, :, :])
        wst_f = self.V(HB, [4, 128], F32)
        kb.dma("sp", wst_f, self.wst[l, :, :].rearrange("p (g t) -> p g t", g=4))
        rep = slot["rep"]
        g_qk = rep[:, 0:128].rearrange("p (a d) -> p a d", a=2)
        g_sub = rep[:, 128:256]
        g_gmn = rep[:, 256:384]
        g_xq = rep[:, 384:640]
        g_xk = rep[:, 640:896]
        lamv = rep[:, 896:1152].rearrange("p (a d) -> p a d", a=4)
        fmv = slot["fm"]
        gfm = fmv[:, 0:32].rearrange("p (a k) -> p a k", a=4)
        gm_b = fmv[:, 32:36]
        conv_w = fmv[:, 36:102].rearrange("p (f j) -> p f j", f=NFC)
        conv_b = fmv[:, 102:124]
        lam = slot["lam"]
        kb.tt("dve", slot["wsT"], wst_f, self.tril.unsqueeze(1).broadcast_to([128, 4, 128]), ALU.mult)
        prod = self.V(HB + 2048, [2, 64], F32)
        kb.tt("dve", prod[:, 0, :], lamv[:, 0, :], lamv[:, 1, :], ALU.mult)
        kb.tt("dve", prod[:, 1, :], lamv[:, 2, :], lamv[:, 3, :], ALU.mult)
        kb.reduce_add(lam[:, 0:2], prod)
        kb.act(lam[:, 0:2], lam[:, 0:2], AF.Exp)
        kb.tt("dve", lam[:, 2:3], lam[:, 1:2], lam[:, 0:1], ALU.subtract)
        kb.ts("dve", lam[:, 3:4], lam[:, 2:3], -lam_init, ALU.add)
        kb.ts("dve", g_sub, g_sub, 1.0 - lam_init, ALU.mult)

        if isp:
            wk_p = [self.load_panel(3 + i, "wk", l, (0, D), (i * 512, (i + 1) * 512), [8, 512]) for i in range(2)]
            wv_p = [self.load_panel(5 + i, "wv", l, (0, D), (i * 512, (i + 1) * 512), [8, 512]) for i in range(2)]
        win = [None] * 5
        for i in range(3):
            win[i] = self.load_panel(i, "w_in", l, (0, D), (i * 512, (i + 1) * 512), [8, 512])

        if self.dbg < 2:
            return
        self.stage_mem(sq, l, slot, gfm, g_xk, wk_p if isp else None, wv_p if isp else None)

        if self.dbg < 3:
            return
        for i in range(3, 5):
            win[i] = self.load_panel(i, "w_in", l, (0, D), (i * 512, (i + 1) * 512), [8, 512])
        woA = self.load_panel(5, "w_out", l, (0, 512), (0, D), [4, D])
        woB = self.load_panel(6, "w_out", l, (512, D), (0, D), [4, D])

        if not os.environ.get("K_NOMEMSET"):
            kb.memset("dve", self.VA[:, :, :, 128:130], 1.0)
        if not isp:
            self.load_past(sq, l)

        QT = self.V(HB + 18432, [4, 512], BF16)
        for blk in range(sq.NBLK):
            for ti in range(sq.TPB):
                t = blk * sq.TPB + ti
                if t >= int(os.environ.get("K_NT", "99")):
                    continue
                self.stage_a_tile(sq, l, t, ti, slot, gfm, g_qk, g_gmn, gm_b, win, woB, QT)
            if blk == sq.NBLK - 1:
                wq_p = [self.load_panel(i, "wq", l, (0, D), (i * 512, (i + 1) * 512), [8, 512]) for i in range(2)]
                wo_p = [self.load_panel(2 + i, "wo", l, (0, D), (i * 512, (i + 1) * 512), [8, 512]) for i in range(2)]
            if not os.environ.get("K_SKIPB"):
                self.stage_b_block(sq, l, blk, slot, g_sub, lam, woA, QT)

        if self.dbg < 4:
            return
        fpan = {}
        fpan[0] = self.load_ffn_group(l, 0, 4)

        self.stage_c(sq, l, gfm, g_xq, wq_p, wo_p)

        if self.dbg < 5:
            return
        fpan[1] = self.load_ffn_group(l, 1, 0)

        self.stage_ffn(sq, l, conv_w, conv_b, fpan)

    def load_ffn_group(self, l, g, s0):
        nf = 4 if g < 5 else 2
        c0 = g * 512
        gate = self.load_panel(s0, "w_up", l, (0, D), (c0, c0 + nf * 128), [8, nf * 128])
        up = self.load_panel(s0 + 1, "w_up", l, (0, D), (DFF + c0, DFF + c0 + nf * 128), [8, nf * 128])
        down = self.load_panel(s0 + 2, "w_down", l, (c0, c0 + nf * 128), (0, D), [nf, D])
        return gate, up, down, nf

    def stage_mem(self, sq, l, slot, gfm, g_xk, wk_p, wv_p):
        kb = self.kb
        HB = self.HB_OFF
        isp = sq.kind == "p"
        memt = self.V(HB + 4096, [D], F32)
        xn = self.V(HB + 8192, [D], BF16)
        mT = self.V(HB + 10240, [8, 128], BF16)
        kst = self.V(HB + 12288, [D], F32)
        kbf = self.V(HB + 16384, [D], BF16)
        vst = self.V(HB + 18432, [D], F32)
        sqr = self.V(HB + 22528, [D], F32)
        ss4 = self.stat[:, 4:8]
        rs4 = self.stat[:, 8:12]
        for mt in range(2):
            if isp:
                kb.dma("sp", memt, self.memp[sq.b, mt * 128:(mt + 1) * 128, :])
                self.norm_transpose(sq, memt, gfm[:, 3, :], mT, xn, sqr, 128)
                pk = self.ps_pair()
                for cb in range(2):
                    for kc in range(8):
                        kb.mm(pk[:, cb, :], lhsT=mT[:, kc, :], rhs=wk_p[cb][:, kc, :], start=(kc == 0), stop=(kc == 7))
                pk2 = pk.rearrange("p a b -> p (a b)")
                kb.act(sqr, pk2, AF.Square)
                kb.reduce_add(ss4, sqr.rearrange("p (h d) -> p h d", h=4))
                self.rstd(rs4, ss4, 256, 128)
                k3 = kst.rearrange("p (h d) -> p h d", h=4)
                kb.tt("dve", k3, pk2.rearrange("p (h d) -> p h d", h=4),
                      rs4.unsqueeze(2).broadcast_to([128, 4, 256]), ALU.mult)
                kb.tt("dve", k3, k3, g_xk.unsqueeze(1).broadcast_to([128, 4, 256]), ALU.mult)
                kb.dma("sp", self.mkp[l, sq.b, mt * 128:(mt + 1) * 128, :], kst)
                kb.cp("act", kbf, kst)
                pv = self.ps_pair()
                for cb in range(2):
                    for kc in range(8):
                        kb.mm(pv[:, cb, :], lhsT=mT[:, kc, :], rhs=wv_p[cb][:, kc, :], start=(kc == 0), stop=(kc == 7))
                pv2 = pv.rearrange("p a b -> p (a b)")
                kb.cp("act", vst, pv2)
                kb.dma("sp", self.mvp[l, sq.b, mt * 128:(mt + 1) * 128, :], vst)
                kb.cp("dve", self.MVA[:, mt, :, 0:256], vst.rearrange("p (h d) -> p h d", h=4))
            else:
                kb.dma("sp", kst, self.cmk[l, mt * 128:(mt + 1) * 128, :])
                kb.cp("act", kbf, kst)
                kb.dma("sp", vst, self.cmv[l, mt * 128:(mt + 1) * 128, :])
                kb.cp("dve", self.MVA[:, mt, :, 0:256], vst.rearrange("p (h d) -> p h d", h=4))
            pt = self.ps_tr(8, 128)
            for kc in range(8):
                kb.tr(pt[:, kc, :], kbf[:, kc * 128:(kc + 1) * 128], self.identb)
            kb.cp("dve", self.mkT[:, :, mt * 128:(mt + 1) * 128], pt)

    def load_past(self, sq, l):
        kb = self.kb
        HB = self.HB_OFF
        st = self.V(HB + 4096, [512], F32)
        sb = self.V(HB + 6144, [512], BF16)
        for kt in range(8):
            kb.dma("sp", st, self.ck[l, kt * 128:(kt + 1) * 128, :])
            kb.cp("act", sb, st)
            pt = self.ps_tr(4, 128)
            for h in range(4):
                kb.tr(pt[:, h, :], sb[:, h * 128:(h + 1) * 128], self.identb)
            kb.cp("dve", self.KT[:, :, kt * 128:(kt + 1) * 128], pt)
            st2 = self.V(HB + 8192, [512], F32)
            kb.dma("sp", st2, self.cv[l, kt * 128:(kt + 1) * 128, :])
            kb.cp("dve", self.VA[:, kt, :, 0:128], st2.rearrange("p (h d) -> p h d", h=4))

    def stage_a_tile(self, sq, l, t, ti, slot, gfm, g_qk, g_gmn, gm_b, win, woB, QT):
        kb = self.kb
        P = sq.TP
        HB = self.HB_OFF
        isp = sq.kind == "p"
        o = HB
        xn = self.V(o, [D], BF16, P); o += 2048
        hT = self.V(o, [8, P], BF16); o += 2048
        sqr = self.V(o, [D], F32, P); o += 4096
        zq = self.V(o, [512], F32, P); o += 2048
        zk = self.V(o, [512], F32, P); o += 2048
        zv = self.V(o, [512], F32, P); o += 2048
        qkb = self.V(o, [D], BF16, P); o += 2048
        gvb = self.V(HB, [512], BF16, P)
        gob = self.V(HB + 1024, [512], BF16, P)
        goT = self.V(o, [4, P], BF16); o += 1024
        rope = self.V(o, [4, 64], F32, P); o += 1024
        assert o <= HB + 18432
        cs = self.csp[:, t, :] if isp else self.css[0:P, :]
        cosv = cs[:, 0:8]
        sinv = cs[:, 8:16]
        xrow = self.X[0:P, t, :]

        astep = float(os.environ.get("K_ASTEP", "99"))
        self.norm_transpose(sq, xrow, gfm[:, 0, :], hT, xn, sqr, P)
        if astep < 1:
            return

        def proj(cg):
            pz = self.ps_mm()[0:P, :]
            for kc in range(8):
                kb.mm(pz, lhsT=hT[:, kc, :], rhs=win[cg][:, kc, :], start=(kc == 0), stop=(kc == 7))
            return pz

        ss8 = self.stat[0:P, 16:24]
        rs8 = self.stat[0:P, 24:32]
        sq512 = sqr[:, 0:512]
        for which, dst in ((0, zq), (1, zk)):
            pz = proj(which)
            if astep < 2:
                kb.cp("act", dst, pz)
                continue
            kb.act(sq512, pz, AF.Square)
            kb.reduce_add(ss8, sq512.rearrange("p (g d) -> p g d", g=8))
            self.rstd(rs8, ss8, 64, P)
            d3 = dst.rearrange("p (g d) -> p g d", g=8)
            kb.tt("dve", d3, pz.rearrange("p (g d) -> p g d", g=8), rs8.unsqueeze(2).broadcast_to([P, 8, 64]), ALU.mult)
            kb.tt("dve", d3, d3, g_qk[0:P, which, :].unsqueeze(1).broadcast_to([P, 8, 64]), ALU.mult)
            if astep < 3:
                continue
            x1 = d3[:, :, 0:8]
            x2 = d3[:, :, 8:16]
            cb_ = cosv.unsqueeze(1).broadcast_to([P, 8, 8])
            sb_ = sinv.unsqueeze(1).broadcast_to([P, 8, 8])
            r3 = rope.rearrange("p a (g d) -> p a g d", g=8)
            kb.tt("dve", r3[:, 0], x1, cb_, ALU.mult)
            kb.tt("dve", r3[:, 1], x2, sb_, ALU.mult)
            kb.tt("dve", r3[:, 2], x2, cb_, ALU.mult)
            kb.tt("dve", r3[:, 3], x1, sb_, ALU.mult)
            kb.tt("dve", x1, r3[:, 0], r3[:, 1], ALU.subtract)
            kb.tt("dve", x2, r3[:, 2], r3[:, 3], ALU.add)
            kb.cp("act", qkb[:, which * 512:(which + 1) * 512], dst)
        if isp:
            kb.dma("sp", self.dkp[l, sq.b, t * 128:(t + 1) * 128, :], zk)
        else:
            kb.dma("sp", self.dks[l, :, :], zk)
        if astep < 3.2:
            return
        pt = self.ps_tr(8, P)
        for j in range(8):
            kb.tr(pt[:, j, :], qkb[:, j * 128:(j + 1) * 128], self.identb[0:P, 0:P])
        kb.cp("dve", QT[:, :, ti * 128:ti * 128 + P], pt[:, 0:4, :])
        kpos = t * 128 if isp else PAST
        kb.cp("act", self.KT[:, :, kpos:kpos + P], pt[:, 4:8, :])
        if astep < 3.5:
            return
        pz = proj(2)
        kb.cp("act", zv, pz)
        if isp:
            kb.dma("sp", self.dvp[l, sq.b, t * 128:(t + 1) * 128, :], zv)
        else:
            kb.dma("sp", self.dvs[l, :, :], zv)
        vt = t if isp else 8
        if astep < 4:
            return
        kb.cp("dve", self.VA[0:P, vt, :, 0:128], zv.rearrange("p (h d) -> p h d", h=4))
        if astep < 5:
            return
        ug = zq
        pu = proj(3)
        kb.act(ug, pu, AF.Gelu_apprx_tanh)
        pg = proj(4)
        gg = sqr[:, 0:512]
        kb.act(gg, pg, AF.Gelu_apprx_tanh)
        sq2 = sqr[:, 512:1024]
        kb.act(sq2, gg, AF.Square)
        ss4 = self.stat[0:P, 4:8]
        rs4 = self.stat[0:P, 8:12]
        kb.reduce_add(ss4, sq2.rearrange("p (g d) -> p g d", g=4))
        self.rstd(rs4, ss4, 128, P)
        gg3 = gg.rearrange("p (g d) -> p g d", g=4)
        kb.tt("dve", gg3, gg3, rs4.unsqueeze(2).broadcast_to([P, 4, 128]), ALU.mult)
        kb.tt("dve", gg3, gg3, g_gmn[0:P, :].unsqueeze(1).broadcast_to([P, 4, 128]), ALU.mult)
        if not isp:
            kb.dma("sp", self.gvs[l, :, :], gg)
        kb.cp("act", gvb, gg)
        if astep < 6:
            return
        pgate = self.ps_mm()[0:P, :]
        for g in range(4):
            kb.mm(pgate[:, g * 128:(g + 1) * 128], lhsT=slot["wsT"][0:P, g, 0:P], rhs=gvb[:, g * 128:(g + 1) * 128])
        go3 = sq2.rearrange("p (g d) -> p g d", g=4)
        kb.tt("dve", go3, pgate.rearrange("p (g d) -> p g d", g=4),
              gm_b[0:P, :].unsqueeze(2).broadcast_to([P, 4, 128]), ALU.add)
        kb.tt("dve", gob, sq2, ug, ALU.mult)
        if astep < 7:
            return
        pt2 = self.ps_tr(4, P)
        for j in range(4):
            kb.tr(pt2[:, j, :], gob[:, j * 128:(j + 1) * 128], self.identb[0:P, 0:P])
        kb.cp("act", goT, pt2)
        py = self.ps_pair()
        for cb in range(2):
            for kc in range(4):
                kb.mm(py[0:P, cb, :], lhsT=goT[:, kc, :], rhs=woB[:, kc, cb * 512:(cb + 1) * 512],
                      start=(kc == 0), stop=(kc == 3))
        kb.tt("dve", xrow, py[0:P].rearrange("p a b -> p (a b)"), xrow, ALU.add)

    def stage_b_block(self, sq, l, blk, slot, g_sub, lam, woA, QT):
        kb = self.kb
        HB = self.HB_OFF
        isp = sq.kind == "p"
        P = sq.TP
        QB = sq.QB
        o = HB + 18432 + 4096
        eTs = [self.V(o + i * 2048, [2, 512], BF16) for i in range(2)]; o += 4096
        otmp = self.V(o, [128], F32); o += 512
        o += 512
        OF = self.V(HB, [4, 512], F32)
        accS = self.V(HB + 10240, [3, 480], F32)
        osq = self.V(HB + 8192, [512], F32)
        OB = self.V(o, [4, 512], BF16); o += 4096
        oT = self.V(o, [4, 128], BF16); o += 1024
        assert o <= HB + 32768, o
        nqs = sq.TPB
        if isp:
            nkt = 4 * blk + 4
        else:
            nkt = 9
        rz = self.stat[0:P, 32:48]
        ssq = self.stat[0:P, 48:49]
        rso = self.stat[0:P, 49:50]
        nl = self.stat[0:P, 50:51]
        ei = 0
        for h in range(4):
            def acc(qs, m):
                r = qs * 2 + m
                return self.ps[0:P, 4 + r // 3, (r % 3) * 160:(r % 3) * 160 + 129]
            for kt in range(nkt):
                KP = 128 if (isp or kt < 8) else DS
                j = kt - 4 * blk if isp else -1
                q0 = max(0, j) * 128
                kpos = kt * 128
                psS = self.ps_pair()
                for m in range(2):
                    kb.mm(psS[0:KP, m, q0:QB], lhsT=self.KT[m * 64:(m + 1) * 64, h, kpos:kpos + KP],
                          rhs=QT[m * 64:(m + 1) * 64, h, q0:QB])
                eT = eTs[ei % 2]
                ei += 1
                kb.act(eT[0:KP, :, q0:QB], psS[0:KP, :, q0:QB], AF.Exp, scale=0.125)
                if j >= 0:
                    kb.memset("dve", eT[64:128, :, q0:q0 + 64], 0.0)
                for qs in range(max(0, j), nqs):
                    last = (kt == 4 * blk + qs) if isp else (kt == nkt - 1)
                    for m in range(2):
                        kb.mm(acc(qs, m), lhsT=eT[0:KP, m, qs * 128:qs * 128 + P], rhs=self.VA[0:KP, kt, h, 0:129],
                              start=(kt == 0 and (qs * 2 + m) % 3 == 0), stop=last)
            nreg = 2 * nqs
            for bk in range((nreg + 2) // 3):
                w_ = min(3, nreg - 3 * bk) * 160
                kb.cp("act", accS[0:P, bk, 0:w_], self.ps[0:P, 4 + bk, 0:w_])

            def accs(qs, m):
                r = qs * 2 + m
                return accS[0:P, r // 3, (r % 3) * 160:(r % 3) * 160 + 129]
            for qs in range(nqs):
                a0 = accs(qs, 0)
                a1 = accs(qs, 1)
                kb.recip(rz[:, 0:1], a0[:, 128:129])
                kb.recip(rz[:, 1:2], a1[:, 128:129])
                kb.tt("dve", nl, rz[:, 1:2], lam[0:P, 3:4], ALU.mult)
                ot = otmp[0:P, :]
                kb.ts("dve", ot, a0[:, 0:128], rz[:, 0:1], ALU.mult)
                kb.stt("dve", OF[0:P, qs, h * 128:(h + 1) * 128], a1[:, 0:128], nl, ot, ALU.mult, ALU.add)
        ss4 = self.stat[0:P, 4:8]
        rs4 = self.stat[0:P, 8:12]
        for qs in range(nqs):
            t = blk * sq.TPB + qs
            of = OF[0:P, qs, :]
            kb.act(osq[0:P, :], of, AF.Square)
            kb.reduce_add(ss4, osq[0:P, :].rearrange("p (h d) -> p h d", h=4))
            self.rstd(rs4, ss4, 128, P)
            of3 = of.rearrange("p (h d) -> p h d", h=4)
            kb.tt("dve", of3, of3, rs4.unsqueeze(2).broadcast_to([P, 4, 128]), ALU.mult)
            kb.tt("dve", OB[0:P, qs, :].rearrange("p (h d) -> p h d", h=4), of3,
                  g_sub[0:P, :].unsqueeze(1).broadcast_to([P, 4, 128]), ALU.mult)
            pt = self.ps_tr(4, P)
            for j in range(4):
                kb.tr(pt[:, j, :], OB[0:P, qs, j * 128:(j + 1) * 128], self.identb[0:P, 0:P])
            kb.cp("act", oT[:, :, 0:P], pt)
            py = self.ps_pair()
            for cb in range(2):
                for kc in range(4):
                    kb.mm(py[0:P, cb, :], lhsT=oT[:, kc, 0:P], rhs=woA[:, kc, cb * 512:(cb + 1) * 512],
                          start=(kc == 0), stop=(kc == 3))
            xrow = self.X[0:P, t, :]
            kb.tt("dve", xrow, py[0:P].rearrange("p a b -> p (a b)"), xrow, ALU.add)

    def stage_c(self, sq, l, gfm, g_xq, wq_p, wo_p):
        kb = self.kb
        P = sq.TP
        NT = sq.NT
        o = self.KT_OFF
        xn1 = self.V(o, [D], BF16, P); o += 2048
        sq1 = self.V(o, [D], F32, P); o += 4096
        h2T = self.V(o, [8, P], BF16); o += 2048
        qcn = self.V(o, [D], BF16, P); o += 2048
        qcTs = [self.V(o + i * 2048, [8, P], BF16) for i in range(2)]; o += 4096
        eT = self.V(o, [4, 2, P], BF16); o += 2048
        ocb = self.V(o, [D], BF16, P); o += 2048
        ocTs = [self.V(o + i * 2048, [8, P], BF16) for i in range(2)]; o += 4096
        xn3 = self.V(o, [D], BF16, P); o += 2048
        sq3 = self.V(o, [D], F32, P); o += 4096
        assert o <= self.KT_OFF + 33024
        ss4 = self.stat[0:P, 4:8]
        rs4 = self.stat[0:P, 8:12]
        rz4 = self.stat[0:P, 68:72]
        HBv = self.V(self.HB_OFF, [8, sq.S], BF16)

        def c1(t):
            xrow = self.X[0:P, t, :]
            qcT = qcTs[t % 2]
            self.norm_transpose(sq, xrow, gfm[:, 1, :], h2T, xn1, sq1, P)
            pq = self.ps_pair()
            for cb in range(2):
                for kc in range(8):
                    kb.mm(pq[0:P, cb, :], lhsT=h2T[:, kc, :], rhs=wq_p[cb][:, kc, :], start=(kc == 0), stop=(kc == 7))
            pq2 = pq[0:P].rearrange("p a b -> p (a b)")
            kb.act(sq1, pq2, AF.Square)
            kb.reduce_add(ss4, sq1.rearrange("p (h d) -> p h d", h=4))
            self.rstd(rs4, ss4, 256, P)
            s3 = sq1.rearrange("p (h d) -> p h d", h=4)
            kb.tt("dve", s3, pq2.rearrange("p (h d) -> p h d", h=4), rs4.unsqueeze(2).broadcast_to([P, 4, 256]), ALU.mult)
            kb.tt("dve", qcn.rearrange("p (h d) -> p h d", h=4), s3, g_xq[0:P, :].unsqueeze(1).broadcast_to([P, 4, 256]),
                  ALU.mult)
            pt = self.ps_tr(8, P)
            for j in range(8):
                kb.tr(pt[:, j, :], qcn[:, j * 128:(j + 1) * 128], self.identb[0:P, 0:P])
            kb.cp("act", qcT, pt)

        def c2(t):
            qcT = qcTs[t % 2]
            ocT = ocTs[t % 2]
            psS = self.ps_pair()
            for h in range(4):
                for mt in range(2):
                    c0 = (h * 2 + mt) * 128
                    dst = psS[:, c0 // 512, (c0 % 512):(c0 % 512) + P]
                    for dc in range(2):
                        kb.mm(dst, lhsT=self.mkT[:, h * 2 + dc, mt * 128:(mt + 1) * 128], rhs=qcT[:, h * 2 + dc, :],
                              start=(dc == 0), stop=(dc == 1))
            for half in range(2):
                kb.act(eT[:, half * 2:half * 2 + 2, :, :],
                       psS[:, half, :].rearrange("p (h m q) -> p h m q", h=2, m=2)[:, :, :, 0:P], AF.Exp, scale=1.0 / 16.0)
            for h in range(4):
                po = self.ps[0:P, 4 + (h % 3), 0:257]
                for mt in range(2):
                    kb.mm(po, lhsT=eT[:, h, mt, :], rhs=self.MVA[:, mt, h, 0:257], start=(mt == 0), stop=(mt == 1))
                kb.recip(rz4[:, h:h + 1], po[:, 256:257])
                kb.ts("dve", ocb[:, h * 256:(h + 1) * 256], po[:, 0:256], rz4[:, h:h + 1], ALU.mult)
            pt2 = self.ps_tr(8, P)
            for j in range(8):
                kb.tr(pt2[:, j, :], ocb[:, j * 128:(j + 1) * 128], self.identb[0:P, 0:P])
            kb.cp("act", ocT, pt2)

        def c3(t):
            xrow = self.X[0:P, t, :]
            ocT = ocTs[t % 2]
            py = self.ps_pair()
            for cb in range(2):
                for kc in range(8):
                    kb.mm(py[0:P, cb, :], lhsT=ocT[:, kc, :], rhs=wo_p[cb][:, kc, :], start=(kc == 0), stop=(kc == 7))
            kb.tt("dve", xrow, py[0:P].rearrange("p a b -> p (a b)"), xrow, ALU.add)
            self.norm_transpose(sq, xrow, gfm[:, 2, :], HBv[:, :, t * 128:t * 128 + P], xn3, sq3, P, sc=64)

        for i in range(NT + 2):
            if i < NT:
                c1(i)
            if 0 <= i - 1 < NT:
                c2(i - 1)
            if 0 <= i - 2 < NT:
                c3(i - 2)

    def stage_ffn(self, sq, l, conv_w, conv_b, fpan):
        kb = self.kb
        isp = sq.kind == "p"
        P = sq.TP
        NTOK = sq.QB
        HBv = self.V(self.HB_OFF, [8, sq.S], BF16)
        o = self.KT_OFF
        gss = [self.V(o + i * 2064, [516], F32) for i in range(2)]; o += 4128
        ccs = [self.V(o + i * 2048, [512], F32) for i in range(2)]; o += 4096
        scs = [self.V(o + i * 2048, [512], F32) for i in range(2)]; o += 4096
        aTs = [self.V(o + i * 4096, [4, 512], BF16) for i in range(2)]; o += 8192
        carry = self.V(o, [NFC, 2], F32); o += 176
        cst = self.V(o, [512], F32, 2); o += 2048
        assert o <= self.KT_OFF + 33024
        if isp:
            kb.memset("dve", carry, 0.0)
        else:
            kb.dma("sp", carry, self.convst_d[:, l, :, :])
        ri = 0
        ai = 0
        pending = None
        for g in range(6):
            gate, up, down, nf = fpan[g]
            for blk in range(sq.NBLK):
                aT = aTs[ai % 2]
                ai += 1
                for fl in range(nf):
                    fc = g * 4 + fl
                    pg = self.ps_mm()
                    for kc in range(8):
                        kb.mm(pg[:, 0:NTOK], lhsT=gate[:, kc, fl * 128:(fl + 1) * 128],
                              rhs=HBv[:, kc, blk * 512:blk * 512 + NTOK], start=(kc == 0), stop=(kc == 7))
                    pu = self.ps_mm()
                    for kc in range(8):
                        kb.mm(pu[:, 0:NTOK], lhsT=up[:, kc, fl * 128:(fl + 1) * 128],
                              rhs=HBv[:, kc, blk * 512:blk * 512 + NTOK], start=(kc == 0), stop=(kc == 7))
                    gs = gss[ri % 2]
                    cc = ccs[ri % 2]
                    sc = scs[ri % 2]
                    ri += 1
                    kb.cp("act", gs[:, 2:2 + NTOK], pg[:, 0:NTOK])
                    kb.cp("dve", gs[:, 0:2], carry[:, fc, :])
                    kb.cp("dve", carry[:, fc, :], gs[:, NTOK:NTOK + 2])
                    kb.act(cc[:, 0:NTOK], pg[:, 0:NTOK], AF.Identity, bias=conv_b[:, fc:fc + 1], scale=conv_w[:, fc, 2:3])
                    kb.stt("dve", cc[:, 0:NTOK], gs[:, 1:1 + NTOK], conv_w[:, fc, 1:2], cc[:, 0:NTOK], ALU.mult, ALU.add)
                    kb.stt("dve", cc[:, 0:NTOK], gs[:, 0:NTOK], conv_w[:, fc, 0:1], cc[:, 0:NTOK], ALU.mult, ALU.add)
                    kb.act(sc[:, 0:NTOK], cc[:, 0:NTOK], AF.Silu)
                    kb.tt("dve", aT[:, fl, 0:NTOK], sc[:, 0:NTOK], pu[:, 0:NTOK], ALU.mult)
                def do_down(blk=blk, aT=aT, down=down, nf=nf):
                    for ti in range(sq.TPB):
                        t = blk * sq.TPB + ti
                        py = self.ps[0:P, 4 + 2 * (t % 2):6 + 2 * (t % 2), :]
                        for cb in range(2):
                            for fl in range(nf):
                                kb.mm(py[:, cb, :], lhsT=aT[:, fl, ti * 128:ti * 128 + P],
                                      rhs=down[:, fl, cb * 512:(cb + 1) * 512], start=(fl == 0), stop=(fl == nf - 1))
                        xrow = self.X[0:P, t, :]
                        kb.tt("dve", xrow, py.rearrange("p a b -> p (a b)"), xrow, ALU.add)
                if pending is not None:
                    pending()
                pending = do_down
            if pending is not None:
                pending()
                pending = None
            pc = self.ps[0:2, 3, :]
            for fl in range(nf):
                kb.tr(pc[:, fl * 128:(fl + 1) * 128], carry[:, g * 4 + fl, :], self.identf)
            kb.cp("act", cst[:, 0:nf * 128], pc[:, 0:nf * 128])
            dst = self.fcp[l, sq.b, :, g * 512:g * 512 + nf * 128] if isp else self.fcs[l, :, g * 512:g * 512 + nf * 128]
            kb.dma("sp", dst, cst[:, 0:nf * 128])
            if g + 2 < 6:
                fpan[g + 2] = self.load_ffn_group(l, g + 2, 4 if (g % 2 == 0) else 0)


_PROG = None


def _get_prog():
    global _PROG
    if _PROG is None:
        _PROG = Prog()
    return _PROG


def _host_consts(inp):
    f32 = np.float32
    rep = np.zeros((L, 128, NREP), f32)
    fm = np.zeros((L, 128, NFM), f32)
    for l in range(L):
        v = np.concatenate([inp["da_q_norm_g"][l], inp["da_k_norm_g"][l], inp["da_subln_g"][l], inp["gm_norm_g"][l],
                            inp["xq_norm_g"][l], inp["xk_norm_g"][l], inp["lambda_q1"][l], inp["lambda_k1"][l],
                            inp["lambda_q2"][l], inp["lambda_k2"][l]]).astype(f32)
        rep[l] = np.broadcast_to(v[None, :], (128, NREP))
        for i, nm in enumerate(["norm_mix_g", "norm_x_g", "norm_ffn_g", "norm_mem_g"]):
            fm[l, :, i * 8:(i + 1) * 8] = inp[nm][l].reshape(8, 128).T
        fm[l, :, 32:36] = inp["gm_b"][l].T
        cw = inp["conv_w"][l].reshape(3, NFC, 128)
        fm[l, :, 36:102] = cw.transpose(2, 1, 0).reshape(128, NFC * 3)
        fm[l, :, 102:124] = inp["conv_b"][l].reshape(NFC, 128).T
    wst = np.ascontiguousarray(inp["gm_w_s"].transpose(0, 3, 1, 2)).reshape(L, 128, 512).astype(f32)
    ident = np.eye(128, dtype=f32)
    tril = np.triu(np.ones((128, 128), f32))
    half = 8
    inv = (np.float32(500000.0) ** (-np.arange(half, dtype=f32) / np.float32(half))).astype(f32)

    def cs(pos):
        ang = pos.astype(f32)[:, None] * inv[None, :]
        return np.concatenate([np.cos(ang), np.sin(ang)], axis=1).astype(f32)

    csp = cs(np.arange(S)).reshape(16, 128, 16).transpose(1, 0, 2)
    css = cs(PAST + np.arange(DS))
    return dict(rep=rep, fm=fm, wst=wst, ident=ident, tril=tril, csp=np.ascontiguousarray(csp), css=css)


def _in_maps(inp, cores):
    hc = _host_consts(inp)
    in_maps = []
    for c in cores:
        m = dict(hc)
        m["xp"] = np.ascontiguousarray(inp["x_prompt"][c * NB:(c + 1) * NB])
        m["xs"] = np.ascontiguousarray(inp["x_sample"][c])
        m["ck"] = np.ascontiguousarray(inp["cache_da_k"][:, c].reshape(L, PAST, 512))
        m["cv"] = np.ascontiguousarray(inp["cache_da_v"][:, c].reshape(L, PAST, 512))
        m["cmk"] = np.ascontiguousarray(inp["cache_mem_k"][:, c].reshape(L, MEM, D))
        m["cmv"] = np.ascontiguousarray(inp["cache_mem_v"][:, c].reshape(L, MEM, D))
        m["memp"] = np.ascontiguousarray(inp["mem_prompt"][c * NB:(c + 1) * NB])
        st = inp["state_ffn_conv"][:, c]
        m["convst"] = np.ascontiguousarray(st.reshape(L, 2, NFC, 128).transpose(3, 0, 2, 1))
        m["w_in"] = inp["w_in"]; m["w_out"] = inp["w_out"]
        m["wq"] = inp["wq_c"]; m["wk"] = inp["wk_c"]; m["wv"] = inp["wv_c"]; m["wo"] = inp["wo_c"]
        m["w_up"] = inp["w_up"]; m["w_down"] = inp["w_down"]
        in_maps.append(m)
    return in_maps


def kernel(**inp):
    inp = {k: np.asarray(v) for k, v in inp.items()}
    prog = _get_prog()
    n = 8
    in_maps = _in_maps(inp, range(n))
    res = run_bass_kernel_spmd(prog.nc, in_maps, core_ids=list(range(n))).results
    cat = lambda k, ax: np.concatenate([r[k] for r in res], axis=ax)
    y_p = cat("yp", 0)
    y_s = np.stack([r["ys"] for r in res], 0)
    dk_p = cat("dkp", 1).reshape(L, 32, S, 4, 2, 64)
    dv_p = cat("dvp", 1).reshape(L, 32, S, 4, 128)
    mk_p = cat("mkp", 1).reshape(L, 32, MEM, 4, 256)
    mv_p = cat("mvp", 1).reshape(L, 32, MEM, 4, 256)
    fc_p = cat("fcp", 1)
    dk_s = np.stack([r["dks"] for r in res], 1).reshape(L, 8, DS, 4, 2, 64)
    dv_s = np.stack([r["dvs"] for r in res], 1).reshape(L, 8, DS, 4, 128)
    gv_s = np.stack([r["gvs"] for r in res], 1).reshape(L, 8, DS, 4, 128)
    fc_s = np.stack([r["fcs"] for r in res], 1)
    return (y_p, y_s, dk_p, dv_p, mk_p, mv_p, fc_p, dk_s, dv_s, gv_s, fc_s)
```

```python
import os
import numpy as np
from contextlib import ExitStack
import concourse.bass as bass
import concourse.mybir as mybir
from concourse.bass_utils import run_bass_kernel_spmd

F32 = mybir.dt.float32
BF16 = mybir.dt.bfloat16
AF = mybir.ActivationFunctionType
ALU = mybir.AluOpType
AX = mybir.AxisListType
CELL = 256
SAME_ENGINE_SYNC = bool(int(os.environ.get("K_SES", "0")))
SMALL_T = int(os.environ.get("K_SMALL", "256"))
N_DMA_SEMS = 12

D = 1024
L = 4
NB = 4
S = 2048
DS = 32
PAST = 1024
MEM = 256
DFF = 2816
NFC = 22
EPS = 1e-6
NREP = 1152
NFM = 124


def _esize(dt):
    return 2 if dt == BF16 else 4


class KB:
    def __init__(self):
        self.nc = bass.Bass("TRN2", target_bir_lowering=False)
        nc = self.nc
        self.es = ExitStack()
        self.eng = {"pe": nc.tensor, "act": nc.scalar, "dve": nc.vector, "pool": nc.gpsimd, "sp": nc.sync}
        self.sems = {}
        self.sem_id = {}
        self.cnt = {}
        self._nsem = 0
        for e in ["pe", "act", "dve", "pool"]:
            self.sems[e] = self._newsem("c_" + e)
            self.cnt[e] = 0
        self.dq = {}
        for q in ["sp", "pool"]:
            pool = [self._newsem(f"d_{q}{i}") for i in range(N_DMA_SEMS)]
            self.dq[q] = {"sems": pool, "vals": [0] * N_DMA_SEMS, "next": 0}
        self.seen = {e: {} for e in ["pe", "act", "dve", "pool", "sp"]}
        self.cells = {}
        self.n_inst = 0
        self.n_wait = 0

    def _newsem(self, name):
        h = self.es.enter_context(self.nc.semaphore(name))
        k = self._nsem
        self._nsem += 1
        self.sem_id[k] = h
        return k

    def _cells(self, ap):
        sp = str(ap.space).upper()
        if "SB" not in sp and "PSUM" not in sp:
            return None
        es = _esize(ap.dtype)
        a = ap.ap
        pstep = a[0][0]
        off = ap.offset
        base = (off % pstep) * es if pstep > 0 else off * es
        region = ap.tensor.name
        cell = CELL if "SB" in sp else 2048
        dims = [(s * es, c) for (s, c) in a[1:]]
        if not dims:
            dims = [(es, 1)]
        ls, lc = dims[-1]
        run = (lc - 1) * ls + es
        starts = [base]
        for (s, c) in dims[:-1]:
            if s == 0 or c == 1:
                continue
            starts = [st + i * s for st in starts for i in range(c)]
        out = set()
        for st in starts:
            for c in range(st // cell, (st + run - 1) // cell + 1):
                out.add((region, c))
        fsz = 1
        for (s_, c_) in a[1:]:
            if s_ != 0:
                fsz *= c_
        self._last_fsz = fsz
        return out

    def _deps(self, reads, writes, own=None):
        deps = {}
        rc = set()
        wc = set()
        msize = 1 << 30
        for ap in reads:
            c = self._cells(ap)
            if c:
                msize = min(msize, self._last_fsz)
                if "PSUM" in str(ap.space).upper():
                    wc |= c
                else:
                    rc |= c
        for ap in writes:
            c = self._cells(ap)
            if c:
                msize = min(msize, self._last_fsz)
                wc |= c
        self._msize = msize
        cells = self.cells
        small = msize <= SMALL_T

        def add(tok):
            k, v, sz = tok
            if k == own and not SAME_ENGINE_SYNC and not (small or sz <= SMALL_T):
                return
            if deps.get(k, 0) < v:
                deps[k] = v
        for c in rc:
            st = cells.get(c)
            if st is not None and st[0] is not None:
                add(st[0])
        for c in wc:
            st = cells.get(c)
            if st is not None:
                if st[0] is not None:
                    add(st[0])
                for k, (v, sz) in st[1].items():
                    add((k, v, sz))
        return deps, rc, wc

    def _commit(self, tok, rc, wc):
        k, v = tok
        sz = self._msize
        cells = self.cells
        for c in rc:
            if c in wc:
                continue
            st = cells.get(c)
            if st is None:
                st = [None, {}]
                cells[c] = st
            old = st[1].get(k)
            if old is None or old[0] < v:
                st[1][k] = (v, sz)
        t3 = (k, v, sz)
        for c in wc:
            cells[c] = [t3, {}]

    def _waits(self, ename, deps):
        e = self.eng[ename]
        seen = self.seen[ename]
        own = self.sems.get(ename)
        for k, v in deps.items():
            if k == own and ename == "pe":
                continue
            if seen.get(k, 0) >= v:
                continue
            e.wait_ge(self.sem_id[k], v)
            seen[k] = v
            self.n_wait += 1
            if os.environ.get("K_TRACE"):
                print("   WAIT", ename, "sem", k, ">=", v)

    def I(self, ename, fn, reads, writes):
        deps, rc, wc = self._deps(reads, writes, self.sems.get(ename))
        self._waits(ename, deps)
        ins = fn()
        self.cnt[ename] += 1
        k = self.sems[ename]
        if os.environ.get("K_TRACE"):
            print("INS", ename, self.cnt[ename], str(ins)[:150])
        ins.then_inc(self.sem_id[k], 1)
        self._commit((k, self.cnt[ename]), rc, wc)
        self.n_inst += 1
        return ins

    def dma(self, q, out, in_, extra_wait=None):
        deps, rc, wc = self._deps([in_], [out])
        if extra_wait:
            for k, v in extra_wait:
                if deps.get(k, 0) < v:
                    deps[k] = v
        d = self.dq[q]
        i = d["next"]
        d["next"] = (i + 1) % N_DMA_SEMS
        k = d["sems"][i]
        if d["vals"][i] > 0:
            deps[k] = max(deps.get(k, 0), d["vals"][i])
        self._waits(q, deps)
        ins = self.eng[q].dma_start(out=out, in_=in_)
        d["vals"][i] += 16
        ins.then_inc(self.sem_id[k], 16)
        tok = (k, d["vals"][i])
        if os.environ.get("K_TRACE"):
            print("DMA", q, tok, str(ins)[:150])
        self._commit(tok, rc, wc)
        self.n_inst += 1
        return tok

    def finish(self):
        last = {}
        for q, d in self.dq.items():
            for k, v in zip(d["sems"], d["vals"]):
                if v > 0:
                    last[k] = v
        for e in ["pe", "act", "dve", "pool"]:
            if self.cnt[e] > 0:
                last[self.sems[e]] = self.cnt[e]
        self._waits("sp", last)

    def mm(self, out, lhsT, rhs, start=True, stop=True):
        return self.I("pe", lambda: self.nc.tensor.matmul(out, lhsT=lhsT, rhs=rhs, start=start, stop=stop),
                      [lhsT, rhs], [out])

    def tr(self, out, in_, ident):
        return self.I("pe", lambda: self.nc.tensor.transpose(out, in_, ident), [in_, ident], [out])

    def act(self, out, in_, func, bias=None, scale=1.0, accum_out=None):
        reads = [in_]
        kw = {}
        if bias is not None:
            kw["bias"] = bias
            if not isinstance(bias, (int, float)):
                reads.append(bias)
        if not isinstance(scale, (int, float)):
            reads.append(scale)
        writes = [out]
        if accum_out is not None:
            kw["accum_out"] = accum_out
            writes.append(accum_out)
        return self.I("act", lambda: self.nc.scalar.activation(out=out, in_=in_, func=func, scale=scale, **kw),
                      reads, writes)

    def tt(self, e, out, in0, in1, op):
        return self.I(e, lambda: self.eng[e].tensor_tensor(out=out, in0=in0, in1=in1, op=op), [in0, in1], [out])

    def ts(self, e, out, in0, s1, op0, s2=None, op1=None):
        reads = [in0]
        if not isinstance(s1, (int, float)):
            reads.append(s1)
        if s2 is not None and not isinstance(s2, (int, float)):
            reads.append(s2)
        kw = {}
        if op1 is not None:
            kw["op1"] = op1
        return self.I(e, lambda: self.eng[e].tensor_scalar(out=out, in0=in0, scalar1=s1, scalar2=s2, op0=op0, **kw),
                      reads, [out])

    def stt(self, e, out, in0, scalar, in1, op0, op1):
        reads = [in0, in1]
        if not isinstance(scalar, (int, float)):
            reads.append(scalar)
        return self.I(e, lambda: self.eng[e].scalar_tensor_tensor(out=out, in0=in0, scalar=scalar, in1=in1,
                                                                  op0=op0, op1=op1), reads, [out])

    def cp(self, e, out, in_):
        if e == "act":
            return self.I("act", lambda: self.nc.scalar.copy(out=out, in_=in_), [in_], [out])
        return self.I(e, lambda: self.eng[e].tensor_copy(out=out, in_=in_), [in_], [out])

    def memset(self, e, out, val):
        return self.I(e, lambda: self.eng[e].memset(out, val), [], [out])

    def recip(self, out, in_):
        return self.I("dve", lambda: self.nc.vector.reciprocal(out=out, in_=in_), [in_], [out])

    def reduce_add(self, out, in_):
        return self.I("dve", lambda: self.nc.vector.tensor_reduce(out=out, in_=in_, axis=AX.X, op=ALU.add),
                      [in_], [out])


class Seq:
    def __init__(self, kind, b):
        self.kind = kind
        self.b = b
        if kind == "p":
            self.S, self.TP, self.NT, self.QB, self.NBLK, self.TPB = S, 128, 16, 512, 4, 4
            self.SK, self.NKT = S, 16
        else:
            self.S, self.TP, self.NT, self.QB, self.NBLK, self.TPB = DS, DS, 1, DS, 1, 1
            self.SK, self.NKT = PAST + DS, 9


class Prog:
    def __init__(self):
        self.kb = KB()
        kb = self.kb
        nc = kb.nc
        self.nc = nc

        def din(name, shape, dt=F32):
            return nc.dram_tensor(name, list(shape), dt, kind="ExternalInput").ap()

        def dout(name, shape):
            return nc.dram_tensor(name, list(shape), F32, kind="ExternalOutput").ap()

        def dint(name, shape):
            return nc.dram_tensor(name, list(shape), BF16, kind="Internal").ap()

        self.xp = din("xp", [NB, S, D])
        self.xs = din("xs", [DS, D])
        self.ck = din("ck", [L, PAST, 512])
        self.cv = din("cv", [L, PAST, 512])
        self.cmk = din("cmk", [L, MEM, D])
        self.cmv = din("cmv", [L, MEM, D])
        self.memp = din("memp", [NB, MEM, D])
        self.wf = {
            "w_in": din("w_in", [L, D, 2560]), "w_out": din("w_out", [L, D, D]),
            "wq": din("wq", [L, D, D]), "wk": din("wk", [L, D, D]), "wv": din("wv", [L, D, D]),
            "wo": din("wo", [L, D, D]), "w_up": din("w_up", [L, D, 2 * DFF]), "w_down": din("w_down", [L, DFF, D]),
        }
        self.wb = {k: dint("b_" + k, v.shape) for k, v in self.wf.items()}
        self.rep = din("rep", [L, 128, NREP])
        self.fm = din("fm", [L, 128, NFM])
        self.wst = din("wst", [L, 128, 512])
        self.ident_d = din("ident", [128, 128])
        self.tril_d = din("tril", [128, 128])
        self.csp_d = din("csp", [128, 16, 16])
        self.css_d = din("css", [DS, 16])
        self.convst_d = din("convst", [128, L, NFC, 2])
        self.yp = dout("yp", [NB, S, D])
        self.ys = dout("ys", [DS, D])
        self.dkp = dout("dkp", [L, NB, S, 512])
        self.dvp = dout("dvp", [L, NB, S, 512])
        self.mkp = dout("mkp", [L, NB, MEM, D])
        self.mvp = dout("mvp", [L, NB, MEM, D])
        self.fcp = dout("fcp", [L, NB, 2, DFF])
        self.dks = dout("dks", [L, DS, 512])
        self.dvs = dout("dvs", [L, DS, 512])
        self.gvs = dout("gvs", [L, DS, 512])
        self.fcs = dout("fcs", [L, 2, DFF])

        ARENA_BYTES = 212800
        self.arena = nc.alloc_sbuf_tensor("arena", [128, ARENA_BYTES // 2], BF16)
        self.ps = nc.alloc_psum_tensor("ps", [128, 8, 512], F32)
        self.X_OFF = 0
        self.HB_OFF = 65536
        self.KT_OFF = 98304
        self.VA_OFF = 114688
        self.W_OFF = 131328
        self.C_OFF = 188672
        self.cast_tok = {}
        self.ps_rr = {"mm": 0, "pair": 0, "acc": 0}
        self.lctr = 0
        self.build()

    def V(self, off, shape, dt, P=128):
        n = int(np.prod(shape)) * _esize(dt)
        assert off % 4 == 0
        v = self.arena[0:P, off // 2:(off + n) // 2]
        if dt != BF16:
            v = v.bitcast(dt)
        if len(shape) == 2:
            v = v.rearrange("p (a b) -> p a b", a=shape[0])
        elif len(shape) == 3:
            v = v.rearrange("p (a b c) -> p a b c", a=shape[0], b=shape[1])
        return v

    def wslot(self, i, shape):
        return self.V(self.W_OFF + i * 8192, shape, BF16)

    def ps_mm(self):
        i = self.ps_rr["mm"]
        self.ps_rr["mm"] = (i + 1) % 4
        return self.ps[:, i, :]

    def ps_pair(self):
        i = self.ps_rr["pair"]
        self.ps_rr["pair"] = (i + 1) % 2
        return self.ps[:, 2 * i:2 * i + 2, :]

    def ps_tr(self, n, w):
        return self.ps[:, 7, 0:(n * w) // 2].bitcast(BF16).rearrange("p (a b) -> p a b", a=n)

    def rstd(self, out, ss, dim, P):
        kb = self.kb
        kb.act(out, ss, AF.Ln, scale=1.0 / dim, bias=EPS)
        kb.act(out, out, AF.Exp, scale=-0.5)

    def build(self):
        kb = self.kb
        c = self.C_OFF
        self.identf = self.V(c, [128], F32); c += 512
        self.identb = self.V(c, [128], BF16); c += 256
        self.tril = self.V(c, [128], F32); c += 512
        self.csp = self.V(c, [16, 16], F32); c += 1024
        self.css = self.V(c, [16], F32); c += 64
        self.lslot = []
        for i in range(2):
            d = {}
            d["rep"] = self.V(c, [NREP], F32); c += NREP * 4
            d["fm"] = self.V(c, [NFM], F32); c += NFM * 4
            d["wsT"] = self.V(c, [4, 128], BF16); c += 1024
            d["lam"] = self.V(c, [16], F32); c += 64
            self.lslot.append(d)
        self.mkT = self.V(c, [8, MEM], BF16); c += 4096
        self.MVA = self.V(c, [2, 4, 258], BF16); c += 4128
        self.stat = self.V(c, [128], F32); c += 512
        assert c <= 212800, c

        kb.dma("sp", self.identf, self.ident_d[:, :])
        kb.dma("sp", self.tril, self.tril_d[:, :])
        kb.dma("sp", self.csp, self.csp_d[:, :, :])
        kb.dma("sp", self.css[0:DS], self.css_d[:, :])
        kb.cp("dve", self.identb, self.identf)
        kb.memset("dve", self.MVA[:, :, :, 256:258], 1.0)

        for l in range(L):
            for name in ["wk", "wv", "w_in", "w_out", "wq", "wo", "w_up", "w_down"]:
                src = self.wf[name]
                dst = self.wb[name]
                rows = src.shape[1]
                toks = []
                for r0 in range(0, rows, 128):
                    toks.append(kb.dma("pool", dst[l, r0:r0 + 128, :], src[l, r0:r0 + 128, :]))
                self.cast_tok[(name, l)] = toks

        import os
        self.dbg = int(os.environ.get("K_DBG", "9"))
        self.dbg_nl = int(os.environ.get("K_NL", str(L)))
        seqs = [Seq("p", b) for b in range(NB)] + [Seq("s", 0)]
        sel = os.environ.get("K_SEQS")
        if sel is not None:
            seqs = [seqs[int(i)] for i in sel.split(",")]
        if os.environ.get("K_NOCAST"):
            pass
        self.va_ones_done = None
        for sq in seqs:
            self.run_seq(sq)
        kb.finish()

    def load_panel(self, slot, name, l, rows, cols, shape):
        src = self.wb[name][l, rows[0]:rows[1], cols[0]:cols[1]].rearrange("(k p) n -> p k n", p=128)
        dst = self.wslot(slot, shape)
        self.kb.dma("sp", dst, src, extra_wait=self.cast_tok[(name, l)])
        return dst

    def run_seq(self, sq):
        kb = self.kb
        TP, NT = sq.TP, sq.NT
        self.X = self.V(self.X_OFF, [16, D], F32)
        for t in range(NT):
            if sq.kind == "p":
                kb.dma("sp", self.X[:, t, :], self.xp[sq.b, t * 128:(t + 1) * 128, :])
            else:
                kb.dma("sp", self.X[0:TP, 0, :], self.xs[:, :])
        self.KT = self.V(self.KT_OFF, [4, sq.SK], BF16)
        self.VA = self.V(self.VA_OFF, [sq.NKT, 4, 130], BF16)
        for l in range(self.dbg_nl):
            self.run_layer(sq, l)
        for t in range(NT):
            if sq.kind == "p":
                kb.dma("sp", self.yp[sq.b, t * 128:(t + 1) * 128, :], self.X[:, t, :])
            else:
                kb.dma("sp", self.ys[:, :], self.X[0:TP, 0, :])

    def norm_transpose(self, sq, xrow, gcol, outT, xn, sqr, P, sc=0):
        kb = self.kb
        ss = self.stat[0:P, sc:sc + 1]
        rs = self.stat[0:P, sc + 1:sc + 2]
        kb.act(sqr, xrow, AF.Square, accum_out=ss)
        self.rstd(rs, ss, D, P)
        kb.ts("dve", xn, xrow, rs, ALU.mult)
        pt = self.ps_tr(8, P)
        for kc in range(8):
            kb.tr(pt[:, kc, :], xn[:, kc * 128:(kc + 1) * 128], self.identb[0:P, 0:P])
        kb.tt("dve", outT, pt, gcol.unsqueeze(2).broadcast_to([128, 8, P]), ALU.mult)

    def run_layer(self, sq, l):
        kb = self.kb
        TP, NT = sq.TP, sq.NT
        isp = sq.kind == "p"
        slot = self.lslot[self.lctr % 2]
        self.lctr += 1
        lam_init = 0.8 - 0.6 * float(np.exp(-0.3 * l))
        HB = self.HB_OFF

        kb.dma("sp", slot["rep"], self.rep[l, :, :])
        kb.dma("sp", slot["fm"], self.fm[l, :, :])
        wst_f = self.V(HB, [4, 128], F32)
        kb.dma("sp", wst_f, self.wst[l, :, :].rearrange("p (g t) -> p g t", g=4))
        rep = slot["rep"]
        g_qk = rep[:, 0:128].rearrange("p (a d) -> p a d", a=2)
        g_sub = rep[:, 128:256]
        g_gmn = rep[:, 256:384]
        g_xq = rep[:, 384:640]
        g_xk = rep[:, 640:896]
        lamv = rep[:, 896:1152].rearrange("p (a d) -> p a d", a=4)
        fmv = slot["fm"]
        gfm = fmv[:, 0:32].rearrange("p (a k) -> p a k", a=4)
        gm_b = fmv[:, 32:36]
        conv_w = fmv[:, 36:102].rearrange("p (f j) -> p f j", f=NFC)
        conv_b = fmv[:, 102:124]
        lam = slot["lam"]
        kb.tt("dve", slot["wsT"], wst_f, self.tril.unsqueeze(1).broadcast_to([128, 4, 128]), ALU.mult)
        prod = self.V(HB + 2048, [2, 64], F32)
        kb.tt("dve", prod[:, 0, :], lamv[:, 0, :], lamv[:, 1, :], ALU.mult)
        kb.tt("dve", prod[:, 1, :], lamv[:, 2, :], lamv[:, 3, :], ALU.mult)
        kb.reduce_add(lam[:, 0:2], prod)
        kb.act(lam[:, 0:2], lam[:, 0:2], AF.Exp)
        kb.tt("dve", lam[:, 2:3], lam[:, 1:2], lam[:, 0:1], ALU.subtract)
        kb.ts("dve", lam[:, 3:4], lam[:, 2:3], -lam_init, ALU.add)
        kb.ts("dve", g_sub, g_sub, 1.0 - lam_init, ALU.mult)

        if isp:
            wk_p = [self.load_panel(3 + i, "wk", l, (0, D), (i * 512, (i + 1) * 512), [8, 512]) for i in range(2)]
            wv_p = [self.load_panel(5 + i, "wv", l, (0, D), (i * 512, (i + 1) * 512), [8, 512]) for i in range(2)]
        win = [None] * 5
        for i in range(3):
            win[i] = self.load_panel(i, "w_in", l, (0, D), (i * 512, (i + 1) * 512), [8, 512])

        if self.dbg < 2:
            return
        self.stage_mem(sq, l, slot, gfm, g_xk, wk_p if isp else None, wv_p if isp else None)

        if self.dbg < 3:
            return
        for i in range(3, 5):
            win[i] = self.load_panel(i, "w_in", l, (0, D), (i * 512, (i + 1) * 512), [8, 512])
        woA = self.load_panel(5, "w_out", l, (0, 512), (0, D), [4, D])
        woB = self.load_panel(6, "w_out", l, (512, D), (0, D), [4, D])

        if not os.environ.get("K_NOMEMSET"):
            kb.memset("dve", self.VA[:, :, :, 128:130], 1.0)
        if not isp:
            self.load_past(sq, l)

        QT = self.V(HB + 21504, [4, 512], BF16)
        self.stage_a_norm(sq, 0, gfm)
        for blk in range(sq.NBLK):
            for ti in range(sq.TPB):
                t = blk * sq.TPB + ti
                nn = (lambda t=t: self.stage_a_norm(sq, t + 1, gfm)) if t + 1 < NT else None
                self.stage_a_tile(sq, l, t, ti, slot, gfm, g_qk, g_gmn, gm_b, win, woB, QT, nn)
            if blk == sq.NBLK - 1:
                wq_p = [self.load_panel(i, "wq", l, (0, D), (i * 512, (i + 1) * 512), [8, 512]) for i in range(2)]
                wo_p = [self.load_panel(2 + i, "wo", l, (0, D), (i * 512, (i + 1) * 512), [8, 512]) for i in range(2)]
            self.stage_b_block(sq, l, blk, slot, g_sub, lam, woA, QT)

        if self.dbg < 4:
            return
        fpan = {}
        fpan[0] = self.load_ffn_group(l, 0, 4)

        self.stage_c(sq, l, gfm, g_xq, wq_p, wo_p)

        if self.dbg < 5:
            return
        fpan[1] = self.load_ffn_group(l, 1, 0)

        self.stage_ffn(sq, l, conv_w, conv_b, fpan)

    def load_ffn_group(self, l, g, s0):
        nf = 4 if g < 5 else 2
        c0 = g * 512
        gate = self.load_panel(s0, "w_up", l, (0, D), (c0, c0 + nf * 128), [8, nf * 128])
        up = self.load_panel(s0 + 1, "w_up", l, (0, D), (DFF + c0, DFF + c0 + nf * 128), [8, nf * 128])
        down = self.load_panel(s0 + 2, "w_down", l, (c0, c0 + nf * 128), (0, D), [nf, D])
        return gate, up, down, nf

    def stage_mem(self, sq, l, slot, gfm, g_xk, wk_p, wv_p):
        kb = self.kb
        HB = self.HB_OFF
        isp = sq.kind == "p"
        memt = self.V(HB + 4096, [D], F32)
        xn = self.V(HB + 8192, [D], BF16)
        mT = self.V(HB + 10240, [8, 128], BF16)
        kst = self.V(HB + 12288, [D], F32)
        kbf = self.V(HB + 16384, [D], BF16)
        vst = self.V(HB + 18432, [D], F32)
        sqr = self.V(HB + 22528, [D], F32)
        ss4 = self.stat[:, 4:8]
        rs4 = self.stat[:, 8:12]
        for mt in range(2):
            if isp:
                kb.dma("sp", memt, self.memp[sq.b, mt * 128:(mt + 1) * 128, :])
                self.norm_transpose(sq, memt, gfm[:, 3, :], mT, xn, sqr, 128)
                pk = self.ps_pair()
                for cb in range(2):
                    for kc in range(8):
                        kb.mm(pk[:, cb, :], lhsT=mT[:, kc, :], rhs=wk_p[cb][:, kc, :], start=(kc == 0), stop=(kc == 7))
                pk2 = pk.rearrange("p a b -> p (a b)")
                kb.act(sqr, pk2, AF.Square)
                kb.reduce_add(ss4, sqr.rearrange("p (h d) -> p h d", h=4))
                self.rstd(rs4, ss4, 256, 128)
                k3 = kst.rearrange("p (h d) -> p h d", h=4)
                kb.tt("dve", k3, pk2.rearrange("p (h d) -> p h d", h=4),
                      rs4.unsqueeze(2).broadcast_to([128, 4, 256]), ALU.mult)
                kb.tt("dve", k3, k3, g_xk.unsqueeze(1).broadcast_to([128, 4, 256]), ALU.mult)
                kb.dma("sp", self.mkp[l, sq.b, mt * 128:(mt + 1) * 128, :], kst)
                kb.cp("act", kbf, kst)
                pv = self.ps_pair()
                for cb in range(2):
                    for kc in range(8):
                        kb.mm(pv[:, cb, :], lhsT=mT[:, kc, :], rhs=wv_p[cb][:, kc, :], start=(kc == 0), stop=(kc == 7))
                pv2 = pv.rearrange("p a b -> p (a b)")
                kb.cp("act", vst, pv2)
                kb.dma("sp", self.mvp[l, sq.b, mt * 128:(mt + 1) * 128, :], vst)
                kb.cp("dve", self.MVA[:, mt, :, 0:256], vst.rearrange("p (h d) -> p h d", h=4))
            else:
                kb.dma("sp", kst, self.cmk[l, mt * 128:(mt + 1) * 128, :])
                kb.cp("act", kbf, kst)
                kb.dma("sp", vst, self.cmv[l, mt * 128:(mt + 1) * 128, :])
                kb.cp("dve", self.MVA[:, mt, :, 0:256], vst.rearrange("p (h d) -> p h d", h=4))
            pt = self.ps_tr(8, 128)
            for kc in range(8):
                kb.tr(pt[:, kc, :], kbf[:, kc * 128:(kc + 1) * 128], self.identb)
            kb.cp("dve", self.mkT[:, :, mt * 128:(mt + 1) * 128], pt)

    def load_past(self, sq, l):
        kb = self.kb
        HB = self.HB_OFF
        st = self.V(HB + 4096, [512], F32)
        sb = self.V(HB + 6144, [512], BF16)
        for kt in range(8):
            kb.dma("sp", st, self.ck[l, kt * 128:(kt + 1) * 128, :])
            kb.cp("act", sb, st)
            pt = self.ps_tr(4, 128)
            for h in range(4):
                kb.tr(pt[:, h, :], sb[:, h * 128:(h + 1) * 128], self.identb)
            kb.cp("dve", self.KT[:, :, kt * 128:(kt + 1) * 128], pt)
            st2 = self.V(HB + 8192, [512], F32)
            kb.dma("sp", st2, self.cv[l, kt * 128:(kt + 1) * 128, :])
            kb.cp("dve", self.VA[:, kt, :, 0:128], st2.rearrange("p (h d) -> p h d", h=4))

    def a_bufs(self, sq):
        P = sq.TP
        HB = self.HB_OFF
        d = {}
        d["sqr"] = self.V(HB, [D], F32, P)
        d["zqk"] = self.V(HB + 4096, [D], F32, P)
        d["zv"] = self.V(HB + 8192, [512], F32, P)
        d["qkb"] = self.V(HB + 10240, [D], BF16, P)
        d["xn"] = self.V(HB + 12288, [D], BF16, P)
        d["gvb"] = self.V(HB + 12288, [512], BF16, P)
        d["gob"] = self.V(HB + 13312, [512], BF16, P)
        d["rope"] = self.V(HB + 14336, [4, 128], F32, P)
        d["goT"] = self.V(HB + 16384, [4, P], BF16)
        d["hT"] = [self.V(HB + 17408 + i * 2048, [8, P], BF16) for i in range(2)]
        return d

    def stage_a_norm(self, sq, t, gfm):
        bf = self.a_bufs(sq)
        P = sq.TP
        self.norm_transpose(sq, self.X[0:P, t, :], gfm[:, 0, :], bf["hT"][t % 2], bf["xn"], bf["sqr"], P)

    def stage_a_tile(self, sq, l, t, ti, slot, gfm, g_qk, g_gmn, gm_b, win, woB, QT, next_norm):
        kb = self.kb
        P = sq.TP
        isp = sq.kind == "p"
        bf = self.a_bufs(sq)
        sqr, zqk, zv, qkb, gvb, gob, rope, goT = (bf[k] for k in ["sqr", "zqk", "zv", "qkb", "gvb", "gob", "rope", "goT"])
        hT = bf["hT"][t % 2]
        cs = self.csp[:, t, :] if isp else self.css[0:P, :]
        cosv = cs[:, 0:8]
        sinv = cs[:, 8:16]
        xrow = self.X[0:P, t, :]

        pqk = self.ps_pair()
        for which in range(2):
            for kc in range(8):
                kb.mm(pqk[0:P, which, :], lhsT=hT[:, kc, :], rhs=win[which][:, kc, :], start=(kc == 0), stop=(kc == 7))
        pz3 = []
        for i, cg in enumerate((2, 3, 4)):
            pz = self.ps[0:P, 4 + i, :]
            for kc in range(8):
                kb.mm(pz, lhsT=hT[:, kc, :], rhs=win[cg][:, kc, :], start=(kc == 0), stop=(kc == 7))
            pz3.append(pz)
        pv_, pu, pg = pz3
        if next_norm is not None:
            next_norm()

        ss16 = self.stat[0:P, 16:32]
        rs16 = self.stat[0:P, 96:112]
        pqk2 = pqk[0:P].rearrange("p a b -> p (a b)")
        kb.act(sqr, pqk2, AF.Square)
        kb.reduce_add(ss16, sqr.rearrange("p (g d) -> p g d", g=16))
        self.rstd(rs16, ss16, 64, P)
        d3 = zqk.rearrange("p (g d) -> p g d", g=16)
        kb.tt("dve", d3, pqk2.rearrange("p (g d) -> p g d", g=16), rs16.unsqueeze(2).broadcast_to([P, 16, 64]), ALU.mult)
        d4 = zqk.rearrange("p (a g d) -> p a g d", a=2, g=8)
        kb.tt("dve", d4, d4, g_qk[0:P].unsqueeze(2).broadcast_to([P, 2, 8, 64]), ALU.mult)
        x1 = d3[:, :, 0:8]
        x2 = d3[:, :, 8:16]
        cb_ = cosv.unsqueeze(1).broadcast_to([P, 16, 8])
        sb_ = sinv.unsqueeze(1).broadcast_to([P, 16, 8])
        r3 = rope.rearrange("p a (g d) -> p a g d", g=16)
        kb.tt("dve", r3[:, 0], x1, cb_, ALU.mult)
        kb.tt("dve", r3[:, 1], x2, sb_, ALU.mult)
        kb.tt("dve", r3[:, 2], x2, cb_, ALU.mult)
        kb.tt("dve", r3[:, 3], x1, sb_, ALU.mult)
        kb.tt("dve", x1, r3[:, 0], r3[:, 1], ALU.subtract)
        kb.tt("dve", x2, r3[:, 2], r3[:, 3], ALU.add)
        kb.cp("act", qkb, zqk)
        zk = zqk[:, 512:1024]
        if isp:
            kb.dma("sp", self.dkp[l, sq.b, t * 128:(t + 1) * 128, :], zk)
        else:
            kb.dma("sp", self.dks[l, :, :], zk)
        pt = self.ps_tr(8, P)
        for j in range(8):
            kb.tr(pt[:, j, :], qkb[:, j * 128:(j + 1) * 128], self.identb[0:P, 0:P])
        kb.cp("dve", QT[:, :, ti * 128:ti * 128 + P], pt[:, 0:4, :])
        kpos = t * 128 if isp else PAST
        kb.cp("act", self.KT[:, :, kpos:kpos + P], pt[:, 4:8, :])
        kb.cp("act", zv, pv_)
        if isp:
            kb.dma("sp", self.dvp[l, sq.b, t * 128:(t + 1) * 128, :], zv)
        else:
            kb.dma("sp", self.dvs[l, :, :], zv)
        vt = t if isp else 8
        kb.cp("dve", self.VA[0:P, vt, :, 0:128], zv.rearrange("p (h d) -> p h d", h=4))
        ug = zqk[:, 0:512]
        kb.act(ug, pu, AF.Gelu_apprx_tanh)
        gg = sqr[:, 0:512]
        kb.act(gg, pg, AF.Gelu_apprx_tanh)
        sq2 = sqr[:, 512:1024]
        kb.act(sq2, gg, AF.Square)
        ss4 = self.stat[0:P, 4:8]
        rs4 = self.stat[0:P, 8:12]
        kb.reduce_add(ss4, sq2.rearrange("p (g d) -> p g d", g=4))
        self.rstd(rs4, ss4, 128, P)
        gg3 = gg.rearrange("p (g d) -> p g d", g=4)
        kb.tt("dve", gg3, gg3, rs4.unsqueeze(2).broadcast_to([P, 4, 128]), ALU.mult)
        kb.tt("dve", gg3, gg3, g_gmn[0:P, :].unsqueeze(1).broadcast_to([P, 4, 128]), ALU.mult)
        if not isp:
            kb.dma("sp", self.gvs[l, :, :], gg)
        kb.cp("act", gvb, gg)
        pgate = self.ps_mm()[0:P, :]
        for g in range(4):
            kb.mm(pgate[:, g * 128:(g + 1) * 128], lhsT=slot["wsT"][0:P, g, 0:P], rhs=gvb[:, g * 128:(g + 1) * 128])
        go3 = sq2.rearrange("p (g d) -> p g d", g=4)
        kb.tt("dve", go3, pgate.rearrange("p (g d) -> p g d", g=4),
              gm_b[0:P, :].unsqueeze(2).broadcast_to([P, 4, 128]), ALU.add)
        kb.tt("dve", gob, sq2, ug, ALU.mult)
        pt2 = self.ps_tr(4, P)
        for j in range(4):
            kb.tr(pt2[:, j, :], gob[:, j * 128:(j + 1) * 128], self.identb[0:P, 0:P])
        kb.cp("act", goT, pt2)
        py = self.ps_pair()
        for cb in range(2):
            for kc in range(4):
                kb.mm(py[0:P, cb, :], lhsT=goT[:, kc, :], rhs=woB[:, kc, cb * 512:(cb + 1) * 512],
                      start=(kc == 0), stop=(kc == 3))
        kb.tt("dve", xrow, py[0:P].rearrange("p a b -> p (a b)"), xrow, ALU.add)

    def stage_b_block(self, sq, l, blk, slot, g_sub, lam, woA, QT):
        kb = self.kb
        HB = self.HB_OFF
        isp = sq.kind == "p"
        P = sq.TP
        QB = sq.QB
        eTs = [self.V(HB + 25600 + i * 2048, [2, 512], BF16) for i in range(2)]
        OB = self.V(HB + 25600, [4, 512], BF16)
        otmp = self.V(HB + 29696, [128], F32)
        oT = self.V(HB + 30208, [4, 128], BF16)
        OF = self.V(HB, [4, 512], F32)
        osq = self.V(HB + 8192, [512], F32)
        accS = self.V(HB + 10240, [3, 480], F32)
        nqs = sq.TPB
        if isp:
            nkt = 4 * blk + 4
        else:
            nkt = 9
        rz = self.stat[0:P, 32:48]
        ssq = self.stat[0:P, 48:49]
        rso = self.stat[0:P, 49:50]
        nl = self.stat[0:P, 50:51]
        def acc(qs, m):
            r = qs * 2 + m
            return self.ps[0:P, 4 + r // 3, (r % 3) * 160:(r % 3) * 160 + 129]

        def accs(qs, m):
            r = qs * 2 + m
            return accS[0:P, r // 3, (r % 3) * 160:(r % 3) * 160 + 129]

        steps = [(h, kt) for h in range(4) for kt in range(nkt)]

        def qk_exp(i):
            h, kt = steps[i]
            KP = 128 if (isp or kt < 8) else DS
            j = kt - 4 * blk if isp else -1
            q0 = max(0, j) * 128
            kpos = kt * 128
            psS = self.ps_pair()
            for m in range(2):
                kb.mm(psS[0:KP, m, q0:QB], lhsT=self.KT[m * 64:(m + 1) * 64, h, kpos:kpos + KP],
                      rhs=QT[m * 64:(m + 1) * 64, h, q0:QB])
            eT = eTs[i % 2]
            kb.act(eT[0:KP, :, q0:QB], psS[0:KP, :, q0:QB], AF.Exp, scale=0.125)
            if j >= 0:
                kb.memset("dve", eT[64:128, :, q0:q0 + 64], 0.0)

        def pv(i):
            h, kt = steps[i]
            KP = 128 if (isp or kt < 8) else DS
            j = kt - 4 * blk if isp else -1
            eT = eTs[i % 2]
            for qs in range(max(0, j), nqs):
                last = (kt == 4 * blk + qs) if isp else (kt == nkt - 1)
                for m in range(2):
                    kb.mm(acc(qs, m), lhsT=eT[0:KP, m, qs * 128:qs * 128 + P], rhs=self.VA[0:KP, kt, h, 0:129],
                          start=(kt == 0 and (qs * 2 + m) % 3 == 0), stop=last)
            if kt == nkt - 1:
                nreg = 2 * nqs
                for bk in range((nreg + 2) // 3):
                    w_ = min(3, nreg - 3 * bk) * 160
                    kb.cp("act", accS[0:P, bk, 0:w_], self.ps[0:P, 4 + bk, 0:w_])
                for qs in range(nqs):
                    a0 = accs(qs, 0)
                    a1 = accs(qs, 1)
                    kb.recip(rz[:, 0:1], a0[:, 128:129])
                    kb.recip(rz[:, 1:2], a1[:, 128:129])
                    kb.tt("dve", nl, rz[:, 1:2], lam[0:P, 3:4], ALU.mult)
                    ot = otmp[0:P, :]
                    kb.ts("dve", ot, a0[:, 0:128], rz[:, 0:1], ALU.mult)
                    kb.stt("dve", OF[0:P, qs, h * 128:(h + 1) * 128], a1[:, 0:128], nl, ot, ALU.mult, ALU.add)

        qk_exp(0)
        for i in range(len(steps)):
            if i + 1 < len(steps):
                qk_exp(i + 1)
            pv(i)
        ss4 = self.stat[0:P, 4:8]
        rs4 = self.stat[0:P, 8:12]
        for qs in range(nqs):
            t = blk * sq.TPB + qs
            of = OF[0:P, qs, :]
            kb.act(osq[0:P, :], of, AF.Square)
            kb.reduce_add(ss4, osq[0:P, :].rearrange("p (h d) -> p h d", h=4))
            self.rstd(rs4, ss4, 128, P)
            of3 = of.rearrange("p (h d) -> p h d", h=4)
            kb.tt("dve", of3, of3, rs4.unsqueeze(2).broadcast_to([P, 4, 128]), ALU.mult)
            kb.tt("dve", OB[0:P, qs, :].rearrange("p (h d) -> p h d", h=4), of3,
                  g_sub[0:P, :].unsqueeze(1).broadcast_to([P, 4, 128]), ALU.mult)
            pt = self.ps_tr(4, P)
            for j in range(4):
                kb.tr(pt[:, j, :], OB[0:P, qs, j * 128:(j + 1) * 128], self.identb[0:P, 0:P])
            kb.cp("act", oT[:, :, 0:P], pt)
            py = self.ps_pair()
            for cb in range(2):
                for kc in range(4):
                    kb.mm(py[0:P, cb, :], lhsT=oT[:, kc, 0:P], rhs=woA[:, kc, cb * 512:(cb + 1) * 512],
                          start=(kc == 0), stop=(kc == 3))
            xrow = self.X[0:P, t, :]
            kb.tt("dve", xrow, py[0:P].rearrange("p a b -> p (a b)"), xrow, ALU.add)

    def stage_c(self, sq, l, gfm, g_xq, wq_p, wo_p):
        kb = self.kb
        P = sq.TP
        NT = sq.NT
        o = self.KT_OFF
        xn1 = self.V(o, [D], BF16, P); o += 2048
        sq1 = self.V(o, [D], F32, P); o += 4096
        h2T = self.V(o, [8, P], BF16); o += 2048
        qcn = self.V(o, [D], BF16, P); o += 2048
        qcTs = [self.V(o + i * 2048, [8, P], BF16) for i in range(2)]; o += 4096
        eT = self.V(o, [4, 2, P], BF16); o += 2048
        ocb = self.V(o, [D], BF16, P); o += 2048
        ocTs = [self.V(o + i * 2048, [8, P], BF16) for i in range(2)]; o += 4096
        xn3 = self.V(o, [D], BF16, P); o += 2048
        sq3 = self.V(o, [D], F32, P); o += 4096
        assert o <= self.KT_OFF + 33024
        ss4 = self.stat[0:P, 4:8]
        rs4 = self.stat[0:P, 8:12]
        rz4 = self.stat[0:P, 68:72]
        HBv = self.V(self.HB_OFF, [8, sq.S], BF16)

        def c1(t):
            xrow = self.X[0:P, t, :]
            qcT = qcTs[t % 2]
            self.norm_transpose(sq, xrow, gfm[:, 1, :], h2T, xn1, sq1, P)
            pq = self.ps_pair()
            for cb in range(2):
                for kc in range(8):
                    kb.mm(pq[0:P, cb, :], lhsT=h2T[:, kc, :], rhs=wq_p[cb][:, kc, :], start=(kc == 0), stop=(kc == 7))
            pq2 = pq[0:P].rearrange("p a b -> p (a b)")
            kb.act(sq1, pq2, AF.Square)
            kb.reduce_add(ss4, sq1.rearrange("p (h d) -> p h d", h=4))
            self.rstd(rs4, ss4, 256, P)
            s3 = sq1.rearrange("p (h d) -> p h d", h=4)
            kb.tt("dve", s3, pq2.rearrange("p (h d) -> p h d", h=4), rs4.unsqueeze(2).broadcast_to([P, 4, 256]), ALU.mult)
            kb.tt("dve", qcn.rearrange("p (h d) -> p h d", h=4), s3, g_xq[0:P, :].unsqueeze(1).broadcast_to([P, 4, 256]),
                  ALU.mult)
            pt = self.ps_tr(8, P)
            for j in range(8):
                kb.tr(pt[:, j, :], qcn[:, j * 128:(j + 1) * 128], self.identb[0:P, 0:P])
            kb.cp("act", qcT, pt)

        def c2(t):
            qcT = qcTs[t % 2]
            ocT = ocTs[t % 2]
            psS = self.ps_pair()
            for h in range(4):
                for mt in range(2):
                    c0 = (h * 2 + mt) * 128
                    dst = psS[:, c0 // 512, (c0 % 512):(c0 % 512) + P]
                    for dc in range(2):
                        kb.mm(dst, lhsT=self.mkT[:, h * 2 + dc, mt * 128:(mt + 1) * 128], rhs=qcT[:, h * 2 + dc, :],
                              start=(dc == 0), stop=(dc == 1))
            for half in range(2):
                kb.act(eT[:, half * 2:half * 2 + 2, :, :],
                       psS[:, half, :].rearrange("p (h m q) -> p h m q", h=2, m=2)[:, :, :, 0:P], AF.Exp, scale=1.0 / 16.0)
            for h in range(4):
                po = self.ps[0:P, 4 + (h % 3), 0:257]
                for mt in range(2):
                    kb.mm(po, lhsT=eT[:, h, mt, :], rhs=self.MVA[:, mt, h, 0:257], start=(mt == 0), stop=(mt == 1))
                kb.recip(rz4[:, h:h + 1], po[:, 256:257])
                kb.ts("dve", ocb[:, h * 256:(h + 1) * 256], po[:, 0:256], rz4[:, h:h + 1], ALU.mult)
            pt2 = self.ps_tr(8, P)
            for j in range(8):
                kb.tr(pt2[:, j, :], ocb[:, j * 128:(j + 1) * 128], self.identb[0:P, 0:P])
            kb.cp("act", ocT, pt2)

        def c3(t):
            xrow = self.X[0:P, t, :]
            ocT = ocTs[t % 2]
            py = self.ps_pair()
            for cb in range(2):
                for kc in range(8):
                    kb.mm(py[0:P, cb, :], lhsT=ocT[:, kc, :], rhs=wo_p[cb][:, kc, :], start=(kc == 0), stop=(kc == 7))
            kb.tt("dve", xrow, py[0:P].rearrange("p a b -> p (a b)"), xrow, ALU.add)
            self.norm_transpose(sq, xrow, gfm[:, 2, :], HBv[:, :, t * 128:t * 128 + P], xn3, sq3, P, sc=64)

        for i in range(NT + 2):
            if i < NT:
                c1(i)
            if 0 <= i - 1 < NT:
                c2(i - 1)
            if 0 <= i - 2 < NT:
                c3(i - 2)

    def stage_ffn(self, sq, l, conv_w, conv_b, fpan):
        kb = self.kb
        isp = sq.kind == "p"
        P = sq.TP
        NTOK = sq.QB
        HBv = self.V(self.HB_OFF, [8, sq.S], BF16)
        o = self.KT_OFF
        gss = [self.V(o + i * 2064, [516], F32) for i in range(2)]; o += 4128
        ccs = [self.V(o + i * 2048, [512], F32) for i in range(2)]; o += 4096
        scs = [self.V(o + i * 2048, [512], F32) for i in range(2)]; o += 4096
        aTs = [self.V(o + i * 4096, [4, 512], BF16) for i in range(2)]; o += 8192
        carry = self.V(o, [NFC, 2], F32); o += 176
        cst = self.V(o, [512], F32, 2); o += 2048
        assert o <= self.KT_OFF + 33024
        if isp:
            kb.memset("dve", carry, 0.0)
        else:
            kb.dma("sp", carry, self.convst_d[:, l, :, :])
        ri = 0
        ai = 0
        pending = None
        for g in range(6):
            gate, up, down, nf = fpan[g]
            for blk in range(sq.NBLK):
                aT = aTs[ai % 2]
                ai += 1
                for fl in range(nf):
                    fc = g * 4 + fl
                    pg = self.ps_mm()
                    for kc in range(8):
                        kb.mm(pg[:, 0:NTOK], lhsT=gate[:, kc, fl * 128:(fl + 1) * 128],
                              rhs=HBv[:, kc, blk * 512:blk * 512 + NTOK], start=(kc == 0), stop=(kc == 7))
                    pu = self.ps_mm()
                    for kc in range(8):
                        kb.mm(pu[:, 0:NTOK], lhsT=up[:, kc, fl * 128:(fl + 1) * 128],
                              rhs=HBv[:, kc, blk * 512:blk * 512 + NTOK], start=(kc == 0), stop=(kc == 7))
                    gs = gss[ri % 2]
                    cc = ccs[ri % 2]
                    sc = scs[ri % 2]
                    ri += 1
                    kb.cp("act", gs[:, 2:2 + NTOK], pg[:, 0:NTOK])
                    kb.cp("dve", gs[:, 0:2], carry[:, fc, :])
                    kb.cp("dve", carry[:, fc, :], gs[:, NTOK:NTOK + 2])
                    kb.act(cc[:, 0:NTOK], pg[:, 0:NTOK], AF.Identity, bias=conv_b[:, fc:fc + 1], scale=conv_w[:, fc, 2:3])
                    kb.stt("dve", cc[:, 0:NTOK], gs[:, 1:1 + NTOK], conv_w[:, fc, 1:2], cc[:, 0:NTOK], ALU.mult, ALU.add)
                    kb.stt("dve", cc[:, 0:NTOK], gs[:, 0:NTOK], conv_w[:, fc, 0:1], cc[:, 0:NTOK], ALU.mult, ALU.add)
                    kb.act(sc[:, 0:NTOK], cc[:, 0:NTOK], AF.Silu)
                    kb.tt("dve", aT[:, fl, 0:NTOK], sc[:, 0:NTOK], pu[:, 0:NTOK], ALU.mult)
                def do_down(blk=blk, aT=aT, down=down, nf=nf):
                    for ti in range(sq.TPB):
                        t = blk * sq.TPB + ti
                        py = self.ps[0:P, 4 + 2 * (t % 2):6 + 2 * (t % 2), :]
                        for cb in range(2):
                            for fl in range(nf):
                                kb.mm(py[:, cb, :], lhsT=aT[:, fl, ti * 128:ti * 128 + P],
                                      rhs=down[:, fl, cb * 512:(cb + 1) * 512], start=(fl == 0), stop=(fl == nf - 1))
                        xrow = self.X[0:P, t, :]
                        kb.tt("dve", xrow, py.rearrange("p a b -> p (a b)"), xrow, ALU.add)
                if pending is not None:
                    pending()
                pending = do_down
            if pending is not None:
                pending()
                pending = None
            pc = self.ps[0:2, 3, :]
            for fl in range(nf):
                kb.tr(pc[:, fl * 128:(fl + 1) * 128], carry[:, g * 4 + fl, :], self.identf)
            kb.cp("act", cst[:, 0:nf * 128], pc[:, 0:nf * 128])
            dst = self.fcp[l, sq.b, :, g * 512:g * 512 + nf * 128] if isp else self.fcs[l, :, g * 512:g * 512 + nf * 128]
            kb.dma("sp", dst, cst[:, 0:nf * 128])
            if g + 2 < 6:
                fpan[g + 2] = self.load_ffn_group(l, g + 2, 4 if (g % 2 == 0) else 0)


_PROG = None


def _get_prog():
    global _PROG
    if _PROG is None:
        _PROG = Prog()
    return _PROG


def _host_consts(inp):
    f32 = np.float32
    rep = np.zeros((L, 128, NREP), f32)
    fm = np.zeros((L, 128, NFM), f32)
    for l in range(L):
        v = np.concatenate([inp["da_q_norm_g"][l], inp["da_k_norm_g"][l], inp["da_subln_g"][l], inp["gm_norm_g"][l],
                            inp["xq_norm_g"][l], inp["xk_norm_g"][l], inp["lambda_q1"][l], inp["lambda_k1"][l],
                            inp["lambda_q2"][l], inp["lambda_k2"][l]]).astype(f32)
        rep[l] = np.broadcast_to(v[None, :], (128, NREP))
        for i, nm in enumerate(["norm_mix_g", "norm_x_g", "norm_ffn_g", "norm_mem_g"]):
            fm[l, :, i * 8:(i + 1) * 8] = inp[nm][l].reshape(8, 128).T
        fm[l, :, 32:36] = inp["gm_b"][l].T
        cw = inp["conv_w"][l].reshape(3, NFC, 128)
        fm[l, :, 36:102] = cw.transpose(2, 1, 0).reshape(128, NFC * 3)
        fm[l, :, 102:124] = inp["conv_b"][l].reshape(NFC, 128).T
    wst = np.ascontiguousarray(inp["gm_w_s"].transpose(0, 3, 1, 2)).reshape(L, 128, 512).astype(f32)
    ident = np.eye(128, dtype=f32)
    tril = np.triu(np.ones((128, 128), f32))
    half = 8
    inv = (np.float32(500000.0) ** (-np.arange(half, dtype=f32) / np.float32(half))).astype(f32)

    def cs(pos):
        ang = pos.astype(f32)[:, None] * inv[None, :]
        return np.concatenate([np.cos(ang), np.sin(ang)], axis=1).astype(f32)

    csp = cs(np.arange(S)).reshape(16, 128, 16).transpose(1, 0, 2)
    css = cs(PAST + np.arange(DS))
    return dict(rep=rep, fm=fm, wst=wst, ident=ident, tril=tril, csp=np.ascontiguousarray(csp), css=css)


def _in_maps(inp, cores):
    hc = _host_consts(inp)
    in_maps = []
    for c in cores:
        m = dict(hc)
        m["xp"] = np.ascontiguousarray(inp["x_prompt"][c * NB:(c + 1) * NB])
        m["xs"] = np.ascontiguousarray(inp["x_sample"][c])
        m["ck"] = np.ascontiguousarray(inp["cache_da_k"][:, c].reshape(L, PAST, 512))
        m["cv"] = np.ascontiguousarray(inp["cache_da_v"][:, c].reshape(L, PAST, 512))
        m["cmk"] = np.ascontiguousarray(inp["cache_mem_k"][:, c].reshape(L, MEM, D))
        m["cmv"] = np.ascontiguousarray(inp["cache_mem_v"][:, c].reshape(L, MEM, D))
        m["memp"] = np.ascontiguousarray(inp["mem_prompt"][c * NB:(c + 1) * NB])
        st = inp["state_ffn_conv"][:, c]
        m["convst"] = np.ascontiguousarray(st.reshape(L, 2, NFC, 128).transpose(3, 0, 2, 1))
        m["w_in"] = inp["w_in"]; m["w_out"] = inp["w_out"]
        m["wq"] = inp["wq_c"]; m["wk"] = inp["wk_c"]; m["wv"] = inp["wv_c"]; m["wo"] = inp["wo_c"]
        m["w_up"] = inp["w_up"]; m["w_down"] = inp["w_down"]
        in_maps.append(m)
    return in_maps


def kernel(**inp):
    inp = {k: np.asarray(v) for k, v in inp.items()}
    prog = _get_prog()
    n = 8
    in_maps = _in_maps(inp, range(n))
    res = run_bass_kernel_spmd(prog.nc, in_maps, core_ids=list(range(n))).results
    cat = lambda k, ax: np.concatenate([r[k] for r in res], axis=ax)
    y_p = cat("yp", 0)
    y_s = np.stack([r["ys"] for r in res], 0)
    dk_p = cat("dkp", 1).reshape(L, 32, S, 4, 2, 64)
    dv_p = cat("dvp", 1).reshape(L, 32, S, 4, 128)
    mk_p = cat("mkp", 1).reshape(L, 32, MEM, 4, 256)
    mv_p = cat("mvp", 1).reshape(L, 32, MEM, 4, 256)
    fc_p = cat("fcp", 1)
    dk_s = np.stack([r["dks"] for r in res], 1).reshape(L, 8, DS, 4, 2, 64)
    dv_s = np.stack([r["dvs"] for r in res], 1).reshape(L, 8, DS, 4, 128)
    gv_s = np.stack([r["gvs"] for r in res], 1).reshape(L, 8, DS, 4, 128)
    fc_s = np.stack([r["fcs"] for r in res], 1)
    return (y_p, y_s, dk_p, dv_p, mk_p, mv_p, fc_p, dk_s, dv_s, gv_s, fc_s)
```

```python
import os
import numpy as np
from contextlib import ExitStack
import concourse.bass as bass
import concourse.mybir as mybir
from concourse.bass_utils import run_bass_kernel_spmd

F32 = mybir.dt.float32
BF16 = mybir.dt.bfloat16
AF = mybir.ActivationFunctionType
ALU = mybir.AluOpType
AX = mybir.AxisListType
CELL = 256
SAME_ENGINE_SYNC = bool(int(os.environ.get("K_SES", "0")))
SMALL_T = int(os.environ.get("K_SMALL", "256"))
N_DMA_SEMS = 12

D = 1024
L = 4
NB = 4
S = 2048
DS = 32
PAST = 1024
MEM = 256
DFF = 2816
NFC = 22
EPS = 1e-6
NREP = 1152
NFM = 124


def _esize(dt):
    return 2 if dt == BF16 else 4


class KB:
    def __init__(self):
        self.nc = bass.Bass("TRN2", target_bir_lowering=False)
        nc = self.nc
        self.es = ExitStack()
        self.eng = {"pe": nc.tensor, "act": nc.scalar, "dve": nc.vector, "pool": nc.gpsimd, "sp": nc.sync}
        self.sems = {}
        self.sem_id = {}
        self.cnt = {}
        self._nsem = 0
        for e in ["pe", "act", "dve", "pool"]:
            self.sems[e] = self._newsem("c_" + e)
            self.cnt[e] = 0
        self.dq = {}
        for q in ["sp", "pool"]:
            pool = [self._newsem(f"d_{q}{i}") for i in range(N_DMA_SEMS)]
            self.dq[q] = {"sems": pool, "vals": [0] * N_DMA_SEMS, "next": 0}
        self.seen = {e: {} for e in ["pe", "act", "dve", "pool", "sp"]}
        self.cells = {}
        self.n_inst = 0
        self.n_wait = 0

    def _newsem(self, name):
        h = self.es.enter_context(self.nc.semaphore(name))
        k = self._nsem
        self._nsem += 1
        self.sem_id[k] = h
        return k

    def _cells(self, ap):
        sp = str(ap.space).upper()
        if "SB" not in sp and "PSUM" not in sp:
            return None
        es = _esize(ap.dtype)
        a = ap.ap
        pstep = a[0][0]
        off = ap.offset
        base = (off % pstep) * es if pstep > 0 else off * es
        region = ap.tensor.name
        cell = CELL if "SB" in sp else 2048
        dims = [(s * es, c) for (s, c) in a[1:]]
        if not dims:
            dims = [(es, 1)]
        ls, lc = dims[-1]
        run = (lc - 1) * ls + es
        starts = [base]
        for (s, c) in dims[:-1]:
            if s == 0 or c == 1:
                continue
            starts = [st + i * s for st in starts for i in range(c)]
        out = set()
        for st in starts:
            for c in range(st // cell, (st + run - 1) // cell + 1):
                out.add((region, c))
        fsz = 1
        for (s_, c_) in a[1:]:
            if s_ != 0:
                fsz *= c_
        self._last_fsz = fsz
        return out

    def _deps(self, reads, writes, own=None):
        deps = {}
        rc = set()
        wc = set()
        msize = 1 << 30
        for ap in reads:
            c = self._cells(ap)
            if c:
                msize = min(msize, self._last_fsz)
                if "PSUM" in str(ap.space).upper():
                    wc |= c
                else:
                    rc |= c
        for ap in writes:
            c = self._cells(ap)
            if c:
                msize = min(msize, self._last_fsz)
                wc |= c
        self._msize = msize
        cells = self.cells
        small = msize <= SMALL_T

        def add(tok):
            k, v, sz = tok
            if k == own and not SAME_ENGINE_SYNC and not (small or sz <= SMALL_T):
                return
            if deps.get(k, 0) < v:
                deps[k] = v
        for c in rc:
            st = cells.get(c)
            if st is not None and st[0] is not None:
                add(st[0])
        for c in wc:
            st = cells.get(c)
            if st is not None:
                if st[0] is not None:
                    add(st[0])
                for k, (v, sz) in st[1].items():
                    add((k, v, sz))
        return deps, rc, wc

    def _commit(self, tok, rc, wc):
        k, v = tok
        sz = self._msize
        cells = self.cells
        for c in rc:
            if c in wc:
                continue
            st = cells.get(c)
            if st is None:
                st = [None, {}]
                cells[c] = st
            old = st[1].get(k)
            if old is None or old[0] < v:
                st[1][k] = (v, sz)
        t3 = (k, v, sz)
        for c in wc:
            cells[c] = [t3, {}]

    def _waits(self, ename, deps):
        e = self.eng[ename]
        seen = self.seen[ename]
        own = self.sems.get(ename)
        for k, v in deps.items():
            if k == own and ename == "pe":
                continue
            if seen.get(k, 0) >= v:
                continue
            e.wait_ge(self.sem_id[k], v)
            seen[k] = v
            self.n_wait += 1
            if os.environ.get("K_TRACE"):
                print("   WAIT", ename, "sem", k, ">=", v)

    def I(self, ename, fn, reads, writes):
        deps, rc, wc = self._deps(reads, writes, self.sems.get(ename))
        self._waits(ename, deps)
        ins = fn()
        self.cnt[ename] += 1
        k = self.sems[ename]
        if os.environ.get("K_TRACE"):
            print("INS", ename, self.cnt[ename], str(ins)[:150])
        ins.then_inc(self.sem_id[k], 1)
        self._commit((k, self.cnt[ename]), rc, wc)
        self.n_inst += 1
        return ins

    def dma(self, q, out, in_, extra_wait=None):
        deps, rc, wc = self._deps([in_], [out])
        if extra_wait:
            for k, v in extra_wait:
                if deps.get(k, 0) < v:
                    deps[k] = v
        d = self.dq[q]
        i = d["next"]
        d["next"] = (i + 1) % N_DMA_SEMS
        k = d["sems"][i]
        if d["vals"][i] > 0:
            deps[k] = max(deps.get(k, 0), d["vals"][i])
        self._waits(q, deps)
        ins = self.eng[q].dma_start(out=out, in_=in_)
        d["vals"][i] += 16
        ins.then_inc(self.sem_id[k], 16)
        tok = (k, d["vals"][i])
        if os.environ.get("K_TRACE"):
            print("DMA", q, tok, str(ins)[:150])
        self._commit(tok, rc, wc)
        self.n_inst += 1
        return tok

    def finish(self):
        last = {}
        for q, d in self.dq.items():
            for k, v in zip(d["sems"], d["vals"]):
                if v > 0:
                    last[k] = v
        for e in ["pe", "act", "dve", "pool"]:
            if self.cnt[e] > 0:
                last[self.sems[e]] = self.cnt[e]
        self._waits("sp", last)

    def mm(self, out, lhsT, rhs, start=True, stop=True):
        return self.I("pe", lambda: self.nc.tensor.matmul(out, lhsT=lhsT, rhs=rhs, start=start, stop=stop),
                      [lhsT, rhs], [out])

    def tr(self, out, in_, ident):
        return self.I("pe", lambda: self.nc.tensor.transpose(out, in_, ident), [in_, ident], [out])

    def act(self, out, in_, func, bias=None, scale=1.0, accum_out=None):
        reads = [in_]
        kw = {}
        if bias is not None:
            kw["bias"] = bias
            if not isinstance(bias, (int, float)):
                reads.append(bias)
        if not isinstance(scale, (int, float)):
            reads.append(scale)
        writes = [out]
        if accum_out is not None:
            kw["accum_out"] = accum_out
            writes.append(accum_out)
        return self.I("act", lambda: self.nc.scalar.activation(out=out, in_=in_, func=func, scale=scale, **kw),
                      reads, writes)

    def tt(self, e, out, in0, in1, op):
        return self.I(e, lambda: self.eng[e].tensor_tensor(out=out, in0=in0, in1=in1, op=op), [in0, in1], [out])

    def ts(self, e, out, in0, s1, op0, s2=None, op1=None):
        reads = [in0]
        if not isinstance(s1, (int, float)):
            reads.append(s1)
        if s2 is not None and not isinstance(s2, (int, float)):
            reads.append(s2)
        kw = {}
        if op1 is not None:
            kw["op1"] = op1
        return self.I(e, lambda: self.eng[e].tensor_scalar(out=out, in0=in0, scalar1=s1, scalar2=s2, op0=op0, **kw),
                      reads, [out])

    def stt(self, e, out, in0, scalar, in1, op0, op1):
        reads = [in0, in1]
        if not isinstance(scalar, (int, float)):
            reads.append(scalar)
        return self.I(e, lambda: self.eng[e].scalar_tensor_tensor(out=out, in0=in0, scalar=scalar, in1=in1,
                                                                  op0=op0, op1=op1), reads, [out])

    def cp(self, e, out, in_):
        if e == "act":
            return self.I("act", lambda: self.nc.scalar.copy(out=out, in_=in_), [in_], [out])
        return self.I(e, lambda: self.eng[e].tensor_copy(out=out, in_=in_), [in_], [out])

    def memset(self, e, out, val):
        return self.I(e, lambda: self.eng[e].memset(out, val), [], [out])

    def recip(self, out, in_):
        return self.I("dve", lambda: self.nc.vector.reciprocal(out=out, in_=in_), [in_], [out])

    def reduce_add(self, out, in_):
        return self.I("dve", lambda: self.nc.vector.tensor_reduce(out=out, in_=in_, axis=AX.X, op=ALU.add),
                      [in_], [out])


class Seq:
    def __init__(self, kind, b):
        self.kind = kind
        self.b = b
        if kind == "p":
            self.S, self.TP, self.NT, self.QB, self.NBLK, self.TPB = S, 128, 16, 512, 4, 4
            self.SK, self.NKT = S, 16
        else:
            self.S, self.TP, self.NT, self.QB, self.NBLK, self.TPB = DS, DS, 1, DS, 1, 1
            self.SK, self.NKT = PAST + DS, 9


class Prog:
    def __init__(self):
        self.kb = KB()
        kb = self.kb
        nc = kb.nc
        self.nc = nc

        def din(name, shape, dt=F32):
            return nc.dram_tensor(name, list(shape), dt, kind="ExternalInput").ap()

        def dout(name, shape):
            return nc.dram_tensor(name, list(shape), F32, kind="ExternalOutput").ap()

        def dint(name, shape):
            return nc.dram_tensor(name, list(shape), BF16, kind="Internal").ap()

        self.xp = din("xp", [NB, S, D])
        self.xs = din("xs", [DS, D])
        self.ck = din("ck", [L, PAST, 512])
        self.cv = din("cv", [L, PAST, 512])
        self.cmk = din("cmk", [L, MEM, D])
        self.cmv = din("cmv", [L, MEM, D])
        self.memp = din("memp", [NB, MEM, D])
        self.wf = {
            "w_in": din("w_in", [L, D, 2560]), "w_out": din("w_out", [L, D, D]),
            "wq": din("wq", [L, D, D]), "wk": din("wk", [L, D, D]), "wv": din("wv", [L, D, D]),
            "wo": din("wo", [L, D, D]), "w_up": din("w_up", [L, D, 2 * DFF]), "w_down": din("w_down", [L, DFF, D]),
        }
        self.wb = {k: dint("b_" + k, v.shape) for k, v in self.wf.items()}
        self.rep = din("rep", [L, 128, NREP])
        self.fm = din("fm", [L, 128, NFM])
        self.wst = din("wst", [L, 128, 512])
        self.ident_d = din("ident", [128, 128])
        self.tril_d = din("tril", [128, 128])
        self.csp_d = din("csp", [128, 16, 16])
        self.css_d = din("css", [DS, 16])
        self.convst_d = din("convst", [128, L, NFC, 2])
        self.yp = dout("yp", [NB, S, D])
        self.ys = dout("ys", [DS, D])
        self.dkp = dout("dkp", [L, NB, S, 512])
        self.dvp = dout("dvp", [L, NB, S, 512])
        self.mkp = dout("mkp", [L, NB, MEM, D])
        self.mvp = dout("mvp", [L, NB, MEM, D])
        self.fcp = dout("fcp", [L, NB, 2, DFF])
        self.dks = dout("dks", [L, DS, 512])
        self.dvs = dout("dvs", [L, DS, 512])
        self.gvs = dout("gvs", [L, DS, 512])
        self.fcs = dout("fcs", [L, 2, DFF])

        ARENA_BYTES = 212800
        self.arena = nc.alloc_sbuf_tensor("arena", [128, ARENA_BYTES // 2], BF16)
        self.ps = nc.alloc_psum_tensor("ps", [128, 8, 512], F32)
        self.X_OFF = 0
        self.HB_OFF = 65536
        self.KT_OFF = 98304
        self.VA_OFF = 114688
        self.W_OFF = 131328
        self.C_OFF = 188672
        self.cast_tok = {}
        self.ps_rr = {"mm": 0, "pair": 0, "acc": 0}
        self.lctr = 0
        self.build()

    def V(self, off, shape, dt, P=128):
        n = int(np.prod(shape)) * _esize(dt)
        assert off % 4 == 0
        v = self.arena[0:P, off // 2:(off + n) // 2]
        if dt != BF16:
            v = v.bitcast(dt)
        if len(shape) == 2:
            v = v.rearrange("p (a b) -> p a b", a=shape[0])
        elif len(shape) == 3:
            v = v.rearrange("p (a b c) -> p a b c", a=shape[0], b=shape[1])
        return v

    def wslot(self, i, shape):
        return self.V(self.W_OFF + i * 8192, shape, BF16)

    def ps_mm(self):
        i = self.ps_rr["mm"]
        self.ps_rr["mm"] = (i + 1) % 4
        return self.ps[:, i, :]

    def ps_pair(self):
        i = self.ps_rr["pair"]
        self.ps_rr["pair"] = (i + 1) % 2
        return self.ps[:, 2 * i:2 * i + 2, :]

    def ps_tr(self, n, w):
        return self.ps[:, 7, 0:(n * w) // 2].bitcast(BF16).rearrange("p (a b) -> p a b", a=n)

    def rstd(self, out, ss, dim, P):
        kb = self.kb
        kb.act(out, ss, AF.Ln, scale=1.0 / dim, bias=EPS)
        kb.act(out, out, AF.Exp, scale=-0.5)

    def build(self):
        kb = self.kb
        c = self.C_OFF
        self.identf = self.V(c, [128], F32); c += 512
        self.identb = self.V(c, [128], BF16); c += 256
        self.tril = self.V(c, [128], F32); c += 512
        self.csp = self.V(c, [16, 16], F32); c += 1024
        self.css = self.V(c, [16], F32); c += 64
        self.lslot = []
        for i in range(2):
            d = {}
            d["rep"] = self.V(c, [NREP], F32); c += NREP * 4
            d["fm"] = self.V(c, [NFM], F32); c += NFM * 4
            d["wsT"] = self.V(c, [4, 128], BF16); c += 1024
            d["lam"] = self.V(c, [16], F32); c += 64
            self.lslot.append(d)
        self.mkT = self.V(c, [8, MEM], BF16); c += 4096
        self.MVA = self.V(c, [2, 4, 258], BF16); c += 4128
        self.stat = self.V(c, [128], F32); c += 512
        assert c <= 212800, c

        kb.dma("sp", self.identf, self.ident_d[:, :])
        kb.dma("sp", self.tril, self.tril_d[:, :])
        kb.dma("sp", self.csp, self.csp_d[:, :, :])
        kb.dma("sp", self.css[0:DS], self.css_d[:, :])
        kb.cp("dve", self.identb, self.identf)
        kb.memset("dve", self.MVA[:, :, :, 256:258], 1.0)

        for l in range(L):
            for name in ["wk", "wv", "w_in", "w_out", "wq", "wo", "w_up", "w_down"]:
                src = self.wf[name]
                dst = self.wb[name]
                rows = src.shape[1]
                toks = []
                for r0 in range(0, rows, 128):
                    toks.append(kb.dma("pool", dst[l, r0:r0 + 128, :], src[l, r0:r0 + 128, :]))
                self.cast_tok[(name, l)] = toks

        import os
        self.dbg = int(os.environ.get("K_DBG", "9"))
        self.dbg_nl = int(os.environ.get("K_NL", str(L)))
        seqs = [Seq("p", b) for b in range(NB)] + [Seq("s", 0)]
        sel = os.environ.get("K_SEQS")
        if sel is not None:
            seqs = [seqs[int(i)] for i in sel.split(",")]
        if os.environ.get("K_NOCAST"):
            pass
        self.va_ones_done = None
        for sq in seqs:
            self.run_seq(sq)
        kb.finish()

    def load_panel(self, slot, name, l, rows, cols, shape):
        src = self.wb[name][l, rows[0]:rows[1], cols[0]:cols[1]].rearrange("(k p) n -> p k n", p=128)
        dst = self.wslot(slot, shape)
        self.kb.dma("sp", dst, src, extra_wait=self.cast_tok[(name, l)])
        return dst

    def run_seq(self, sq):
        kb = self.kb
        TP, NT = sq.TP, sq.NT
        self.X = self.V(self.X_OFF, [16, D], F32)
        for t in range(NT):
            if sq.kind == "p":
                kb.dma("sp", self.X[:, t, :], self.xp[sq.b, t * 128:(t + 1) * 128, :])
            else:
                kb.dma("sp", self.X[0:TP, 0, :], self.xs[:, :])
        self.KT = self.V(self.KT_OFF, [4, sq.SK], BF16)
        self.VA = self.V(self.VA_OFF, [sq.NKT, 4, 130], BF16)
        for l in range(self.dbg_nl):
            self.run_layer(sq, l)
        for t in range(NT):
            if sq.kind == "p":
                kb.dma("sp", self.yp[sq.b, t * 128:(t + 1) * 128, :], self.X[:, t, :])
            else:
                kb.dma("sp", self.ys[:, :], self.X[0:TP, 0, :])

    def norm_transpose(self, sq, xrow, gcol, outT, xn, sqr, P, sc=0):
        kb = self.kb
        ss = self.stat[0:P, sc:sc + 1]
        rs = self.stat[0:P, sc + 1:sc + 2]
        kb.act(sqr, xrow, AF.Square, accum_out=ss)
        self.rstd(rs, ss, D, P)
        kb.ts("dve", xn, xrow, rs, ALU.mult)
        pt = self.ps_tr(8, P)
        for kc in range(8):
            kb.tr(pt[:, kc, :], xn[:, kc * 128:(kc + 1) * 128], self.identb[0:P, 0:P])
        kb.tt("dve", outT, pt, gcol.unsqueeze(2).broadcast_to([128, 8, P]), ALU.mult)

    def run_layer(self, sq, l):
        kb = self.kb
        TP, NT = sq.TP, sq.NT
        isp = sq.kind == "p"
        slot = self.lslot[self.lctr % 2]
        self.lctr += 1
        lam_init = 0.8 - 0.6 * float(np.exp(-0.3 * l))
        HB = self.HB_OFF

        kb.dma("sp", slot["rep"], self.rep[l, :, :])
        kb.dma("sp", slot["fm"], self.fm[l, :, :])
        wst_f = self.V(HB, [4, 128], F32)
        kb.dma("sp", wst_f, self.wst[l, :, :].rearrange("p (g t) -> p g t", g=4))
        rep = slot["rep"]
        g_qk = rep[:, 0:128].rearrange("p (a d) -> p a d", a=2)
        g_sub = rep[:, 128:256]
        g_gmn = rep[:, 256:384]
        g_xq = rep[:, 384:640]
        g_xk = rep[:, 640:896]
        lamv = rep[:, 896:1152].rearrange("p (a d) -> p a d", a=4)
        fmv = slot["fm"]
        gfm = fmv[:, 0:32].rearrange("p (a k) -> p a k", a=4)
        gm_b = fmv[:, 32:36]
        conv_w = fmv[:, 36:102].rearrange("p (f j) -> p f j", f=NFC)
        conv_b = fmv[:, 102:124]
        lam = slot["lam"]
        kb.tt("dve", slot["wsT"], wst_f, self.tril.unsqueeze(1).broadcast_to([128, 4, 128]), ALU.mult)
        prod = self.V(HB + 2048, [2, 64], F32)
        kb.tt("dve", prod[:, 0, :], lamv[:, 0, :], lamv[:, 1, :], ALU.mult)
        kb.tt("dve", prod[:, 1, :], lamv[:, 2, :], lamv[:, 3, :], ALU.mult)
        kb.reduce_add(lam[:, 0:2], prod)
        kb.act(lam[:, 0:2], lam[:, 0:2], AF.Exp)
        kb.tt("dve", lam[:, 2:3], lam[:, 1:2], lam[:, 0:1], ALU.subtract)
        kb.ts("dve", lam[:, 3:4], lam[:, 2:3], -lam_init, ALU.add)
        kb.ts("dve", g_sub, g_sub, 1.0 - lam_init, ALU.mult)

        if isp:
            wk_p = [self.load_panel(3 + i, "wk", l, (0, D), (i * 512, (i + 1) * 512), [8, 512]) for i in range(2)]
            wv_p = [self.load_panel(5 + i, "wv", l, (0, D), (i * 512, (i + 1) * 512), [8, 512]) for i in range(2)]
        win = [None] * 5
        for i in range(3):
            win[i] = self.load_panel(i, "w_in", l, (0, D), (i * 512, (i + 1) * 512), [8, 512])

        if self.dbg < 2:
            return
        self.stage_mem(sq, l, slot, gfm, g_xk, wk_p if isp else None, wv_p if isp else None)

        if self.dbg < 3:
            return
        for i in range(3, 5):
            win[i] = self.load_panel(i, "w_in", l, (0, D), (i * 512, (i + 1) * 512), [8, 512])
        woA = self.load_panel(5, "w_out", l, (0, 512), (0, D), [4, D])
        woB = self.load_panel(6, "w_out", l, (512, D), (0, D), [4, D])

        if not os.environ.get("K_NOMEMSET"):
            kb.memset("dve", self.VA[:, :, :, 128:130], 1.0)
        if not isp:
            self.load_past(sq, l)

        QT = self.V(HB + 21504, [4, 512], BF16)
        self.stage_a_norm(sq, 0, gfm)
        for blk in range(sq.NBLK):
            for ti in range(sq.TPB):
                t = blk * sq.TPB + ti
                nn = (lambda t=t: self.stage_a_norm(sq, t + 1, gfm)) if t + 1 < NT else None
                self.stage_a_tile(sq, l, t, ti, slot, gfm, g_qk, g_gmn, gm_b, win, woB, QT, nn)
            if blk == sq.NBLK - 1:
                wq_p = [self.load_panel(i, "wq", l, (0, D), (i * 512, (i + 1) * 512), [8, 512]) for i in range(2)]
                wo_p = [self.load_panel(2 + i, "wo", l, (0, D), (i * 512, (i + 1) * 512), [8, 512]) for i in range(2)]
            self.stage_b_block(sq, l, blk, slot, g_sub, lam, woA, QT)

        if self.dbg < 4:
            return
        fpan = {}
        fpan[0] = self.load_ffn_group(l, 0, 4)

        self.stage_c(sq, l, gfm, g_xq, wq_p, wo_p)

        if self.dbg < 5:
            return
        fpan[1] = self.load_ffn_group(l, 1, 0)

        self.stage_ffn(sq, l, conv_w, conv_b, fpan)

    def load_ffn_group(self, l, g, s0):
        nf = 4 if g < 5 else 2
        c0 = g * 512
        gate = self.load_panel(s0, "w_up", l, (0, D), (c0, c0 + nf * 128), [8, nf * 128])
        up = self.load_panel(s0 + 1, "w_up", l, (0, D), (DFF + c0, DFF + c0 + nf * 128), [8, nf * 128])
        down = self.load_panel(s0 + 2, "w_down", l, (c0, c0 + nf * 128), (0, D), [nf, D])
        return gate, up, down, nf

    def stage_mem(self, sq, l, slot, gfm, g_xk, wk_p, wv_p):
        kb = self.kb
        HB = self.HB_OFF
        isp = sq.kind == "p"
        memt = self.V(HB + 4096, [D], F32)
        xn = self.V(HB + 8192, [D], BF16)
        mT = self.V(HB + 10240, [8, 128], BF16)
        kst = self.V(HB + 12288, [D], F32)
        kbf = self.V(HB + 16384, [D], BF16)
        vst = self.V(HB + 18432, [D], F32)
        sqr = self.V(HB + 22528, [D], F32)
        ss4 = self.stat[:, 4:8]
        rs4 = self.stat[:, 8:12]
        for mt in range(2):
            if isp:
                kb.dma("sp", memt, self.memp[sq.b, mt * 128:(mt + 1) * 128, :])
                self.norm_transpose(sq, memt, gfm[:, 3, :], mT, xn, sqr, 128)
                pk = self.ps_pair()
                for cb in range(2):
                    for kc in range(8):
                        kb.mm(pk[:, cb, :], lhsT=mT[:, kc, :], rhs=wk_p[cb][:, kc, :], start=(kc == 0), stop=(kc == 7))
                pk2 = pk.rearrange("p a b -> p (a b)")
                kb.act(sqr, pk2, AF.Square)
                kb.reduce_add(ss4, sqr.rearrange("p (h d) -> p h d", h=4))
                self.rstd(rs4, ss4, 256, 128)
                k3 = kst.rearrange("p (h d) -> p h d", h=4)
                kb.tt("dve", k3, pk2.rearrange("p (h d) -> p h d", h=4),
                      rs4.unsqueeze(2).broadcast_to([128, 4, 256]), ALU.mult)
                kb.tt("dve", k3, k3, g_xk.unsqueeze(1).broadcast_to([128, 4, 256]), ALU.mult)
                kb.dma("sp", self.mkp[l, sq.b, mt * 128:(mt + 1) * 128, :], kst)
                kb.cp("act", kbf, kst)
                pv = self.ps_pair()
                for cb in range(2):
                    for kc in range(8):
                        kb.mm(pv[:, cb, :], lhsT=mT[:, kc, :], rhs=wv_p[cb][:, kc, :], start=(kc == 0), stop=(kc == 7))
                pv2 = pv.rearrange("p a b -> p (a b)")
                kb.cp("act", vst, pv2)
                kb.dma("sp", self.mvp[l, sq.b, mt * 128:(mt + 1) * 128, :], vst)
                kb.cp("dve", self.MVA[:, mt, :, 0:256], vst.rearrange("p (h d) -> p h d", h=4))
            else:
                kb.dma("sp", kst, self.cmk[l, mt * 128:(mt + 1) * 128, :])
                kb.cp("act", kbf, kst)
                kb.dma("sp", vst, self.cmv[l, mt * 128:(mt + 1) * 128, :])
                kb.cp("dve", self.MVA[:, mt, :, 0:256], vst.rearrange("p (h d) -> p h d", h=4))
            pt = self.ps_tr(8, 128)
            for kc in range(8):
                kb.tr(pt[:, kc, :], kbf[:, kc * 128:(kc + 1) * 128], self.identb)
            kb.cp("dve", self.mkT[:, :, mt * 128:(mt + 1) * 128], pt)

    def load_past(self, sq, l):
        kb = self.kb
        HB = self.HB_OFF
        st = self.V(HB + 4096, [512], F32)
        sb = self.V(HB + 6144, [512], BF16)
        for kt in range(8):
            kb.dma("sp", st, self.ck[l, kt * 128:(kt + 1) * 128, :])
            kb.cp("act", sb, st)
            pt = self.ps_tr(4, 128)
            for h in range(4):
                kb.tr(pt[:, h, :], sb[:, h * 128:(h + 1) * 128], self.identb)
            kb.cp("dve", self.KT[:, :, kt * 128:(kt + 1) * 128], pt)
            st2 = self.V(HB + 8192, [512], F32)
            kb.dma("sp", st2, self.cv[l, kt * 128:(kt + 1) * 128, :])
            kb.cp("dve", self.VA[:, kt, :, 0:128], st2.rearrange("p (h d) -> p h d", h=4))

    def a_bufs(self, sq):
        P = sq.TP
        HB = self.HB_OFF
        d = {}
        d["sqr"] = self.V(HB, [D], F32, P)
        d["zqk"] = self.V(HB + 4096, [D], F32, P)
        d["zv"] = self.V(HB + 8192, [512], F32, P)
        d["qkb"] = self.V(HB + 10240, [D], BF16, P)
        d["xn"] = self.V(HB + 12288, [D], BF16, P)
        d["gvb"] = self.V(HB + 12288, [512], BF16, P)
        d["gob"] = self.V(HB + 13312, [512], BF16, P)
        d["rope"] = self.V(HB + 14336, [4, 128], F32, P)
        d["goT"] = self.V(HB + 16384, [4, P], BF16)
        d["hT"] = [self.V(HB + 17408 + i * 2048, [8, P], BF16) for i in range(2)]
        return d

    def stage_a_norm(self, sq, t, gfm):
        bf = self.a_bufs(sq)
        P = sq.TP
        self.norm_transpose(sq, self.X[0:P, t, :], gfm[:, 0, :], bf["hT"][t % 2], bf["xn"], bf["sqr"], P)

    def stage_a_tile(self, sq, l, t, ti, slot, gfm, g_qk, g_gmn, gm_b, win, woB, QT, next_norm):
        kb = self.kb
        P = sq.TP
        isp = sq.kind == "p"
        bf = self.a_bufs(sq)
        sqr, zqk, zv, qkb, gvb, gob, rope, goT = (bf[k] for k in ["sqr", "zqk", "zv", "qkb", "gvb", "gob", "rope", "goT"])
        hT = bf["hT"][t % 2]
        cs = self.csp[:, t, :] if isp else self.css[0:P, :]
        cosv = cs[:, 0:8]
        sinv = cs[:, 8:16]
        xrow = self.X[0:P, t, :]

        pqk = self.ps_pair()
        for which in range(2):
            for kc in range(8):
                kb.mm(pqk[0:P, which, :], lhsT=hT[:, kc, :], rhs=win[which][:, kc, :], start=(kc == 0), stop=(kc == 7))
        pz3 = []
        for i, cg in enumerate((2, 3, 4)):
            pz = self.ps[0:P, 4 + i, :]
            for kc in range(8):
                kb.mm(pz, lhsT=hT[:, kc, :], rhs=win[cg][:, kc, :], start=(kc == 0), stop=(kc == 7))
            pz3.append(pz)
        pv_, pu, pg = pz3
        if next_norm is not None:
            next_norm()

        ss16 = self.stat[0:P, 16:32]
        rs16 = self.stat[0:P, 96:112]
        pqk2 = pqk[0:P].rearrange("p a b -> p (a b)")
        kb.act(sqr, pqk2, AF.Square)
        kb.reduce_add(ss16, sqr.rearrange("p (g d) -> p g d", g=16))
        self.rstd(rs16, ss16, 64, P)
        d3 = zqk.rearrange("p (g d) -> p g d", g=16)
        kb.tt("dve", d3, pqk2.rearrange("p (g d) -> p g d", g=16), rs16.unsqueeze(2).broadcast_to([P, 16, 64]), ALU.mult)
        d4 = zqk.rearrange("p (a g d) -> p a g d", a=2, g=8)
        kb.tt("dve", d4, d4, g_qk[0:P].unsqueeze(2).broadcast_to([P, 2, 8, 64]), ALU.mult)
        x1 = d3[:, :, 0:8]
        x2 = d3[:, :, 8:16]
        cb_ = cosv.unsqueeze(1).broadcast_to([P, 16, 8])
        sb_ = sinv.unsqueeze(1).broadcast_to([P, 16, 8])
        r3 = rope.rearrange("p a (g d) -> p a g d", g=16)
        kb.tt("dve", r3[:, 0], x1, cb_, ALU.mult)
        kb.tt("dve", r3[:, 1], x2, sb_, ALU.mult)
        kb.tt("dve", r3[:, 2], x2, cb_, ALU.mult)
        kb.tt("dve", r3[:, 3], x1, sb_, ALU.mult)
        kb.tt("dve", x1, r3[:, 0], r3[:, 1], ALU.subtract)
        kb.tt("dve", x2, r3[:, 2], r3[:, 3], ALU.add)
        kb.cp("act", qkb, zqk)
        zk = zqk[:, 512:1024]
        if isp:
            kb.dma("sp", self.dkp[l, sq.b, t * 128:(t + 1) * 128, :], zk)
        else:
            kb.dma("sp", self.dks[l, :, :], zk)
        pt = self.ps_tr(8, P)
        for j in range(8):
            kb.tr(pt[:, j, :], qkb[:, j * 128:(j + 1) * 128], self.identb[0:P, 0:P])
        kb.cp("dve", QT[:, :, ti * 128:ti * 128 + P], pt[:, 0:4, :])
        kpos = t * 128 if isp else PAST
        kb.cp("act", self.KT[:, :, kpos:kpos + P], pt[:, 4:8, :])
        kb.cp("act", zv, pv_)
        if isp:
            kb.dma("sp", self.dvp[l, sq.b, t * 128:(t + 1) * 128, :], zv)
        else:
            kb.dma("sp", self.dvs[l, :, :], zv)
        vt = t if isp else 8
        kb.cp("dve", self.VA[0:P, vt, :, 0:128], zv.rearrange("p (h d) -> p h d", h=4))
        ug = zqk[:, 0:512]
        kb.act(ug, pu, AF.Gelu_apprx_tanh)
        gg = sqr[:, 0:512]
        kb.act(gg, pg, AF.Gelu_apprx_tanh)
        sq2 = sqr[:, 512:1024]
        kb.act(sq2, gg, AF.Square)
        ss4 = self.stat[0:P, 4:8]
        rs4 = self.stat[0:P, 8:12]
        kb.reduce_add(ss4, sq2.rearrange("p (g d) -> p g d", g=4))
        self.rstd(rs4, ss4, 128, P)
        gg3 = gg.rearrange("p (g d) -> p g d", g=4)
        kb.tt("dve", gg3, gg3, rs4.unsqueeze(2).broadcast_to([P, 4, 128]), ALU.mult)
        kb.tt("dve", gg3, gg3, g_gmn[0:P, :].unsqueeze(1).broadcast_to([P, 4, 128]), ALU.mult)
        if not isp:
            kb.dma("sp", self.gvs[l, :, :], gg)
        kb.cp("act", gvb, gg)
        pgate = self.ps_mm()[0:P, :]
        for g in range(4):
            kb.mm(pgate[:, g * 128:(g + 1) * 128], lhsT=slot["wsT"][0:P, g, 0:P], rhs=gvb[:, g * 128:(g + 1) * 128])
        go3 = sq2.rearrange("p (g d) -> p g d", g=4)
        kb.tt("dve", go3, pgate.rearrange("p (g d) -> p g d", g=4),
              gm_b[0:P, :].unsqueeze(2).broadcast_to([P, 4, 128]), ALU.add)
        kb.tt("dve", gob, sq2, ug, ALU.mult)
        pt2 = self.ps_tr(4, P)
        for j in range(4):
            kb.tr(pt2[:, j, :], gob[:, j * 128:(j + 1) * 128], self.identb[0:P, 0:P])
        kb.cp("act", goT, pt2)
        py = self.ps_pair()
        for cb in range(2):
            for kc in range(4):
                kb.mm(py[0:P, cb, :], lhsT=goT[:, kc, :], rhs=woB[:, kc, cb * 512:(cb + 1) * 512],
                      start=(kc == 0), stop=(kc == 3))
        kb.tt("dve", xrow, py[0:P].rearrange("p a b -> p (a b)"), xrow, ALU.add)

    def stage_b_block(self, sq, l, blk, slot, g_sub, lam, woA, QT):
        kb = self.kb
        HB = self.HB_OFF
        isp = sq.kind == "p"
        P = sq.TP
        QB = sq.QB
        eTs = [self.V(HB + 25600 + i * 2048, [2, 512], BF16) for i in range(2)]
        OB = self.V(HB + 25600, [4, 512], BF16)
        otmp = self.V(HB + 29696, [128], F32)
        oT = self.V(HB + 30208, [4, 128], BF16)
        OF = self.V(HB, [4, 512], F32)
        osq = self.V(HB + 8192, [512], F32)
        accS = self.V(HB + 10240, [3, 480], F32)
        nqs = sq.TPB
        if isp:
            nkt = 4 * blk + 4
        else:
            nkt = 9
        rz = self.stat[0:P, 32:48]
        ssq = self.stat[0:P, 48:49]
        rso = self.stat[0:P, 49:50]
        nl = self.stat[0:P, 50:51]
        def acc(qs, m):
            r = qs * 2 + m
            return self.ps[0:P, 4 + r // 3, (r % 3) * 160:(r % 3) * 160 + 129]

        def accs(qs, m):
            r = qs * 2 + m
            return accS[0:P, r // 3, (r % 3) * 160:(r % 3) * 160 + 129]

        steps = [(h, kt) for h in range(4) for kt in range(nkt)]

        def qk_exp(i):
            h, kt = steps[i]
            KP = 128 if (isp or kt < 8) else DS
            j = kt - 4 * blk if isp else -1
            q0 = max(0, j) * 128
            kpos = kt * 128
            psS = self.ps_pair()
            for m in range(2):
                kb.mm(psS[0:KP, m, q0:QB], lhsT=self.KT[m * 64:(m + 1) * 64, h, kpos:kpos + KP],
                      rhs=QT[m * 64:(m + 1) * 64, h, q0:QB])
            eT = eTs[i % 2]
            kb.act(eT[0:KP, :, q0:QB], psS[0:KP, :, q0:QB], AF.Exp, scale=0.125)
            if j >= 0:
                kb.memset("dve", eT[64:128, :, q0:q0 + 64], 0.0)

        def pv(i):
            h, kt = steps[i]
            KP = 128 if (isp or kt < 8) else DS
            j = kt - 4 * blk if isp else -1
            eT = eTs[i % 2]
            for qs in range(max(0, j), nqs):
                last = (kt == 4 * blk + qs) if isp else (kt == nkt - 1)
                for m in range(2):
                    kb.mm(acc(qs, m), lhsT=eT[0:KP, m, qs * 128:qs * 128 + P], rhs=self.VA[0:KP, kt, h, 0:129],
                          start=(kt == 0 and (qs * 2 + m) % 3 == 0), stop=last)
            if kt == nkt - 1:
                nreg = 2 * nqs
                for bk in range((nreg + 2) // 3):
                    w_ = min(3, nreg - 3 * bk) * 160
                    kb.cp("act", accS[0:P, bk, 0:w_], self.ps[0:P, 4 + bk, 0:w_])
                for qs in range(nqs):
                    a0 = accs(qs, 0)
                    a1 = accs(qs, 1)
                    kb.recip(rz[:, 0:1], a0[:, 128:129])
                    kb.recip(rz[:, 1:2], a1[:, 128:129])
                    kb.tt("dve", nl, rz[:, 1:2], lam[0:P, 3:4], ALU.mult)
                    ot = otmp[0:P, :]
                    kb.ts("dve", ot, a0[:, 0:128], rz[:, 0:1], ALU.mult)
                    kb.stt("dve", OF[0:P, qs, h * 128:(h + 1) * 128], a1[:, 0:128], nl, ot, ALU.mult, ALU.add)

        qk_exp(0)
        for i in range(len(steps)):
            if i + 1 < len(steps):
                qk_exp(i + 1)
            pv(i)
        ss4 = self.stat[0:P, 4:8]
        rs4 = self.stat[0:P, 8:12]
        for qs in range(nqs):
            t = blk * sq.TPB + qs
            of = OF[0:P, qs, :]
            kb.act(osq[0:P, :], of, AF.Square)
            kb.reduce_add(ss4, osq[0:P, :].rearrange("p (h d) -> p h d", h=4))
            self.rstd(rs4, ss4, 128, P)
            of3 = of.rearrange("p (h d) -> p h d", h=4)
            kb.tt("dve", of3, of3, rs4.unsqueeze(2).broadcast_to([P, 4, 128]), ALU.mult)
            kb.tt("dve", OB[0:P, qs, :].rearrange("p (h d) -> p h d", h=4), of3,
                  g_sub[0:P, :].unsqueeze(1).broadcast_to([P, 4, 128]), ALU.mult)
            pt = self.ps_tr(4, P)
            for j in range(4):
                kb.tr(pt[:, j, :], OB[0:P, qs, j * 128:(j + 1) * 128], self.identb[0:P, 0:P])
            kb.cp("act", oT[:, :, 0:P], pt)
            py = self.ps_pair()
            for cb in range(2):
                for kc in range(4):
                    kb.mm(py[0:P, cb, :], lhsT=oT[:, kc, 0:P], rhs=woA[:, kc, cb * 512:(cb + 1) * 512],
                          start=(kc == 0), stop=(kc == 3))
            xrow = self.X[0:P, t, :]
            kb.tt("dve", xrow, py[0:P].rearrange("p a b -> p (a b)"), xrow, ALU.add)

    def stage_c(self, sq, l, gfm, g_xq, wq_p, wo_p):
        kb = self.kb
        P = sq.TP
        NT = sq.NT
        o = self.KT_OFF
        xn1 = self.V(o, [D], BF16, P); o += 2048
        sqj = self.V(o, [D], F32, P); o += 4096
        h2Ts = [self.V(o + i * 2048, [8, P], BF16) for i in range(2)]; o += 4096
        sqq = self.V(o, [D], F32, P); o += 4096
        qcn = self.V(o, [D], BF16, P); o += 2048
        qcTs = [self.V(o + i * 2048, [8, P], BF16) for i in range(2)]; o += 4096
        eTs = [self.V(o + i * 2048, [4, 2, P], BF16) for i in range(2)]; o += 4096
        ocb = self.V(o, [D], BF16, P); o += 2048
        ocTs = [self.V(o + i * 2048, [8, P], BF16) for i in range(2)]; o += 4096
        xn3 = self.V(o, [D], BF16, P); o += 2048
        assert o <= self.KT_OFF + 33024, o
        ss4 = self.stat[0:P, 4:8]
        rs4 = self.stat[0:P, 8:12]
        rz4 = self.stat[0:P, 68:72]
        HBv = self.V(self.HB_OFF, [8, sq.S], BF16)

        def c1a(t):
            self.norm_transpose(sq, self.X[0:P, t, :], gfm[:, 1, :], h2Ts[t % 2], xn1, sqj, P)

        def c1b(t):
            h2T = h2Ts[t % 2]
            qcT = qcTs[t % 2]
            pq = self.ps_pair()
            for cb in range(2):
                for kc in range(8):
                    kb.mm(pq[0:P, cb, :], lhsT=h2T[:, kc, :], rhs=wq_p[cb][:, kc, :], start=(kc == 0), stop=(kc == 7))
            pq2 = pq[0:P].rearrange("p a b -> p (a b)")
            kb.act(sqq, pq2, AF.Square)
            kb.reduce_add(ss4, sqq.rearrange("p (h d) -> p h d", h=4))
            self.rstd(rs4, ss4, 256, P)
            s3 = sqq.rearrange("p (h d) -> p h d", h=4)
            kb.tt("dve", s3, pq2.rearrange("p (h d) -> p h d", h=4), rs4.unsqueeze(2).broadcast_to([P, 4, 256]), ALU.mult)
            kb.tt("dve", qcn.rearrange("p (h d) -> p h d", h=4), s3, g_xq[0:P, :].unsqueeze(1).broadcast_to([P, 4, 256]),
                  ALU.mult)
            pt = self.ps_tr(8, P)
            for j in range(8):
                kb.tr(pt[:, j, :], qcn[:, j * 128:(j + 1) * 128], self.identb[0:P, 0:P])
            kb.cp("act", qcT, pt)

        def c2a(t):
            qcT = qcTs[t % 2]
            eT = eTs[t % 2]
            psS = self.ps_pair()
            for h in range(4):
                for mt in range(2):
                    c0 = (h * 2 + mt) * 128
                    dst = psS[:, c0 // 512, (c0 % 512):(c0 % 512) + P]
                    for dc in range(2):
                        kb.mm(dst, lhsT=self.mkT[:, h * 2 + dc, mt * 128:(mt + 1) * 128], rhs=qcT[:, h * 2 + dc, :],
                              start=(dc == 0), stop=(dc == 1))
            for half in range(2):
                kb.act(eT[:, half * 2:half * 2 + 2, :, :],
                       psS[:, half, :].rearrange("p (h m q) -> p h m q", h=2, m=2)[:, :, :, 0:P], AF.Exp, scale=1.0 / 16.0)

        def c2b(t):
            eT = eTs[t % 2]
            ocT = ocTs[t % 2]
            for h in range(4):
                po = self.ps[0:P, 4 + (h % 3), 0:257]
                for mt in range(2):
                    kb.mm(po, lhsT=eT[:, h, mt, :], rhs=self.MVA[:, mt, h, 0:257], start=(mt == 0), stop=(mt == 1))
                kb.recip(rz4[:, h:h + 1], po[:, 256:257])
                kb.ts("dve", ocb[:, h * 256:(h + 1) * 256], po[:, 0:256], rz4[:, h:h + 1], ALU.mult)
            pt2 = self.ps_tr(8, P)
            for j in range(8):
                kb.tr(pt2[:, j, :], ocb[:, j * 128:(j + 1) * 128], self.identb[0:P, 0:P])
            kb.cp("act", ocT, pt2)

        def c3a(t):
            xrow = self.X[0:P, t, :]
            ocT = ocTs[t % 2]
            py = self.ps_pair()
            for cb in range(2):
                for kc in range(8):
                    kb.mm(py[0:P, cb, :], lhsT=ocT[:, kc, :], rhs=wo_p[cb][:, kc, :], start=(kc == 0), stop=(kc == 7))
            kb.tt("dve", xrow, py[0:P].rearrange("p a b -> p (a b)"), xrow, ALU.add)

        def c3b(t):
            self.norm_transpose(sq, self.X[0:P, t, :], gfm[:, 2, :], HBv[:, :, t * 128:t * 128 + P], xn3, sqj, P, sc=64)

        phases = [c1a, c1b, c2a, c2b, c3a, c3b]
        for i in range(NT + len(phases) - 1):
            for k, ph in enumerate(phases):
                t = i - k
                if 0 <= t < NT:
                    ph(t)

    def stage_ffn(self, sq, l, conv_w, conv_b, fpan):
        kb = self.kb
        isp = sq.kind == "p"
        P = sq.TP
        NTOK = sq.QB
        HBv = self.V(self.HB_OFF, [8, sq.S], BF16)
        o = self.KT_OFF
        gss = [self.V(o + i * 2064, [516], F32) for i in range(2)]; o += 4128
        ccs = [self.V(o + i * 2048, [512], F32) for i in range(2)]; o += 4096
        scs = [self.V(o + i * 2048, [512], F32) for i in range(2)]; o += 4096
        aTs = [self.V(o + i * 4096, [4, 512], BF16) for i in range(2)]; o += 8192
        carry = self.V(o, [NFC, 2], F32); o += 176
        cst = self.V(o, [512], F32, 2); o += 2048
        assert o <= self.KT_OFF + 33024
        if isp:
            kb.memset("dve", carry, 0.0)
        else:
            kb.dma("sp", carry, self.convst_d[:, l, :, :])
        ri = 0
        ai = 0
        pending = None
        for g in range(6):
            gate, up, down, nf = fpan[g]
            for blk in range(sq.NBLK):
                aT = aTs[ai % 2]
                ai += 1
                for fl in range(nf):
                    fc = g * 4 + fl
                    pg = self.ps_mm()
                    for kc in range(8):
                        kb.mm(pg[:, 0:NTOK], lhsT=gate[:, kc, fl * 128:(fl + 1) * 128],
                              rhs=HBv[:, kc, blk * 512:blk * 512 + NTOK], start=(kc == 0), stop=(kc == 7))
                    pu = self.ps_mm()
                    for kc in range(8):
                        kb.mm(pu[:, 0:NTOK], lhsT=up[:, kc, fl * 128:(fl + 1) * 128],
                              rhs=HBv[:, kc, blk * 512:blk * 512 + NTOK], start=(kc == 0), stop=(kc == 7))
                    gs = gss[ri % 2]
                    cc = ccs[ri % 2]
                    sc = scs[ri % 2]
                    ri += 1
                    kb.cp("act", gs[:, 2:2 + NTOK], pg[:, 0:NTOK])
                    kb.cp("dve", gs[:, 0:2], carry[:, fc, :])
                    kb.cp("dve", carry[:, fc, :], gs[:, NTOK:NTOK + 2])
                    kb.act(cc[:, 0:NTOK], pg[:, 0:NTOK], AF.Identity, bias=conv_b[:, fc:fc + 1], scale=conv_w[:, fc, 2:3])
                    kb.stt("dve", cc[:, 0:NTOK], gs[:, 1:1 + NTOK], conv_w[:, fc, 1:2], cc[:, 0:NTOK], ALU.mult, ALU.add)
                    kb.stt("dve", cc[:, 0:NTOK], gs[:, 0:NTOK], conv_w[:, fc, 0:1], cc[:, 0:NTOK], ALU.mult, ALU.add)
                    kb.act(sc[:, 0:NTOK], cc[:, 0:NTOK], AF.Silu)
                    kb.tt("dve", aT[:, fl, 0:NTOK], sc[:, 0:NTOK], pu[:, 0:NTOK], ALU.mult)
                def do_down(blk=blk, aT=aT, down=down, nf=nf):
                    for ti in range(sq.TPB):
                        t = blk * sq.TPB + ti
                        py = self.ps[0:P, 4 + 2 * (t % 2):6 + 2 * (t % 2), :]
                        for cb in range(2):
                            for fl in range(nf):
                                kb.mm(py[:, cb, :], lhsT=aT[:, fl, ti * 128:ti * 128 + P],
                                      rhs=down[:, fl, cb * 512:(cb + 1) * 512], start=(fl == 0), stop=(fl == nf - 1))
                        xrow = self.X[0:P, t, :]
                        kb.tt("dve", xrow, py.rearrange("p a b -> p (a b)"), xrow, ALU.add)
                if pending is not None:
                    pending()
                pending = do_down
            if pending is not None:
                pending()
                pending = None
            pc = self.ps[0:2, 3, :]
            for fl in range(nf):
                kb.tr(pc[:, fl * 128:(fl + 1) * 128], carry[:, g * 4 + fl, :], self.identf)
            kb.cp("act", cst[:, 0:nf * 128], pc[:, 0:nf * 128])
            dst = self.fcp[l, sq.b, :, g * 512:g * 512 + nf * 128] if isp else self.fcs[l, :, g * 512:g * 512 + nf * 128]
            kb.dma("sp", dst, cst[:, 0:nf * 128])
            if g + 2 < 6:
                fpan[g + 2] = self.load_ffn_group(l, g + 2, 4 if (g % 2 == 0) else 0)


_PROG = None


def _get_prog():
    global _PROG
    if _PROG is None:
        _PROG = Prog()
    return _PROG


def _host_consts(inp):
    f32 = np.float32
    rep = np.zeros((L, 128, NREP), f32)
    fm = np.zeros((L, 128, NFM), f32)
    for l in range(L):
        v = np.concatenate([inp["da_q_norm_g"][l], inp["da_k_norm_g"][l], inp["da_subln_g"][l], inp["gm_norm_g"][l],
                            inp["xq_norm_g"][l], inp["xk_norm_g"][l], inp["lambda_q1"][l], inp["lambda_k1"][l],
                            inp["lambda_q2"][l], inp["lambda_k2"][l]]).astype(f32)
        rep[l] = np.broadcast_to(v[None, :], (128, NREP))
        for i, nm in enumerate(["norm_mix_g", "norm_x_g", "norm_ffn_g", "norm_mem_g"]):
            fm[l, :, i * 8:(i + 1) * 8] = inp[nm][l].reshape(8, 128).T
        fm[l, :, 32:36] = inp["gm_b"][l].T
        cw = inp["conv_w"][l].reshape(3, NFC, 128)
        fm[l, :, 36:102] = cw.transpose(2, 1, 0).reshape(128, NFC * 3)
        fm[l, :, 102:124] = inp["conv_b"][l].reshape(NFC, 128).T
    wst = np.ascontiguousarray(inp["gm_w_s"].transpose(0, 3, 1, 2)).reshape(L, 128, 512).astype(f32)
    ident = np.eye(128, dtype=f32)
    tril = np.triu(np.ones((128, 128), f32))
    half = 8
    inv = (np.float32(500000.0) ** (-np.arange(half, dtype=f32) / np.float32(half))).astype(f32)

    def cs(pos):
        ang = pos.astype(f32)[:, None] * inv[None, :]
        return np.concatenate([np.cos(ang), np.sin(ang)], axis=1).astype(f32)

    csp = cs(np.arange(S)).reshape(16, 128, 16).transpose(1, 0, 2)
    css = cs(PAST + np.arange(DS))
    return dict(rep=rep, fm=fm, wst=wst, ident=ident, tril=tril, csp=np.ascontiguousarray(csp), css=css)


def _in_maps(inp, cores):
    hc = _host_consts(inp)
    in_maps = []
    for c in cores:
        m = dict(hc)
        m["xp"] = np.ascontiguousarray(inp["x_prompt"][c * NB:(c + 1) * NB])
        m["xs"] = np.ascontiguousarray(inp["x_sample"][c])
        m["ck"] = np.ascontiguousarray(inp["cache_da_k"][:, c].reshape(L, PAST, 512))
        m["cv"] = np.ascontiguousarray(inp["cache_da_v"][:, c].reshape(L, PAST, 512))
        m["cmk"] = np.ascontiguousarray(inp["cache_mem_k"][:, c].reshape(L, MEM, D))
        m["cmv"] = np.ascontiguousarray(inp["cache_mem_v"][:, c].reshape(L, MEM, D))
        m["memp"] = np.ascontiguousarray(inp["mem_prompt"][c * NB:(c + 1) * NB])
        st = inp["state_ffn_conv"][:, c]
        m["convst"] = np.ascontiguousarray(st.reshape(L, 2, NFC, 128).transpose(3, 0, 2, 1))
        m["w_in"] = inp["w_in"]; m["w_out"] = inp["w_out"]
        m["wq"] = inp["wq_c"]; m["wk"] = inp["wk_c"]; m["wv"] = inp["wv_c"]; m["wo"] = inp["wo_c"]
        m["w_up"] = inp["w_up"]; m["w_down"] = inp["w_down"]
        in_maps.append(m)
    return in_maps


def kernel(**inp):
    inp = {k: np.asarray(v) for k, v in inp.items()}
    prog = _get_prog()
    n = 8
    in_maps = _in_maps(inp, range(n))
    res = run_bass_kernel_spmd(prog.nc, in_maps, core_ids=list(range(n))).results
    cat = lambda k, ax: np.concatenate([r[k] for r in res], axis=ax)
    y_p = cat("yp", 0)
    y_s = np.stack([r["ys"] for r in res], 0)
    dk_p = cat("dkp", 1).reshape(L, 32, S, 4, 2, 64)
    dv_p = cat("dvp", 1).reshape(L, 32, S, 4, 128)
    mk_p = cat("mkp", 1).reshape(L, 32, MEM, 4, 256)
    mv_p = cat("mvp", 1).reshape(L, 32, MEM, 4, 256)
    fc_p = cat("fcp", 1)
    dk_s = np.stack([r["dks"] for r in res], 1).reshape(L, 8, DS, 4, 2, 64)
    dv_s = np.stack([r["dvs"] for r in res], 1).reshape(L, 8, DS, 4, 128)
    gv_s = np.stack([r["gvs"] for r in res], 1).reshape(L, 8, DS, 4, 128)
    fc_s = np.stack([r["fcs"] for r in res], 1)
    return (y_p, y_s, dk_p, dv_p, mk_p, mv_p, fc_p, dk_s, dv_s, gv_s, fc_s)
```

```python
import os
import numpy as np
from contextlib import ExitStack
import concourse.bass as bass
import concourse.mybir as mybir
from concourse.bass_utils import run_bass_kernel_spmd

F32 = mybir.dt.float32
BF16 = mybir.dt.bfloat16
AF = mybir.ActivationFunctionType
ALU = mybir.AluOpType
AX = mybir.AxisListType
CELL = 256
SAME_ENGINE_SYNC = bool(int(os.environ.get("K_SES", "0")))
SMALL_T = int(os.environ.get("K_SMALL", "256"))
N_DMA_SEMS = 12

D = 1024
L = 4
NB = 4
S = 2048
DS = 32
PAST = 1024
MEM = 256
DFF = 2816
NFC = 22
EPS = 1e-6
NREP = 1152
NFM = 124


def _esize(dt):
    return 2 if dt == BF16 else 4


class KB:
    def __init__(self):
        self.nc = bass.Bass("TRN2", target_bir_lowering=False)
        nc = self.nc
        self.es = ExitStack()
        self.eng = {"pe": nc.tensor, "act": nc.scalar, "dve": nc.vector, "pool": nc.gpsimd, "sp": nc.sync}
        self.sems = {}
        self.sem_id = {}
        self.cnt = {}
        self._nsem = 0
        for e in ["pe", "act", "dve", "pool"]:
            self.sems[e] = self._newsem("c_" + e)
            self.cnt[e] = 0
        self.dq = {}
        for q in ["sp", "pool"]:
            pool = [self._newsem(f"d_{q}{i}") for i in range(N_DMA_SEMS)]
            self.dq[q] = {"sems": pool, "vals": [0] * N_DMA_SEMS, "next": 0}
        self.seen = {e: {} for e in ["pe", "act", "dve", "pool", "sp"]}
        self.cells = {}
        self.n_inst = 0
        self.n_wait = 0

    def _newsem(self, name):
        h = self.es.enter_context(self.nc.semaphore(name))
        k = self._nsem
        self._nsem += 1
        self.sem_id[k] = h
        return k

    def _cells(self, ap):
        sp = str(ap.space).upper()
        if "SB" not in sp and "PSUM" not in sp:
            return None
        es = _esize(ap.dtype)
        a = ap.ap
        pstep = a[0][0]
        off = ap.offset
        base = (off % pstep) * es if pstep > 0 else off * es
        region = ap.tensor.name
        cell = CELL if "SB" in sp else 2048
        dims = [(s * es, c) for (s, c) in a[1:]]
        if not dims:
            dims = [(es, 1)]
        ls, lc = dims[-1]
        run = (lc - 1) * ls + es
        starts = [base]
        for (s, c) in dims[:-1]:
            if s == 0 or c == 1:
                continue
            starts = [st + i * s for st in starts for i in range(c)]
        out = set()
        for st in starts:
            for c in range(st // cell, (st + run - 1) // cell + 1):
                out.add((region, c))
        fsz = 1
        for (s_, c_) in a[1:]:
            if s_ != 0:
                fsz *= c_
        self._last_fsz = fsz
        return out

    def _deps(self, reads, writes, own=None):
        deps = {}
        rc = set()
        wc = set()
        msize = 1 << 30
        for ap in reads:
            c = self._cells(ap)
            if c:
                msize = min(msize, self._last_fsz)
                if "PSUM" in str(ap.space).upper():
                    wc |= c
                else:
                    rc |= c
        for ap in writes:
            c = self._cells(ap)
            if c:
                msize = min(msize, self._last_fsz)
                wc |= c
        self._msize = msize
        cells = self.cells
        small = msize <= SMALL_T

        def add(tok):
            k, v, sz = tok
            if k == own and not SAME_ENGINE_SYNC and not (small or sz <= SMALL_T):
                return
            if deps.get(k, 0) < v:
                deps[k] = v
        for c in rc:
            st = cells.get(c)
            if st is not None and st[0] is not None:
                add(st[0])
        for c in wc:
            st = cells.get(c)
            if st is not None:
                if st[0] is not None:
                    add(st[0])
                for k, (v, sz) in st[1].items():
                    add((k, v, sz))
        return deps, rc, wc

    def _commit(self, tok, rc, wc):
        k, v = tok
        sz = self._msize
        cells = self.cells
        for c in rc:
            if c in wc:
                continue
            st = cells.get(c)
            if st is None:
                st = [None, {}]
                cells[c] = st
            old = st[1].get(k)
            if old is None or old[0] < v:
                st[1][k] = (v, sz)
        t3 = (k, v, sz)
        for c in wc:
            cells[c] = [t3, {}]

    def _waits(self, ename, deps):
        e = self.eng[ename]
        seen = self.seen[ename]
        own = self.sems.get(ename)
        for k, v in deps.items():
            if k == own and ename == "pe":
                continue
            if seen.get(k, 0) >= v:
                continue
            e.wait_ge(self.sem_id[k], v)
            seen[k] = v
            self.n_wait += 1
            if os.environ.get("K_TRACE"):
                print("   WAIT", ename, "sem", k, ">=", v)

    def I(self, ename, fn, reads, writes):
        deps, rc, wc = self._deps(reads, writes, self.sems.get(ename))
        self._waits(ename, deps)
        ins = fn()
        self.cnt[ename] += 1
        k = self.sems[ename]
        if os.environ.get("K_TRACE"):
            print("INS", ename, self.cnt[ename], str(ins)[:150])
        ins.then_inc(self.sem_id[k], 1)
        self._commit((k, self.cnt[ename]), rc, wc)
        self.n_inst += 1
        return ins

    def dma(self, q, out, in_, extra_wait=None):
        deps, rc, wc = self._deps([in_], [out])
        if extra_wait:
            for k, v in extra_wait:
                if deps.get(k, 0) < v:
                    deps[k] = v
        d = self.dq[q]
        i = d["next"]
        d["next"] = (i + 1) % N_DMA_SEMS
        k = d["sems"][i]
        if d["vals"][i] > 0:
            deps[k] = max(deps.get(k, 0), d["vals"][i])
        self._waits(q, deps)
        ins = self.eng[q].dma_start(out=out, in_=in_)
        d["vals"][i] += 16
        ins.then_inc(self.sem_id[k], 16)
        tok = (k, d["vals"][i])
        if os.environ.get("K_TRACE"):
            print("DMA", q, tok, str(ins)[:150])
        self._commit(tok, rc, wc)
        self.n_inst += 1
        return tok

    def finish(self):
        last = {}
        for q, d in self.dq.items():
            for k, v in zip(d["sems"], d["vals"]):
                if v > 0:
                    last[k] = v
        for e in ["pe", "act", "dve", "pool"]:
            if self.cnt[e] > 0:
                last[self.sems[e]] = self.cnt[e]
        self._waits("sp", last)

    def mm(self, out, lhsT, rhs, start=True, stop=True):
        return self.I("pe", lambda: self.nc.tensor.matmul(out, lhsT=lhsT, rhs=rhs, start=start, stop=stop),
                      [lhsT, rhs], [out])

    def tr(self, out, in_, ident):
        return self.I("pe", lambda: self.nc.tensor.transpose(out, in_, ident), [in_, ident], [out])

    def act(self, out, in_, func, bias=None, scale=1.0, accum_out=None):
        reads = [in_]
        kw = {}
        if bias is not None:
            kw["bias"] = bias
            if not isinstance(bias, (int, float)):
                reads.append(bias)
        if not isinstance(scale, (int, float)):
            reads.append(scale)
        writes = [out]
        if accum_out is not None:
            kw["accum_out"] = accum_out
            writes.append(accum_out)
        return self.I("act", lambda: self.nc.scalar.activation(out=out, in_=in_, func=func, scale=scale, **kw),
                      reads, writes)

    def tt(self, e, out, in0, in1, op):
        return self.I(e, lambda: self.eng[e].tensor_tensor(out=out, in0=in0, in1=in1, op=op), [in0, in1], [out])

    def ts(self, e, out, in0, s1, op0, s2=None, op1=None):
        reads = [in0]
        if not isinstance(s1, (int, float)):
            reads.append(s1)
        if s2 is not None and not isinstance(s2, (int, float)):
            reads.append(s2)
        kw = {}
        if op1 is not None:
            kw["op1"] = op1
        return self.I(e, lambda: self.eng[e].tensor_scalar(out=out, in0=in0, scalar1=s1, scalar2=s2, op0=op0, **kw),
                      reads, [out])

    def stt(self, e, out, in0, scalar, in1, op0, op1):
        reads = [in0, in1]
        if not isinstance(scalar, (int, float)):
            reads.append(scalar)
        return self.I(e, lambda: self.eng[e].scalar_tensor_tensor(out=out, in0=in0, scalar=scalar, in1=in1,
                                                                  op0=op0, op1=op1), reads, [out])

    def cp(self, e, out, in_):
        if e == "act":
            return self.I("act", lambda: self.nc.scalar.copy(out=out, in_=in_), [in_], [out])
        return self.I(e, lambda: self.eng[e].tensor_copy(out=out, in_=in_), [in_], [out])

    def memset(self, e, out, val):
        return self.I(e, lambda: self.eng[e].memset(out, val), [], [out])

    def recip(self, out, in_):
        return self.I("dve", lambda: self.nc.vector.reciprocal(out=out, in_=in_), [in_], [out])

    def reduce_add(self, out, in_):
        return self.I("dve", lambda: self.nc.vector.tensor_reduce(out=out, in_=in_, axis=AX.X, op=ALU.add),
                      [in_], [out])


class Seq:
    def __init__(self, kind, b):
        self.kind = kind
        self.b = b
        if kind == "p":
            self.S, self.TP, self.NT, self.QB, self.NBLK, self.TPB = S, 128, 16, 512, 4, 4
            self.SK, self.NKT = S, 16
        else:
            self.S, self.TP, self.NT, self.QB, self.NBLK, self.TPB = DS, DS, 1, DS, 1, 1
            self.SK, self.NKT = PAST + DS, 9


class Prog:
    def __init__(self):
        self.kb = KB()
        kb = self.kb
        nc = kb.nc
        self.nc = nc

        def din(name, shape, dt=F32):
            return nc.dram_tensor(name, list(shape), dt, kind="ExternalInput").ap()

        def dout(name, shape):
            return nc.dram_tensor(name, list(shape), F32, kind="ExternalOutput").ap()

        def dint(name, shape):
            return nc.dram_tensor(name, list(shape), BF16, kind="Internal").ap()

        self.xp = din("xp", [NB, S, D])
        self.xs = din("xs", [DS, D])
        self.ck = din("ck", [L, PAST, 512])
        self.cv = din("cv", [L, PAST, 512])
        self.cmk = din("cmk", [L, MEM, D])
        self.cmv = din("cmv", [L, MEM, D])
        self.memp = din("memp", [NB, MEM, D])
        self.wf = {
            "w_in": din("w_in", [L, D, 2560]), "w_out": din("w_out", [L, D, D]),
            "wq": din("wq", [L, D, D]), "wk": din("wk", [L, D, D]), "wv": din("wv", [L, D, D]),
            "wo": din("wo", [L, D, D]), "w_up": din("w_up", [L, D, 2 * DFF]), "w_down": din("w_down", [L, DFF, D]),
        }
        self.wb = {k: dint("b_" + k, v.shape) for k, v in self.wf.items()}
        self.rep = din("rep", [L, 128, NREP])
        self.fm = din("fm", [L, 128, NFM])
        self.wst = din("wst", [L, 128, 512])
        self.ident_d = din("ident", [128, 128])
        self.tril_d = din("tril", [128, 128])
        self.csp_d = din("csp", [128, 16, 16])
        self.css_d = din("css", [DS, 16])
        self.convst_d = din("convst", [128, L, NFC, 2])
        self.yp = dout("yp", [NB, S, D])
        self.ys = dout("ys", [DS, D])
        self.dkp = dout("dkp", [L, NB, S, 512])
        self.dvp = dout("dvp", [L, NB, S, 512])
        self.mkp = dout("mkp", [L, NB, MEM, D])
        self.mvp = dout("mvp", [L, NB, MEM, D])
        self.fcp = dout("fcp", [L, NB, 2, DFF])
        self.dks = dout("dks", [L, DS, 512])
        self.dvs = dout("dvs", [L, DS, 512])
        self.gvs = dout("gvs", [L, DS, 512])
        self.fcs = dout("fcs", [L, 2, DFF])

        ARENA_BYTES = 212800
        self.arena = nc.alloc_sbuf_tensor("arena", [128, ARENA_BYTES // 2], BF16)
        self.ps = nc.alloc_psum_tensor("ps", [128, 8, 512], F32)
        self.X_OFF = 0
        self.HB_OFF = 65536
        self.KT_OFF = 98304
        self.VA_OFF = 114688
        self.W_OFF = 131328
        self.C_OFF = 188672
        self.cast_tok = {}
        self.ps_rr = {"mm": 0, "pair": 0, "acc": 0}
        self.lctr = 0
        self.build()

    def V(self, off, shape, dt, P=128):
        n = int(np.prod(shape)) * _esize(dt)
        assert off % 4 == 0
        v = self.arena[0:P, off // 2:(off + n) // 2]
        if dt != BF16:
            v = v.bitcast(dt)
        if len(shape) == 2:
            v = v.rearrange("p (a b) -> p a b", a=shape[0])
        elif len(shape) == 3:
            v = v.rearrange("p (a b c) -> p a b c", a=shape[0], b=shape[1])
        return v

    def wslot(self, i, shape):
        return self.V(self.W_OFF + i * 8192, shape, BF16)

    def ps_mm(self):
        i = self.ps_rr["mm"]
        self.ps_rr["mm"] = (i + 1) % 4
        return self.ps[:, i, :]

    def ps_pair(self):
        i = self.ps_rr["pair"]
        self.ps_rr["pair"] = (i + 1) % 2
        return self.ps[:, 2 * i:2 * i + 2, :]

    def ps_tr(self, n, w):
        return self.ps[:, 7, 0:(n * w) // 2].bitcast(BF16).rearrange("p (a b) -> p a b", a=n)

    def rstd(self, out, ss, dim, P):
        kb = self.kb
        kb.act(out, ss, AF.Ln, scale=1.0 / dim, bias=EPS)
        kb.act(out, out, AF.Exp, scale=-0.5)

    def build(self):
        kb = self.kb
        c = self.C_OFF
        self.identf = self.V(c, [128], F32); c += 512
        self.identb = self.V(c, [128], BF16); c += 256
        self.tril = self.V(c, [128], F32); c += 512
        self.csp = self.V(c, [16, 16], F32); c += 1024
        self.css = self.V(c, [16], F32); c += 64
        self.lslot = []
        for i in range(2):
            d = {}
            d["rep"] = self.V(c, [NREP], F32); c += NREP * 4
            d["fm"] = self.V(c, [NFM], F32); c += NFM * 4
            d["wsT"] = self.V(c, [4, 128], BF16); c += 1024
            d["lam"] = self.V(c, [16], F32); c += 64
            self.lslot.append(d)
        self.mkT = self.V(c, [8, MEM], BF16); c += 4096
        self.MVA = self.V(c, [2, 4, 258], BF16); c += 4128
        self.stat = self.V(c, [128], F32); c += 512
        assert c <= 212800, c

        kb.dma("sp", self.identf, self.ident_d[:, :])
        kb.dma("sp", self.tril, self.tril_d[:, :])
        kb.dma("sp", self.csp, self.csp_d[:, :, :])
        kb.dma("sp", self.css[0:DS], self.css_d[:, :])
        kb.cp("dve", self.identb, self.identf)
        kb.memset("dve", self.MVA[:, :, :, 256:258], 1.0)

        for l in range(L):
            for name in ["wk", "wv", "w_in", "w_out", "wq", "wo", "w_up", "w_down"]:
                src = self.wf[name]
                dst = self.wb[name]
                rows = src.shape[1]
                toks = []
                for r0 in range(0, rows, 128):
                    toks.append(kb.dma("pool", dst[l, r0:r0 + 128, :], src[l, r0:r0 + 128, :]))
                self.cast_tok[(name, l)] = toks

        import os
        self.dbg = int(os.environ.get("K_DBG", "9"))
        self.dbg_nl = int(os.environ.get("K_NL", str(L)))
        seqs = [Seq("p", b) for b in range(NB)] + [Seq("s", 0)]
        sel = os.environ.get("K_SEQS")
        if sel is not None:
            seqs = [seqs[int(i)] for i in sel.split(",")]
        if os.environ.get("K_NOCAST"):
            pass
        self.va_ones_done = None
        for sq in seqs:
            self.run_seq(sq)
        kb.finish()

    def load_panel(self, slot, name, l, rows, cols, shape):
        src = self.wb[name][l, rows[0]:rows[1], cols[0]:cols[1]].rearrange("(k p) n -> p k n", p=128)
        dst = self.wslot(slot, shape)
        self.kb.dma("sp", dst, src, extra_wait=self.cast_tok[(name, l)])
        return dst

    def run_seq(self, sq):
        kb = self.kb
        TP, NT = sq.TP, sq.NT
        self.X = self.V(self.X_OFF, [16, D], F32)
        for t in range(NT):
            if sq.kind == "p":
                kb.dma("sp", self.X[:, t, :], self.xp[sq.b, t * 128:(t + 1) * 128, :])
            else:
                kb.dma("sp", self.X[0:TP, 0, :], self.xs[:, :])
        self.KT = self.V(self.KT_OFF, [4, sq.SK], BF16)
        self.VA = self.V(self.VA_OFF, [sq.NKT, 4, 130], BF16)
        for l in range(self.dbg_nl):
            self.run_layer(sq, l)
        for t in range(NT):
            if sq.kind == "p":
                kb.dma("sp", self.yp[sq.b, t * 128:(t + 1) * 128, :], self.X[:, t, :])
            else:
                kb.dma("sp", self.ys[:, :], self.X[0:TP, 0, :])

    def norm_transpose(self, sq, xrow, gcol, outT, xn, sqr, P, sc=0):
        kb = self.kb
        ss = self.stat[0:P, sc:sc + 1]
        rs = self.stat[0:P, sc + 1:sc + 2]
        kb.act(sqr, xrow, AF.Square, accum_out=ss)
        self.rstd(rs, ss, D, P)
        kb.ts("dve", xn, xrow, rs, ALU.mult)
        pt = self.ps_tr(8, P)
        for kc in range(8):
            kb.tr(pt[:, kc, :], xn[:, kc * 128:(kc + 1) * 128], self.identb[0:P, 0:P])
        kb.tt("dve", outT, pt, gcol.unsqueeze(2).broadcast_to([128, 8, P]), ALU.mult)

    def run_layer(self, sq, l):
        kb = self.kb
        TP, NT = sq.TP, sq.NT
        isp = sq.kind == "p"
        slot = self.lslot[self.lctr % 2]
        self.lctr += 1
        lam_init = 0.8 - 0.6 * float(np.exp(-0.3 * l))
        HB = self.HB_OFF

        kb.dma("sp", slot["rep"], self.rep[l, :, :])
        kb.dma("sp", slot["fm"], self.fm[l, :, :])
        wst_f = self.V(HB, [4, 128], F32)
        kb.dma("sp", wst_f, self.wst[l, :, :].rearrange("p (g t) -> p g t", g=4))
        rep = slot["rep"]
        g_qk = rep[:, 0:128].rearrange("p (a d) -> p a d", a=2)
        g_sub = rep[:, 128:256]
        g_gmn = rep[:, 256:384]
        g_xq = rep[:, 384:640]
        g_xk = rep[:, 640:896]
        lamv = rep[:, 896:1152].rearrange("p (a d) -> p a d", a=4)
        fmv = slot["fm"]
        gfm = fmv[:, 0:32].rearrange("p (a k) -> p a k", a=4)
        gm_b = fmv[:, 32:36]
        conv_w = fmv[:, 36:102].rearrange("p (f j) -> p f j", f=NFC)
        conv_b = fmv[:, 102:124]
        lam = slot["lam"]
        kb.tt("dve", slot["wsT"], wst_f, self.tril.unsqueeze(1).broadcast_to([128, 4, 128]), ALU.mult)
        prod = self.V(HB + 2048, [2, 64], F32)
        kb.tt("dve", prod[:, 0, :], lamv[:, 0, :], lamv[:, 1, :], ALU.mult)
        kb.tt("dve", prod[:, 1, :], lamv[:, 2, :], lamv[:, 3, :], ALU.mult)
        kb.reduce_add(lam[:, 0:2], prod)
        kb.act(lam[:, 0:2], lam[:, 0:2], AF.Exp)
        kb.tt("dve", lam[:, 2:3], lam[:, 1:2], lam[:, 0:1], ALU.subtract)
        kb.ts("dve", lam[:, 3:4], lam[:, 2:3], -lam_init, ALU.add)
        kb.ts("dve", g_sub, g_sub, 1.0 - lam_init, ALU.mult)

        if isp:
            wk_p = [self.load_panel(3 + i, "wk", l, (0, D), (i * 512, (i + 1) * 512), [8, 512]) for i in range(2)]
            wv_p = [self.load_panel(5 + i, "wv", l, (0, D), (i * 512, (i + 1) * 512), [8, 512]) for i in range(2)]
        win = [None] * 5
        for i in range(3):
            win[i] = self.load_panel(i, "w_in", l, (0, D), (i * 512, (i + 1) * 512), [8, 512])

        if self.dbg < 2:
            return
        self.stage_mem(sq, l, slot, gfm, g_xk, wk_p if isp else None, wv_p if isp else None)

        if self.dbg < 3:
            return
        for i in range(3, 5):
            win[i] = self.load_panel(i, "w_in", l, (0, D), (i * 512, (i + 1) * 512), [8, 512])
        woA = self.load_panel(5, "w_out", l, (0, 512), (0, D), [4, D])
        woB = self.load_panel(6, "w_out", l, (512, D), (0, D), [4, D])

        if not os.environ.get("K_NOMEMSET"):
            kb.memset("dve", self.VA[:, :, :, 128:130], 1.0)
        if not isp:
            self.load_past(sq, l)

        QT = self.V(HB + 22528, [4, 512], BF16)
        cw = {}

        def after_last_proj():
            cw["wq"] = [self.load_panel(i, "wq", l, (0, D), (i * 512, (i + 1) * 512), [8, 512]) for i in range(2)]
            cw["wo"] = [self.load_panel(2 + i, "wo", l, (0, D), (i * 512, (i + 1) * 512), [8, 512]) for i in range(2)]
        self.stage_ab(sq, l, slot, gfm, g_qk, g_gmn, gm_b, g_sub, lam, win, woA, woB, QT, after_last_proj)
        wq_p, wo_p = cw["wq"], cw["wo"]

        if self.dbg < 4:
            return
        fpan = {}
        fpan[0] = self.load_ffn_group(l, 0, 4)

        self.stage_c(sq, l, gfm, g_xq, wq_p, wo_p)

        if self.dbg < 5:
            return
        fpan[1] = self.load_ffn_group(l, 1, 0)

        self.stage_ffn(sq, l, conv_w, conv_b, fpan)

    def load_ffn_group(self, l, g, s0):
        nf = 4 if g < 5 else 2
        c0 = g * 512
        gate = self.load_panel(s0, "w_up", l, (0, D), (c0, c0 + nf * 128), [8, nf * 128])
        up = self.load_panel(s0 + 1, "w_up", l, (0, D), (DFF + c0, DFF + c0 + nf * 128), [8, nf * 128])
        down = self.load_panel(s0 + 2, "w_down", l, (c0, c0 + nf * 128), (0, D), [nf, D])
        return gate, up, down, nf

    def stage_mem(self, sq, l, slot, gfm, g_xk, wk_p, wv_p):
        kb = self.kb
        HB = self.HB_OFF
        isp = sq.kind == "p"
        memt = self.V(HB + 4096, [D], F32)
        xn = self.V(HB + 8192, [D], BF16)
        mT = self.V(HB + 10240, [8, 128], BF16)
        kst = self.V(HB + 12288, [D], F32)
        kbf = self.V(HB + 16384, [D], BF16)
        vst = self.V(HB + 18432, [D], F32)
        sqr = self.V(HB + 22528, [D], F32)
        ss4 = self.stat[:, 4:8]
        rs4 = self.stat[:, 8:12]
        for mt in range(2):
            if isp:
                kb.dma("sp", memt, self.memp[sq.b, mt * 128:(mt + 1) * 128, :])
                self.norm_transpose(sq, memt, gfm[:, 3, :], mT, xn, sqr, 128)
                pk = self.ps_pair()
                for cb in range(2):
                    for kc in range(8):
                        kb.mm(pk[:, cb, :], lhsT=mT[:, kc, :], rhs=wk_p[cb][:, kc, :], start=(kc == 0), stop=(kc == 7))
                pk2 = pk.rearrange("p a b -> p (a b)")
                kb.act(sqr, pk2, AF.Square)
                kb.reduce_add(ss4, sqr.rearrange("p (h d) -> p h d", h=4))
                self.rstd(rs4, ss4, 256, 128)
                k3 = kst.rearrange("p (h d) -> p h d", h=4)
                kb.tt("dve", k3, pk2.rearrange("p (h d) -> p h d", h=4),
                      rs4.unsqueeze(2).broadcast_to([128, 4, 256]), ALU.mult)
                kb.tt("dve", k3, k3, g_xk.unsqueeze(1).broadcast_to([128, 4, 256]), ALU.mult)
                kb.dma("sp", self.mkp[l, sq.b, mt * 128:(mt + 1) * 128, :], kst)
                kb.cp("act", kbf, kst)
                pv = self.ps_pair()
                for cb in range(2):
                    for kc in range(8):
                        kb.mm(pv[:, cb, :], lhsT=mT[:, kc, :], rhs=wv_p[cb][:, kc, :], start=(kc == 0), stop=(kc == 7))
                pv2 = pv.rearrange("p a b -> p (a b)")
                kb.cp("act", vst, pv2)
                kb.dma("sp", self.mvp[l, sq.b, mt * 128:(mt + 1) * 128, :], vst)
                kb.cp("dve", self.MVA[:, mt, :, 0:256], vst.rearrange("p (h d) -> p h d", h=4))
            else:
                kb.dma("sp", kst, self.cmk[l, mt * 128:(mt + 1) * 128, :])
                kb.cp("act", kbf, kst)
                kb.dma("sp", vst, self.cmv[l, mt * 128:(mt + 1) * 128, :])
                kb.cp("dve", self.MVA[:, mt, :, 0:256], vst.rearrange("p (h d) -> p h d", h=4))
            pt = self.ps_tr(8, 128)
            for kc in range(8):
                kb.tr(pt[:, kc, :], kbf[:, kc * 128:(kc + 1) * 128], self.identb)
            kb.cp("dve", self.mkT[:, :, mt * 128:(mt + 1) * 128], pt)

    def load_past(self, sq, l):
        kb = self.kb
        HB = self.HB_OFF
        st = self.V(HB + 4096, [512], F32)
        sb = self.V(HB + 6144, [512], BF16)
        for kt in range(8):
            kb.dma("sp", st, self.ck[l, kt * 128:(kt + 1) * 128, :])
            kb.cp("act", sb, st)
            pt = self.ps_tr(4, 128)
            for h in range(4):
                kb.tr(pt[:, h, :], sb[:, h * 128:(h + 1) * 128], self.identb)
            kb.cp("dve", self.KT[:, :, kt * 128:(kt + 1) * 128], pt)
            st2 = self.V(HB + 8192, [512], F32)
            kb.dma("sp", st2, self.cv[l, kt * 128:(kt + 1) * 128, :])
            kb.cp("dve", self.VA[:, kt, :, 0:128], st2.rearrange("p (h d) -> p h d", h=4))

    def a_bufs(self, sq):
        P = sq.TP
        HB = self.HB_OFF
        d = {}
        d["sqr"] = self.V(HB, [D], F32, P)
        d["rope"] = self.V(HB, [4, 128], F32, P)
        d["zqk"] = self.V(HB + 4096, [D], F32, P)
        d["zv"] = self.V(HB + 8192, [512], F32, P)
        d["qkb"] = self.V(HB + 10240, [D], BF16, P)
        d["ug"] = self.V(HB + 12288, [512], F32, P)
        d["gvb"] = self.V(HB + 14336, [512], BF16, P)
        d["xn"] = self.V(HB + 16384, [D], BF16, P)
        d["hT"] = self.V(HB + 18432, [8, P], BF16)
        d["gob"] = self.V(HB + 20480, [512], BF16, P)
        d["goT"] = self.V(HB + 21504, [4, P], BF16)
        return d

    def stage_ab(self, sq, l, slot, gfm, g_qk, g_gmn, gm_b, g_sub, lam, win, woA, woB, QT, after_last_proj):
        kb = self.kb
        P = sq.TP
        NT = sq.NT
        isp = sq.kind == "p"
        bf = self.a_bufs(sq)
        sqr, rope, zqk, zv, qkb, ug, gvb, xn, hT, gob, goT = (bf[k] for k in
            ["sqr", "rope", "zqk", "zv", "qkb", "ug", "gvb", "xn", "hT", "gob", "goT"])
        gg = sqr[:, 0:512]
        sq2 = sqr[:, 512:1024]

        def a0(t):
            self.norm_ew(self.X[0:P, t, :], xn, sqr, P, 0)

        def a1(t):
            self.tr_evac(xn, gfm[:, 0, :], hT, P)

        def a2(t):
            cs = self.csp[:, t, :] if isp else self.css[0:P, :]
            cosv = cs[:, 0:8]
            sinv = cs[:, 8:16]
            pqk = self.ps_pair()
            for which in range(2):
                for kc in range(8):
                    kb.mm(pqk[0:P, which, :], lhsT=hT[:, kc, :], rhs=win[which][:, kc, :], start=(kc == 0), stop=(kc == 7))
            pz3 = []
            for i, cg in enumerate((2, 3, 4)):
                pz = self.ps[0:P, 4 + i, :]
                for kc in range(8):
                    kb.mm(pz, lhsT=hT[:, kc, :], rhs=win[cg][:, kc, :], start=(kc == 0), stop=(kc == 7))
                pz3.append(pz)
            pv_, pu, pg = pz3
            if t == NT - 1:
                after_last_proj()
            ss16 = self.stat[0:P, 16:32]
            rs16 = self.stat[0:P, 96:112]
            pqk2 = pqk[0:P].rearrange("p a b -> p (a b)")
            kb.act(sqr, pqk2, AF.Square)
            kb.reduce_add(ss16, sqr.rearrange("p (g d) -> p g d", g=16))
            self.rstd(rs16, ss16, 64, P)
            kb.cp("act", zv, pv_)
            kb.act(ug, pu, AF.Gelu_apprx_tanh)
            d3 = zqk.rearrange("p (g d) -> p g d", g=16)
            kb.tt("dve", d3, pqk2.rearrange("p (g d) -> p g d", g=16), rs16.unsqueeze(2).broadcast_to([P, 16, 64]), ALU.mult)
            d4 = zqk.rearrange("p (a g d) -> p a g d", a=2, g=8)
            kb.tt("dve", d4, d4, g_qk[0:P].unsqueeze(2).broadcast_to([P, 2, 8, 64]), ALU.mult)
            x1 = d3[:, :, 0:8]
            x2 = d3[:, :, 8:16]
            cb_ = cosv.unsqueeze(1).broadcast_to([P, 16, 8])
            sb_ = sinv.unsqueeze(1).broadcast_to([P, 16, 8])
            r3 = rope.rearrange("p a (g d) -> p a g d", g=16)
            kb.tt("dve", r3[:, 0], x1, cb_, ALU.mult)
            kb.tt("dve", r3[:, 1], x2, sb_, ALU.mult)
            kb.tt("dve", r3[:, 2], x2, cb_, ALU.mult)
            kb.tt("dve", r3[:, 3], x1, sb_, ALU.mult)
            kb.tt("dve", x1, r3[:, 0], r3[:, 1], ALU.subtract)
            kb.tt("dve", x2, r3[:, 2], r3[:, 3], ALU.add)
            kb.cp("act", qkb, zqk)
            zk = zqk[:, 512:1024]
            if isp:
                kb.dma("sp", self.dkp[l, sq.b, t * 128:(t + 1) * 128, :], zk)
                kb.dma("sp", self.dvp[l, sq.b, t * 128:(t + 1) * 128, :], zv)
            else:
                kb.dma("sp", self.dks[l, :, :], zk)
                kb.dma("sp", self.dvs[l, :, :], zv)
            vt = t if isp else 8
            kb.cp("dve", self.VA[0:P, vt, :, 0:128], zv.rearrange("p (h d) -> p h d", h=4))
            kb.act(gg, pg, AF.Gelu_apprx_tanh)
            kb.act(sq2, gg, AF.Square)
            ss4 = self.stat[0:P, 4:8]
            rs4 = self.stat[0:P, 8:12]
            kb.reduce_add(ss4, sq2.rearrange("p (g d) -> p g d", g=4))
            self.rstd(rs4, ss4, 128, P)
            gg3 = gg.rearrange("p (g d) -> p g d", g=4)
            kb.tt("dve", gg3, gg3, rs4.unsqueeze(2).broadcast_to([P, 4, 128]), ALU.mult)
            kb.tt("dve", gg3, gg3, g_gmn[0:P, :].unsqueeze(1).broadcast_to([P, 4, 128]), ALU.mult)
            if not isp:
                kb.dma("sp", self.gvs[l, :, :], gg)
            kb.cp("act", gvb, gg)

        def a3(t):
            ti = t % sq.TPB
            pt = self.ps_tr(8, P)
            for j in range(8):
                kb.tr(pt[:, j, :], qkb[:, j * 128:(j + 1) * 128], self.identb[0:P, 0:P])
            pgate = self.ps_mm()[0:P, :]
            for g in range(4):
                kb.mm(pgate[:, g * 128:(g + 1) * 128], lhsT=slot["wsT"][0:P, g, 0:P], rhs=gvb[:, g * 128:(g + 1) * 128])
            kb.cp("dve", QT[:, :, ti * 128:ti * 128 + P], pt[:, 0:4, :])
            kpos = t * 128 if isp else PAST
            kb.cp("act", self.KT[:, :, kpos:kpos + P], pt[:, 4:8, :])
            go3 = sq2.rearrange("p (g d) -> p g d", g=4)
            kb.tt("dve", go3, pgate.rearrange("p (g d) -> p g d", g=4),
                  gm_b[0:P, :].unsqueeze(2).broadcast_to([P, 4, 128]), ALU.add)
            kb.tt("dve", gob, sq2, ug, ALU.mult)
            if ti == sq.TPB - 1:
                self.stage_b_block(sq, l, t // sq.TPB, slot, g_sub, lam, woA, QT)

        def a4(t):
            pt2 = self.ps_tr(4, P)
            for j in range(4):
                kb.tr(pt2[:, j, :], gob[:, j * 128:(j + 1) * 128], self.identb[0:P, 0:P])
            kb.cp("act", goT, pt2)

        def a5(t):
            xrow = self.X[0:P, t, :]
            py = self.ps_pair()
            for cb in range(2):
                for kc in range(4):
                    kb.mm(py[0:P, cb, :], lhsT=goT[:, kc, :], rhs=woB[:, kc, cb * 512:(cb + 1) * 512],
                          start=(kc == 0), stop=(kc == 3))
            kb.tt("dve", xrow, py[0:P].rearrange("p a b -> p (a b)"), xrow, ALU.add)

        phases = [a0, a1, a2, a3, a4, a5]
        for i in range(NT + len(phases) - 1):
            for k in reversed(range(len(phases))):
                t = i - k
                if 0 <= t < NT:
                    phases[k](t)

    def stage_b_block(self, sq, l, blk, slot, g_sub, lam, woA, QT):
        kb = self.kb
        HB = self.HB_OFF
        isp = sq.kind == "p"
        P = sq.TP
        QB = sq.QB
        eTs = [self.V(HB + 26624 + i * 2048, [2, 512], BF16) for i in range(2)]
        OB = self.V(HB + 26624, [4, 512], BF16)
        otmp = self.V(HB + 30720, [128], F32)
        oT = self.V(HB + 31232, [4, 128], BF16)
        OF = self.V(HB, [4, 512], F32)
        osq = self.V(HB + 8192, [512], F32)
        accS = self.V(HB + 10240, [3, 480], F32)
        nqs = sq.TPB
        if isp:
            nkt = 4 * blk + 4
        else:
            nkt = 9
        rz = self.stat[0:P, 32:48]
        ssq = self.stat[0:P, 48:49]
        rso = self.stat[0:P, 49:50]
        nl = self.stat[0:P, 50:51]
        def acc(qs, m):
            r = qs * 2 + m
            return self.ps[0:P, 4 + r // 3, (r % 3) * 160:(r % 3) * 160 + 129]

        def accs(qs, m):
            r = qs * 2 + m
            return accS[0:P, r // 3, (r % 3) * 160:(r % 3) * 160 + 129]

        steps = [(h, kt) for h in range(4) for kt in range(nkt)]

        def qk_exp(i):
            h, kt = steps[i]
            KP = 128 if (isp or kt < 8) else DS
            j = kt - 4 * blk if isp else -1
            q0 = max(0, j) * 128
            kpos = kt * 128
            psS = self.ps_pair()
            for m in range(2):
                kb.mm(psS[0:KP, m, q0:QB], lhsT=self.KT[m * 64:(m + 1) * 64, h, kpos:kpos + KP],
                      rhs=QT[m * 64:(m + 1) * 64, h, q0:QB])
            eT = eTs[i % 2]
            kb.act(eT[0:KP, :, q0:QB], psS[0:KP, :, q0:QB], AF.Exp, scale=0.125)
            if j >= 0:
                kb.memset("dve", eT[64:128, :, q0:q0 + 64], 0.0)

        def pv(i):
            h, kt = steps[i]
            KP = 128 if (isp or kt < 8) else DS
            j = kt - 4 * blk if isp else -1
            eT = eTs[i % 2]
            for qs in range(max(0, j), nqs):
                last = (kt == 4 * blk + qs) if isp else (kt == nkt - 1)
                for m in range(2):
                    kb.mm(acc(qs, m), lhsT=eT[0:KP, m, qs * 128:qs * 128 + P], rhs=self.VA[0:KP, kt, h, 0:129],
                          start=(kt == 0 and (qs * 2 + m) % 3 == 0), stop=last)
            if kt == nkt - 1:
                nreg = 2 * nqs
                for bk in range((nreg + 2) // 3):
                    w_ = min(3, nreg - 3 * bk) * 160
                    kb.cp("act", accS[0:P, bk, 0:w_], self.ps[0:P, 4 + bk, 0:w_])
                for qs in range(nqs):
                    a0 = accs(qs, 0)
                    a1 = accs(qs, 1)
                    kb.recip(rz[:, 0:1], a0[:, 128:129])
                    kb.recip(rz[:, 1:2], a1[:, 128:129])
                    kb.tt("dve", nl, rz[:, 1:2], lam[0:P, 3:4], ALU.mult)
                    ot = otmp[0:P, :]
                    kb.ts("dve", ot, a0[:, 0:128], rz[:, 0:1], ALU.mult)
                    kb.stt("dve", OF[0:P, qs, h * 128:(h + 1) * 128], a1[:, 0:128], nl, ot, ALU.mult, ALU.add)

        qk_exp(0)
        for i in range(len(steps)):
            if i + 1 < len(steps):
                qk_exp(i + 1)
            pv(i)
        ss4 = self.stat[0:P, 4:8]
        rs4 = self.stat[0:P, 8:12]
        for qs in range(nqs):
            t = blk * sq.TPB + qs
            of = OF[0:P, qs, :]
            kb.act(osq[0:P, :], of, AF.Square)
            kb.reduce_add(ss4, osq[0:P, :].rearrange("p (h d) -> p h d", h=4))
            self.rstd(rs4, ss4, 128, P)
            of3 = of.rearrange("p (h d) -> p h d", h=4)
            kb.tt("dve", of3, of3, rs4.unsqueeze(2).broadcast_to([P, 4, 128]), ALU.mult)
            kb.tt("dve", OB[0:P, qs, :].rearrange("p (h d) -> p h d", h=4), of3,
                  g_sub[0:P, :].unsqueeze(1).broadcast_to([P, 4, 128]), ALU.mult)
            pt = self.ps_tr(4, P)
            for j in range(4):
                kb.tr(pt[:, j, :], OB[0:P, qs, j * 128:(j + 1) * 128], self.identb[0:P, 0:P])
            kb.cp("act", oT[:, :, 0:P], pt)
            py = self.ps_pair()
            for cb in range(2):
                for kc in range(4):
                    kb.mm(py[0:P, cb, :], lhsT=oT[:, kc, 0:P], rhs=woA[:, kc, cb * 512:(cb + 1) * 512],
                          start=(kc == 0), stop=(kc == 3))
            xrow = self.X[0:P, t, :]
            kb.tt("dve", xrow, py[0:P].rearrange("p a b -> p (a b)"), xrow, ALU.add)

    def norm_ew(self, xrow, xn, sqr, P, sc):
        kb = self.kb
        ss = self.stat[0:P, sc:sc + 1]
        rs = self.stat[0:P, sc + 1:sc + 2]
        kb.act(sqr, xrow, AF.Square, accum_out=ss)
        self.rstd(rs, ss, D, P)
        kb.ts("dve", xn, xrow, rs, ALU.mult)

    def tr_evac(self, xn, gcol, outT, P):
        kb = self.kb
        pt = self.ps_tr(8, P)
        for kc in range(8):
            kb.tr(pt[:, kc, :], xn[:, kc * 128:(kc + 1) * 128], self.identb[0:P, 0:P])
        kb.tt("dve", outT, pt, gcol.unsqueeze(2).broadcast_to([128, 8, P]), ALU.mult)

    def stage_c(self, sq, l, gfm, g_xq, wq_p, wo_p):
        kb = self.kb
        P = sq.TP
        NT = sq.NT
        o = self.KT_OFF
        xn1 = self.V(o, [D], BF16, P); o += 2048
        sqj = self.V(o, [D], F32, P); o += 4096
        h2T = self.V(o, [8, P], BF16); o += 2048
        sqq = self.V(o, [D], F32, P); o += 4096
        qcn = self.V(o, [D], BF16, P); o += 2048
        qcT = self.V(o, [8, P], BF16); o += 2048
        eT = self.V(o, [4, 2, P], BF16); o += 2048
        ocb = self.V(o, [D], BF16, P); o += 2048
        ocT = self.V(o, [8, P], BF16); o += 2048
        xn3 = self.V(o, [D], BF16, P); o += 2048
        assert o <= self.KT_OFF + 33024, o
        ss4 = self.stat[0:P, 4:8]
        rs4 = self.stat[0:P, 8:12]
        rz4 = self.stat[0:P, 68:72]
        HBv = self.V(self.HB_OFF, [8, sq.S], BF16)

        def f0(t):
            self.norm_ew(self.X[0:P, t, :], xn1, sqj, P, 0)

        def f1(t):
            self.tr_evac(xn1, gfm[:, 1, :], h2T, P)

        def f2(t):
            pq = self.ps_pair()
            for cb in range(2):
                for kc in range(8):
                    kb.mm(pq[0:P, cb, :], lhsT=h2T[:, kc, :], rhs=wq_p[cb][:, kc, :], start=(kc == 0), stop=(kc == 7))
            pq2 = pq[0:P].rearrange("p a b -> p (a b)")
            kb.act(sqq, pq2, AF.Square)
            kb.reduce_add(ss4, sqq.rearrange("p (h d) -> p h d", h=4))
            self.rstd(rs4, ss4, 256, P)
            s3 = sqq.rearrange("p (h d) -> p h d", h=4)
            kb.tt("dve", s3, pq2.rearrange("p (h d) -> p h d", h=4), rs4.unsqueeze(2).broadcast_to([P, 4, 256]), ALU.mult)
            kb.tt("dve", qcn.rearrange("p (h d) -> p h d", h=4), s3, g_xq[0:P, :].unsqueeze(1).broadcast_to([P, 4, 256]),
                  ALU.mult)

        def f3(t):
            pt = self.ps_tr(8, P)
            for j in range(8):
                kb.tr(pt[:, j, :], qcn[:, j * 128:(j + 1) * 128], self.identb[0:P, 0:P])
            kb.cp("act", qcT, pt)

        def f4(t):
            psS = self.ps_pair()
            for h in range(4):
                for mt in range(2):
                    c0 = (h * 2 + mt) * 128
                    dst = psS[:, c0 // 512, (c0 % 512):(c0 % 512) + P]
                    for dc in range(2):
                        kb.mm(dst, lhsT=self.mkT[:, h * 2 + dc, mt * 128:(mt + 1) * 128], rhs=qcT[:, h * 2 + dc, :],
                              start=(dc == 0), stop=(dc == 1))
            for half in range(2):
                kb.act(eT[:, half * 2:half * 2 + 2, :, :],
                       psS[:, half, :].rearrange("p (h m q) -> p h m q", h=2, m=2)[:, :, :, 0:P], AF.Exp, scale=1.0 / 16.0)

        def f5(t):
            for h in range(4):
                po = self.ps[0:P, 4 + (h % 3), 0:257]
                for mt in range(2):
                    kb.mm(po, lhsT=eT[:, h, mt, :], rhs=self.MVA[:, mt, h, 0:257], start=(mt == 0), stop=(mt == 1))
                kb.recip(rz4[:, h:h + 1], po[:, 256:257])
                kb.ts("dve", ocb[:, h * 256:(h + 1) * 256], po[:, 0:256], rz4[:, h:h + 1], ALU.mult)

        def f6(t):
            pt2 = self.ps_tr(8, P)
            for j in range(8):
                kb.tr(pt2[:, j, :], ocb[:, j * 128:(j + 1) * 128], self.identb[0:P, 0:P])
            kb.cp("act", ocT, pt2)

        def f7(t):
            xrow = self.X[0:P, t, :]
            py = self.ps_pair()
            for cb in range(2):
                for kc in range(8):
                    kb.mm(py[0:P, cb, :], lhsT=ocT[:, kc, :], rhs=wo_p[cb][:, kc, :], start=(kc == 0), stop=(kc == 7))
            kb.tt("dve", xrow, py[0:P].rearrange("p a b -> p (a b)"), xrow, ALU.add)
            self.norm_ew(xrow, xn3, sqj, P, 64)

        def f8(t):
            self.tr_evac(xn3, gfm[:, 2, :], HBv[:, :, t * 128:t * 128 + P], P)

        phases = [f0, f1, f2, f3, f4, f5, f6, f7, f8]
        for i in range(NT + len(phases) - 1):
            for k in reversed(range(len(phases))):
                t = i - k
                if 0 <= t < NT:
                    phases[k](t)

    def stage_ffn(self, sq, l, conv_w, conv_b, fpan):
        kb = self.kb
        isp = sq.kind == "p"
        P = sq.TP
        NTOK = sq.QB
        HBv = self.V(self.HB_OFF, [8, sq.S], BF16)
        o = self.KT_OFF
        gss = [self.V(o + i * 2064, [516], F32) for i in range(2)]; o += 4128
        ccs = [self.V(o + i * 2048, [512], F32) for i in range(2)]; o += 4096
        scs = [self.V(o + i * 2048, [512], F32) for i in range(2)]; o += 4096
        aTs = [self.V(o + i * 4096, [4, 512], BF16) for i in range(2)]; o += 8192
        carry = self.V(o, [NFC, 2], F32); o += 176
        cst = self.V(o, [512], F32, 2); o += 2048
        assert o <= self.KT_OFF + 33024
        if isp:
            kb.memset("dve", carry, 0.0)
        else:
            kb.dma("sp", carry, self.convst_d[:, l, :, :])
        ri = 0
        ai = 0
        pending = None
        for g in range(6):
            gate, up, down, nf = fpan[g]
            for blk in range(sq.NBLK):
                aT = aTs[ai % 2]
                ai += 1
                for fl in range(nf):
                    fc = g * 4 + fl
                    pg = self.ps_mm()
                    for kc in range(8):
                        kb.mm(pg[:, 0:NTOK], lhsT=gate[:, kc, fl * 128:(fl + 1) * 128],
                              rhs=HBv[:, kc, blk * 512:blk * 512 + NTOK], start=(kc == 0), stop=(kc == 7))
                    pu = self.ps_mm()
                    for kc in range(8):
                        kb.mm(pu[:, 0:NTOK], lhsT=up[:, kc, fl * 128:(fl + 1) * 128],
                              rhs=HBv[:, kc, blk * 512:blk * 512 + NTOK], start=(kc == 0), stop=(kc == 7))
                    gs = gss[ri % 2]
                    cc = ccs[ri % 2]
                    sc = scs[ri % 2]
                    ri += 1
                    kb.cp("act", gs[:, 2:2 + NTOK], pg[:, 0:NTOK])
                    kb.cp("dve", gs[:, 0:2], carry[:, fc, :])
                    kb.cp("dve", carry[:, fc, :], gs[:, NTOK:NTOK + 2])
                    kb.act(cc[:, 0:NTOK], pg[:, 0:NTOK], AF.Identity, bias=conv_b[:, fc:fc + 1], scale=conv_w[:, fc, 2:3])
                    kb.stt("dve", cc[:, 0:NTOK], gs[:, 1:1 + NTOK], conv_w[:, fc, 1:2], cc[:, 0:NTOK], ALU.mult, ALU.add)
                    kb.stt("dve", cc[:, 0:NTOK], gs[:, 0:NTOK], conv_w[:, fc, 0:1], cc[:, 0:NTOK], ALU.mult, ALU.add)
                    kb.act(sc[:, 0:NTOK], cc[:, 0:NTOK], AF.Silu)
                    kb.tt("dve", aT[:, fl, 0:NTOK], sc[:, 0:NTOK], pu[:, 0:NTOK], ALU.mult)
                def do_down(blk=blk, aT=aT, down=down, nf=nf):
                    for ti in range(sq.TPB):
                        t = blk * sq.TPB + ti
                        py = self.ps[0:P, 4 + 2 * (t % 2):6 + 2 * (t % 2), :]
                        for cb in range(2):
                            for fl in range(nf):
                                kb.mm(py[:, cb, :], lhsT=aT[:, fl, ti * 128:ti * 128 + P],
                                      rhs=down[:, fl, cb * 512:(cb + 1) * 512], start=(fl == 0), stop=(fl == nf - 1))
                        xrow = self.X[0:P, t, :]
                        kb.tt("dve", xrow, py.rearrange("p a b -> p (a b)"), xrow, ALU.add)
                if pending is not None:
                    pending()
                pending = do_down
            if pending is not None:
                pending()
                pending = None
            pc = self.ps[0:2, 3, :]
            for fl in range(nf):
                kb.tr(pc[:, fl * 128:(fl + 1) * 128], carry[:, g * 4 + fl, :], self.identf)
            kb.cp("act", cst[:, 0:nf * 128], pc[:, 0:nf * 128])
            dst = self.fcp[l, sq.b, :, g * 512:g * 512 + nf * 128] if isp else self.fcs[l, :, g * 512:g * 512 + nf * 128]
            kb.dma("sp", dst, cst[:, 0:nf * 128])
            if g + 2 < 6:
                fpan[g + 2] = self.load_ffn_group(l, g + 2, 4 if (g % 2 == 0) else 0)


_PROG = None


def _get_prog():
    global _PROG
    if _PROG is None:
        _PROG = Prog()
    return _PROG


def _host_consts(inp):
    f32 = np.float32
    rep = np.zeros((L, 128, NREP), f32)
    fm = np.zeros((L, 128, NFM), f32)
    for l in range(L):
        v = np.concatenate([inp["da_q_norm_g"][l], inp["da_k_norm_g"][l], inp["da_subln_g"][l], inp["gm_norm_g"][l],
                            inp["xq_norm_g"][l], inp["xk_norm_g"][l], inp["lambda_q1"][l], inp["lambda_k1"][l],
                            inp["lambda_q2"][l], inp["lambda_k2"][l]]).astype(f32)
        rep[l] = np.broadcast_to(v[None, :], (128, NREP))
        for i, nm in enumerate(["norm_mix_g", "norm_x_g", "norm_ffn_g", "norm_mem_g"]):
            fm[l, :, i * 8:(i + 1) * 8] = inp[nm][l].reshape(8, 128).T
        fm[l, :, 32:36] = inp["gm_b"][l].T
        cw = inp["conv_w"][l].reshape(3, NFC, 128)
        fm[l, :, 36:102] = cw.transpose(2, 1, 0).reshape(128, NFC * 3)
        fm[l, :, 102:124] = inp["conv_b"][l].reshape(NFC, 128).T
    wst = np.ascontiguousarray(inp["gm_w_s"].transpose(0, 3, 1, 2)).reshape(L, 128, 512).astype(f32)
    ident = np.eye(128, dtype=f32)
    tril = np.triu(np.ones((128, 128), f32))
    half = 8
    inv = (np.float32(500000.0) ** (-np.arange(half, dtype=f32) / np.float32(half))).astype(f32)

    def cs(pos):
        ang = pos.astype(f32)[:, None] * inv[None, :]
        return np.concatenate([np.cos(ang), np.sin(ang)], axis=1).astype(f32)

    csp = cs(np.arange(S)).reshape(16, 128, 16).transpose(1, 0, 2)
    css = cs(PAST + np.arange(DS))
    return dict(rep=rep, fm=fm, wst=wst, ident=ident, tril=tril, csp=np.ascontiguousarray(csp), css=css)


def _in_maps(inp, cores):
    hc = _host_consts(inp)
    in_maps = []
    for c in cores:
        m = dict(hc)
        m["xp"] = np.ascontiguousarray(inp["x_prompt"][c * NB:(c + 1) * NB])
        m["xs"] = np.ascontiguousarray(inp["x_sample"][c])
        m["ck"] = np.ascontiguousarray(inp["cache_da_k"][:, c].reshape(L, PAST, 512))
        m["cv"] = np.ascontiguousarray(inp["cache_da_v"][:, c].reshape(L, PAST, 512))
        m["cmk"] = np.ascontiguousarray(inp["cache_mem_k"][:, c].reshape(L, MEM, D))
        m["cmv"] = np.ascontiguousarray(inp["cache_mem_v"][:, c].reshape(L, MEM, D))
        m["memp"] = np.ascontiguousarray(inp["mem_prompt"][c * NB:(c + 1) * NB])
        st = inp["state_ffn_conv"][:, c]
        m["convst"] = np.ascontiguousarray(st.reshape(L, 2, NFC, 128).transpose(3, 0, 2, 1))
        m["w_in"] = inp["w_in"]; m["w_out"] = inp["w_out"]
        m["wq"] = inp["wq_c"]; m["wk"] = inp["wk_c"]; m["wv"] = inp["wv_c"]; m["wo"] = inp["wo_c"]
        m["w_up"] = inp["w_up"]; m["w_down"] = inp["w_down"]
        in_maps.append(m)
    return in_maps


def kernel(**inp):
    inp = {k: np.asarray(v) for k, v in inp.items()}
    prog = _get_prog()
    n = 8
    in_maps = _in_maps(inp, range(n))
    res = run_bass_kernel_spmd(prog.nc, in_maps, core_ids=list(range(n))).results
    cat = lambda k, ax: np.concatenate([r[k] for r in res], axis=ax)
    y_p = cat("yp", 0)
    y_s = np.stack([r["ys"] for r in res], 0)
    dk_p = cat("dkp", 1).reshape(L, 32, S, 4, 2, 64)
    dv_p = cat("dvp", 1).reshape(L, 32, S, 4, 128)
    mk_p = cat("mkp", 1).reshape(L, 32, MEM, 4, 256)
    mv_p = cat("mvp", 1).reshape(L, 32, MEM, 4, 256)
    fc_p = cat("fcp", 1)
    dk_s = np.stack([r["dks"] for r in res], 1).reshape(L, 8, DS, 4, 2, 64)
    dv_s = np.stack([r["dvs"] for r in res], 1).reshape(L, 8, DS, 4, 128)
    gv_s = np.stack([r["gvs"] for r in res], 1).reshape(L, 8, DS, 4, 128)
    fc_s = np.stack([r["fcs"] for r in res], 1)
    return (y_p, y_s, dk_p, dv_p, mk_p, mv_p, fc_p, dk_s, dv_s, gv_s, fc_s)
```

```python
import os
import numpy as np
from contextlib import ExitStack
import concourse.bass as bass
import concourse.mybir as mybir
from concourse.bass_utils import run_bass_kernel_spmd

F32 = mybir.dt.float32
BF16 = mybir.dt.bfloat16
AF = mybir.ActivationFunctionType
ALU = mybir.AluOpType
AX = mybir.AxisListType
CELL = 256
SAME_ENGINE_SYNC = bool(int(os.environ.get("K_SES", "0")))
SMALL_T = int(os.environ.get("K_SMALL", "256"))
N_DMA_SEMS = 12

D = 1024
L = 4
NB = 4
S = 2048
DS = 32
PAST = 1024
MEM = 256
DFF = 2816
NFC = 22
EPS = 1e-6
NREP = 1152
NFM = 124


def _esize(dt):
    return 2 if dt == BF16 else 4


class KB:
    def __init__(self):
        self.nc = bass.Bass("TRN2", target_bir_lowering=False)
        nc = self.nc
        self.es = ExitStack()
        self.eng = {"pe": nc.tensor, "act": nc.scalar, "dve": nc.vector, "pool": nc.gpsimd, "sp": nc.sync}
        self.sems = {}
        self.sem_id = {}
        self.cnt = {}
        self._nsem = 0
        for e in ["pe", "act", "dve", "pool"]:
            self.sems[e] = self._newsem("c_" + e)
            self.cnt[e] = 0
        self.dq = {}
        for q in ["sp", "pool"]:
            pool = [self._newsem(f"d_{q}{i}") for i in range(N_DMA_SEMS)]
            self.dq[q] = {"sems": pool, "vals": [0] * N_DMA_SEMS, "next": 0}
        self.seen = {e: {} for e in ["pe", "act", "dve", "pool", "sp"]}
        self.cells = {}
        self.n_inst = 0
        self.n_wait = 0

    def _newsem(self, name):
        h = self.es.enter_context(self.nc.semaphore(name))
        k = self._nsem
        self._nsem += 1
        self.sem_id[k] = h
        return k

    def _cells(self, ap):
        sp = str(ap.space).upper()
        if "SB" not in sp and "PSUM" not in sp:
            return None
        es = _esize(ap.dtype)
        a = ap.ap
        pstep = a[0][0]
        off = ap.offset
        base = (off % pstep) * es if pstep > 0 else off * es
        region = ap.tensor.name
        cell = CELL if "SB" in sp else 2048
        dims = [(s * es, c) for (s, c) in a[1:]]
        if not dims:
            dims = [(es, 1)]
        ls, lc = dims[-1]
        run = (lc - 1) * ls + es
        starts = [base]
        for (s, c) in dims[:-1]:
            if s == 0 or c == 1:
                continue
            starts = [st + i * s for st in starts for i in range(c)]
        out = set()
        for st in starts:
            for c in range(st // cell, (st + run - 1) // cell + 1):
                out.add((region, c))
        fsz = 1
        for (s_, c_) in a[1:]:
            if s_ != 0:
                fsz *= c_
        self._last_fsz = fsz
        return out

    def _deps(self, reads, writes, own=None):
        deps = {}
        rc = set()
        wc = set()
        msize = 1 << 30
        for ap in reads:
            c = self._cells(ap)
            if c:
                msize = min(msize, self._last_fsz)
                if "PSUM" in str(ap.space).upper():
                    wc |= c
                else:
                    rc |= c
        for ap in writes:
            c = self._cells(ap)
            if c:
                msize = min(msize, self._last_fsz)
                wc |= c
        self._msize = msize
        cells = self.cells
        small = msize <= SMALL_T

        def add(tok):
            k, v, sz = tok
            if k == own and not SAME_ENGINE_SYNC and not (small or sz <= SMALL_T):
                return
            if deps.get(k, 0) < v:
                deps[k] = v
        for c in rc:
            st = cells.get(c)
            if st is not None and st[0] is not None:
                add(st[0])
        for c in wc:
            st = cells.get(c)
            if st is not None:
                if st[0] is not None:
                    add(st[0])
                for k, (v, sz) in st[1].items():
                    add((k, v, sz))
        return deps, rc, wc

    def _commit(self, tok, rc, wc):
        k, v = tok
        sz = self._msize
        cells = self.cells
        for c in rc:
            if c in wc:
                continue
            st = cells.get(c)
            if st is None:
                st = [None, {}]
                cells[c] = st
            old = st[1].get(k)
            if old is None or old[0] < v:
                st[1][k] = (v, sz)
        t3 = (k, v, sz)
        for c in wc:
            cells[c] = [t3, {}]

    def _waits(self, ename, deps):
        e = self.eng[ename]
        seen = self.seen[ename]
        own = self.sems.get(ename)
        for k, v in deps.items():
            if k == own and ename == "pe":
                continue
            if seen.get(k, 0) >= v:
                continue
            e.wait_ge(self.sem_id[k], v)
            seen[k] = v
            self.n_wait += 1
            if os.environ.get("K_TRACE"):
                print("   WAIT", ename, "sem", k, ">=", v)

    def I(self, ename, fn, reads, writes):
        deps, rc, wc = self._deps(reads, writes, self.sems.get(ename))
        self._waits(ename, deps)
        ins = fn()
        self.cnt[ename] += 1
        k = self.sems[ename]
        if os.environ.get("K_TRACE"):
            print("INS", ename, self.cnt[ename], str(ins)[:150])
        ins.then_inc(self.sem_id[k], 1)
        self._commit((k, self.cnt[ename]), rc, wc)
        self.n_inst += 1
        return ins

    def dma(self, q, out, in_, extra_wait=None):
        deps, rc, wc = self._deps([in_], [out])
        if extra_wait:
            for k, v in extra_wait:
                if deps.get(k, 0) < v:
                    deps[k] = v
        d = self.dq[q]
        i = d["next"]
        d["next"] = (i + 1) % N_DMA_SEMS
        k = d["sems"][i]
        if d["vals"][i] > 0:
            deps[k] = max(deps.get(k, 0), d["vals"][i])
        self._waits(q, deps)
        ins = self.eng[q].dma_start(out=out, in_=in_)
        d["vals"][i] += 16
        ins.then_inc(self.sem_id[k], 16)
        tok = (k, d["vals"][i])
        if os.environ.get("K_TRACE"):
            print("DMA", q, tok, str(ins)[:150])
        self._commit(tok, rc, wc)
        self.n_inst += 1
        return tok

    def finish(self):
        last = {}
        for q, d in self.dq.items():
            for k, v in zip(d["sems"], d["vals"]):
                if v > 0:
                    last[k] = v
        for e in ["pe", "act", "dve", "pool"]:
            if self.cnt[e] > 0:
                last[self.sems[e]] = self.cnt[e]
        self._waits("sp", last)

    def mm(self, out, lhsT, rhs, start=True, stop=True):
        return self.I("pe", lambda: self.nc.tensor.matmul(out, lhsT=lhsT, rhs=rhs, start=start, stop=stop),
                      [lhsT, rhs], [out])

    def tr(self, out, in_, ident):
        return self.I("pe", lambda: self.nc.tensor.transpose(out, in_, ident), [in_, ident], [out])

    def act(self, out, in_, func, bias=None, scale=1.0, accum_out=None):
        reads = [in_]
        kw = {}
        if bias is not None:
            kw["bias"] = bias
            if not isinstance(bias, (int, float)):
                reads.append(bias)
        if not isinstance(scale, (int, float)):
            reads.append(scale)
        writes = [out]
        if accum_out is not None:
            kw["accum_out"] = accum_out
            writes.append(accum_out)
        return self.I("act", lambda: self.nc.scalar.activation(out=out, in_=in_, func=func, scale=scale, **kw),
                      reads, writes)

    def tt(self, e, out, in0, in1, op):
        return self.I(e, lambda: self.eng[e].tensor_tensor(out=out, in0=in0, in1=in1, op=op), [in0, in1], [out])

    def ts(self, e, out, in0, s1, op0, s2=None, op1=None):
        reads = [in0]
        if not isinstance(s1, (int, float)):
            reads.append(s1)
        if s2 is not None and not isinstance(s2, (int, float)):
            reads.append(s2)
        kw = {}
        if op1 is not None:
            kw["op1"] = op1
        return self.I(e, lambda: self.eng[e].tensor_scalar(out=out, in0=in0, scalar1=s1, scalar2=s2, op0=op0, **kw),
                      reads, [out])

    def stt(self, e, out, in0, scalar, in1, op0, op1):
        reads = [in0, in1]
        if not isinstance(scalar, (int, float)):
            reads.append(scalar)
        return self.I(e, lambda: self.eng[e].scalar_tensor_tensor(out=out, in0=in0, scalar=scalar, in1=in1,
                                                                  op0=op0, op1=op1), reads, [out])

    def cp(self, e, out, in_):
        if e == "act":
            return self.I("act", lambda: self.nc.scalar.copy(out=out, in_=in_), [in_], [out])
        return self.I(e, lambda: self.eng[e].tensor_copy(out=out, in_=in_), [in_], [out])

    def memset(self, e, out, val):
        return self.I(e, lambda: self.eng[e].memset(out, val), [], [out])

    def recip(self, out, in_):
        return self.I("dve", lambda: self.nc.vector.reciprocal(out=out, in_=in_), [in_], [out])

    def reduce_add(self, out, in_):
        return self.I("dve", lambda: self.nc.vector.tensor_reduce(out=out, in_=in_, axis=AX.X, op=ALU.add),
                      [in_], [out])


class Seq:
    def __init__(self, kind, b):
        self.kind = kind
        self.b = b
        if kind == "p":
            self.S, self.TP, self.NT, self.QB, self.NBLK, self.TPB = S, 128, 16, 512, 4, 4
            self.SK, self.NKT = S, 16
        else:
            self.S, self.TP, self.NT, self.QB, self.NBLK, self.TPB = DS, DS, 1, DS, 1, 1
            self.SK, self.NKT = PAST + DS, 9


class Prog:
    def __init__(self):
        self.kb = KB()
        kb = self.kb
        nc = kb.nc
        self.nc = nc

        def din(name, shape, dt=F32):
            return nc.dram_tensor(name, list(shape), dt, kind="ExternalInput").ap()

        def dout(name, shape):
            return nc.dram_tensor(name, list(shape), F32, kind="ExternalOutput").ap()

        def dint(name, shape):
            return nc.dram_tensor(name, list(shape), BF16, kind="Internal").ap()

        self.xp = din("xp", [NB, S, D])
        self.xs = din("xs", [DS, D])
        self.ck = din("ck", [L, PAST, 512])
        self.cv = din("cv", [L, PAST, 512])
        self.cmk = din("cmk", [L, MEM, D])
        self.cmv = din("cmv", [L, MEM, D])
        self.memp = din("memp", [NB, MEM, D])
        self.wf = {
            "w_in": din("w_in", [L, D, 2560]), "w_out": din("w_out", [L, D, D]),
            "wq": din("wq", [L, D, D]), "wk": din("wk", [L, D, D]), "wv": din("wv", [L, D, D]),
            "wo": din("wo", [L, D, D]), "w_up": din("w_up", [L, D, 2 * DFF]), "w_down": din("w_down", [L, DFF, D]),
        }
        self.wb = {k: dint("b_" + k, v.shape) for k, v in self.wf.items()}
        self.rep = din("rep", [L, 128, NREP])
        self.fm = din("fm", [L, 128, NFM])
        self.wst = din("wst", [L, 128, 512])
        self.ident_d = din("ident", [128, 128])
        self.tril_d = din("tril", [128, 128])
        self.csp_d = din("csp", [128, 16, 16])
        self.css_d = din("css", [DS, 16])
        self.convst_d = din("convst", [128, L, NFC, 2])
        self.yp = dout("yp", [NB, S, D])
        self.ys = dout("ys", [DS, D])
        self.dkp = dout("dkp", [L, NB, S, 512])
        self.dvp = dout("dvp", [L, NB, S, 512])
        self.mkp = dout("mkp", [L, NB, MEM, D])
        self.mvp = dout("mvp", [L, NB, MEM, D])
        self.fcp = dout("fcp", [L, NB, 2, DFF])
        self.dks = dout("dks", [L, DS, 512])
        self.dvs = dout("dvs", [L, DS, 512])
        self.gvs = dout("gvs", [L, DS, 512])
        self.fcs = dout("fcs", [L, 2, DFF])

        ARENA_BYTES = 212800
        self.arena = nc.alloc_sbuf_tensor("arena", [128, ARENA_BYTES // 2], BF16)
        self.ps = nc.alloc_psum_tensor("ps", [128, 8, 512], F32)
        self.X_OFF = 0
        self.HB_OFF = 65536
        self.KT_OFF = 98304
        self.VA_OFF = 114688
        self.W_OFF = 131328
        self.C_OFF = 188672
        self.cast_tok = {}
        self.ps_rr = {"mm": 0, "pair": 0, "acc": 0}
        self.lctr = 0
        self.build()

    def V(self, off, shape, dt, P=128):
        n = int(np.prod(shape)) * _esize(dt)
        assert off % 4 == 0
        v = self.arena[0:P, off // 2:(off + n) // 2]
        if dt != BF16:
            v = v.bitcast(dt)
        if len(shape) == 2:
            v = v.rearrange("p (a b) -> p a b", a=shape[0])
        elif len(shape) == 3:
            v = v.rearrange("p (a b c) -> p a b c", a=shape[0], b=shape[1])
        return v

    def wslot(self, i, shape):
        return self.V(self.W_OFF + i * 8192, shape, BF16)

    def ps_mm(self):
        i = self.ps_rr["mm"]
        self.ps_rr["mm"] = (i + 1) % 4
        return self.ps[:, i, :]

    def ps_pair(self):
        i = self.ps_rr["pair"]
        self.ps_rr["pair"] = (i + 1) % 2
        return self.ps[:, 2 * i:2 * i + 2, :]

    def ps_tr(self, n, w):
        banks = getattr(self, "tr_banks", [7])
        self.tr_i = getattr(self, "tr_i", 0) + 1
        bk = banks[self.tr_i % len(banks)]
        return self.ps[:, bk, 0:(n * w) // 2].bitcast(BF16).rearrange("p (a b) -> p a b", a=n)

    def rstd(self, out, ss, dim, P):
        kb = self.kb
        kb.act(out, ss, AF.Ln, scale=1.0 / dim, bias=EPS)
        kb.act(out, out, AF.Exp, scale=-0.5)

    def build(self):
        kb = self.kb
        c = self.C_OFF
        self.identf = self.V(c, [128], F32); c += 512
        self.identb = self.V(c, [128], BF16); c += 256
        self.tril = self.V(c, [128], F32); c += 512
        self.csp = self.V(c, [16, 16], F32); c += 1024
        self.css = self.V(c, [16], F32); c += 64
        self.lslot = []
        for i in range(2):
            d = {}
            d["rep"] = self.V(c, [NREP], F32); c += NREP * 4
            d["fm"] = self.V(c, [NFM], F32); c += NFM * 4
            d["wsT"] = self.V(c, [4, 128], BF16); c += 1024
            d["lam"] = self.V(c, [16], F32); c += 64
            self.lslot.append(d)
        self.mkT = self.V(c, [8, MEM], BF16); c += 4096
        self.MVA = self.V(c, [2, 4, 258], BF16); c += 4128
        self.stat = self.V(c, [128], F32); c += 512
        assert c <= 212800, c

        kb.dma("sp", self.identf, self.ident_d[:, :])
        kb.dma("sp", self.tril, self.tril_d[:, :])
        kb.dma("sp", self.csp, self.csp_d[:, :, :])
        kb.dma("sp", self.css[0:DS], self.css_d[:, :])
        kb.cp("dve", self.identb, self.identf)
        kb.memset("dve", self.MVA[:, :, :, 256:258], 1.0)

        for l in range(L):
            for name in ["wk", "wv", "w_in", "w_out", "wq", "wo", "w_up", "w_down"]:
                src = self.wf[name]
                dst = self.wb[name]
                rows = src.shape[1]
                toks = []
                for r0 in range(0, rows, 128):
                    toks.append(kb.dma("pool", dst[l, r0:r0 + 128, :], src[l, r0:r0 + 128, :]))
                self.cast_tok[(name, l)] = toks

        import os
        self.dbg = int(os.environ.get("K_DBG", "9"))
        self.dbg_nl = int(os.environ.get("K_NL", str(L)))
        seqs = [Seq("p", b) for b in range(NB)] + [Seq("s", 0)]
        sel = os.environ.get("K_SEQS")
        if sel is not None:
            seqs = [seqs[int(i)] for i in sel.split(",")]
        if os.environ.get("K_NOCAST"):
            pass
        self.va_ones_done = None
        for sq in seqs:
            self.run_seq(sq)
        kb.finish()

    def load_panel(self, slot, name, l, rows, cols, shape):
        src = self.wb[name][l, rows[0]:rows[1], cols[0]:cols[1]].rearrange("(k p) n -> p k n", p=128)
        dst = self.wslot(slot, shape)
        self.kb.dma("sp", dst, src, extra_wait=self.cast_tok[(name, l)])
        return dst

    def run_seq(self, sq):
        kb = self.kb
        TP, NT = sq.TP, sq.NT
        self.X = self.V(self.X_OFF, [16, D], F32)
        for t in range(NT):
            if sq.kind == "p":
                kb.dma("sp", self.X[:, t, :], self.xp[sq.b, t * 128:(t + 1) * 128, :])
            else:
                kb.dma("sp", self.X[0:TP, 0, :], self.xs[:, :])
        self.KT = self.V(self.KT_OFF, [4, sq.SK], BF16)
        self.VA = self.V(self.VA_OFF, [sq.NKT, 4, 130], BF16)
        for l in range(self.dbg_nl):
            self.run_layer(sq, l)
        for t in range(NT):
            if sq.kind == "p":
                kb.dma("sp", self.yp[sq.b, t * 128:(t + 1) * 128, :], self.X[:, t, :])
            else:
                kb.dma("sp", self.ys[:, :], self.X[0:TP, 0, :])

    def norm_transpose(self, sq, xrow, gcol, outT, xn, sqr, P, sc=0):
        kb = self.kb
        ss = self.stat[0:P, sc:sc + 1]
        rs = self.stat[0:P, sc + 1:sc + 2]
        kb.act(sqr, xrow, AF.Square, accum_out=ss)
        self.rstd(rs, ss, D, P)
        kb.ts("dve", xn, xrow, rs, ALU.mult)
        pt = self.ps_tr(8, P)
        for kc in range(8):
            kb.tr(pt[:, kc, :], xn[:, kc * 128:(kc + 1) * 128], self.identb[0:P, 0:P])
        kb.tt("dve", outT, pt, gcol.unsqueeze(2).broadcast_to([128, 8, P]), ALU.mult)

    def run_layer(self, sq, l):
        kb = self.kb
        TP, NT = sq.TP, sq.NT
        isp = sq.kind == "p"
        slot = self.lslot[self.lctr % 2]
        self.lctr += 1
        lam_init = 0.8 - 0.6 * float(np.exp(-0.3 * l))
        HB = self.HB_OFF

        kb.dma("sp", slot["rep"], self.rep[l, :, :])
        kb.dma("sp", slot["fm"], self.fm[l, :, :])
        wst_f = self.V(HB, [4, 128], F32)
        kb.dma("sp", wst_f, self.wst[l, :, :].rearrange("p (g t) -> p g t", g=4))
        rep = slot["rep"]
        g_qk = rep[:, 0:128].rearrange("p (a d) -> p a d", a=2)
        g_sub = rep[:, 128:256]
        g_gmn = rep[:, 256:384]
        g_xq = rep[:, 384:640]
        g_xk = rep[:, 640:896]
        lamv = rep[:, 896:1152].rearrange("p (a d) -> p a d", a=4)
        fmv = slot["fm"]
        gfm = fmv[:, 0:32].rearrange("p (a k) -> p a k", a=4)
        gm_b = fmv[:, 32:36]
        conv_w = fmv[:, 36:102].rearrange("p (f j) -> p f j", f=NFC)
        conv_b = fmv[:, 102:124]
        lam = slot["lam"]
        kb.tt("dve", slot["wsT"], wst_f, self.tril.unsqueeze(1).broadcast_to([128, 4, 128]), ALU.mult)
        prod = self.V(HB + 2048, [2, 64], F32)
        kb.tt("dve", prod[:, 0, :], lamv[:, 0, :], lamv[:, 1, :], ALU.mult)
        kb.tt("dve", prod[:, 1, :], lamv[:, 2, :], lamv[:, 3, :], ALU.mult)
        kb.reduce_add(lam[:, 0:2], prod)
        kb.act(lam[:, 0:2], lam[:, 0:2], AF.Exp)
        kb.tt("dve", lam[:, 2:3], lam[:, 1:2], lam[:, 0:1], ALU.subtract)
        kb.ts("dve", lam[:, 3:4], lam[:, 2:3], -lam_init, ALU.add)
        kb.ts("dve", g_sub, g_sub, 1.0 - lam_init, ALU.mult)

        if isp:
            wk_p = [self.load_panel(3 + i, "wk", l, (0, D), (i * 512, (i + 1) * 512), [8, 512]) for i in range(2)]
            wv_p = [self.load_panel(5 + i, "wv", l, (0, D), (i * 512, (i + 1) * 512), [8, 512]) for i in range(2)]
        win = [None] * 5
        for i in range(3):
            win[i] = self.load_panel(i, "w_in", l, (0, D), (i * 512, (i + 1) * 512), [8, 512])

        if self.dbg < 2:
            return
        self.stage_mem(sq, l, slot, gfm, g_xk, wk_p if isp else None, wv_p if isp else None)

        if self.dbg < 3:
            return
        for i in range(3, 5):
            win[i] = self.load_panel(i, "w_in", l, (0, D), (i * 512, (i + 1) * 512), [8, 512])
        woA = self.load_panel(5, "w_out", l, (0, 512), (0, D), [4, D])
        woB = self.load_panel(6, "w_out", l, (512, D), (0, D), [4, D])

        if not os.environ.get("K_NOMEMSET"):
            kb.memset("dve", self.VA[:, :, :, 128:130], 1.0)
        if not isp:
            self.load_past(sq, l)

        QT = self.V(HB + 22528, [4, 512], BF16)
        cw = {}

        def after_last_proj():
            cw["wq"] = [self.load_panel(i, "wq", l, (0, D), (i * 512, (i + 1) * 512), [8, 512]) for i in range(2)]
            cw["wo"] = [self.load_panel(2 + i, "wo", l, (0, D), (i * 512, (i + 1) * 512), [8, 512]) for i in range(2)]
        self.stage_ab(sq, l, slot, gfm, g_qk, g_gmn, gm_b, g_sub, lam, win, woA, woB, QT, after_last_proj)
        wq_p, wo_p = cw["wq"], cw["wo"]

        if self.dbg < 4:
            return
        fpan = {}
        fpan[0] = self.load_ffn_group(l, 0, 4)

        self.stage_c(sq, l, gfm, g_xq, wq_p, wo_p)

        if self.dbg < 5:
            return
        fpan[1] = self.load_ffn_group(l, 1, 0)

        self.stage_ffn(sq, l, conv_w, conv_b, fpan)

    def load_ffn_group(self, l, g, s0):
        nf = 4 if g < 5 else 2
        c0 = g * 512
        gate = self.load_panel(s0, "w_up", l, (0, D), (c0, c0 + nf * 128), [8, nf * 128])
        up = self.load_panel(s0 + 1, "w_up", l, (0, D), (DFF + c0, DFF + c0 + nf * 128), [8, nf * 128])
        down = self.load_panel(s0 + 2, "w_down", l, (c0, c0 + nf * 128), (0, D), [nf, D])
        return gate, up, down, nf

    def stage_mem(self, sq, l, slot, gfm, g_xk, wk_p, wv_p):
        kb = self.kb
        HB = self.HB_OFF
        isp = sq.kind == "p"
        memt = self.V(HB + 4096, [D], F32)
        xn = self.V(HB + 8192, [D], BF16)
        mT = self.V(HB + 10240, [8, 128], BF16)
        kst = self.V(HB + 12288, [D], F32)
        kbf = self.V(HB + 16384, [D], BF16)
        vst = self.V(HB + 18432, [D], F32)
        sqr = self.V(HB + 22528, [D], F32)
        ss4 = self.stat[:, 4:8]
        rs4 = self.stat[:, 8:12]
        for mt in range(2):
            if isp:
                kb.dma("sp", memt, self.memp[sq.b, mt * 128:(mt + 1) * 128, :])
                self.norm_transpose(sq, memt, gfm[:, 3, :], mT, xn, sqr, 128)
                pk = self.ps_pair()
                for cb in range(2):
                    for kc in range(8):
                        kb.mm(pk[:, cb, :], lhsT=mT[:, kc, :], rhs=wk_p[cb][:, kc, :], start=(kc == 0), stop=(kc == 7))
                pk2 = pk.rearrange("p a b -> p (a b)")
                kb.act(sqr, pk2, AF.Square)
                kb.reduce_add(ss4, sqr.rearrange("p (h d) -> p h d", h=4))
                self.rstd(rs4, ss4, 256, 128)
                k3 = kst.rearrange("p (h d) -> p h d", h=4)
                kb.tt("dve", k3, pk2.rearrange("p (h d) -> p h d", h=4),
                      rs4.unsqueeze(2).broadcast_to([128, 4, 256]), ALU.mult)
                kb.tt("dve", k3, k3, g_xk.unsqueeze(1).broadcast_to([128, 4, 256]), ALU.mult)
                kb.dma("sp", self.mkp[l, sq.b, mt * 128:(mt + 1) * 128, :], kst)
                kb.cp("act", kbf, kst)
                pv = self.ps_pair()
                for cb in range(2):
                    for kc in range(8):
                        kb.mm(pv[:, cb, :], lhsT=mT[:, kc, :], rhs=wv_p[cb][:, kc, :], start=(kc == 0), stop=(kc == 7))
                pv2 = pv.rearrange("p a b -> p (a b)")
                kb.cp("act", vst, pv2)
                kb.dma("sp", self.mvp[l, sq.b, mt * 128:(mt + 1) * 128, :], vst)
                kb.cp("dve", self.MVA[:, mt, :, 0:256], vst.rearrange("p (h d) -> p h d", h=4))
            else:
                kb.dma("sp", kst, self.cmk[l, mt * 128:(mt + 1) * 128, :])
                kb.cp("act", kbf, kst)
                kb.dma("sp", vst, self.cmv[l, mt * 128:(mt + 1) * 128, :])
                kb.cp("dve", self.MVA[:, mt, :, 0:256], vst.rearrange("p (h d) -> p h d", h=4))
            pt = self.ps_tr(8, 128)
            for kc in range(8):
                kb.tr(pt[:, kc, :], kbf[:, kc * 128:(kc + 1) * 128], self.identb)
            kb.cp("dve", self.mkT[:, :, mt * 128:(mt + 1) * 128], pt)

    def load_past(self, sq, l):
        kb = self.kb
        HB = self.HB_OFF
        st = self.V(HB + 4096, [512], F32)
        sb = self.V(HB + 6144, [512], BF16)
        for kt in range(8):
            kb.dma("sp", st, self.ck[l, kt * 128:(kt + 1) * 128, :])
            kb.cp("act", sb, st)
            pt = self.ps_tr(4, 128)
            for h in range(4):
                kb.tr(pt[:, h, :], sb[:, h * 128:(h + 1) * 128], self.identb)
            kb.cp("dve", self.KT[:, :, kt * 128:(kt + 1) * 128], pt)
            st2 = self.V(HB + 8192, [512], F32)
            kb.dma("sp", st2, self.cv[l, kt * 128:(kt + 1) * 128, :])
            kb.cp("dve", self.VA[:, kt, :, 0:128], st2.rearrange("p (h d) -> p h d", h=4))

    def a_bufs(self, sq):
        P = sq.TP
        HB = self.HB_OFF
        d = {}
        d["sqr"] = self.V(HB, [D], F32, P)
        d["rope"] = self.V(HB, [4, 128], F32, P)
        d["zqk"] = self.V(HB + 4096, [D], F32, P)
        d["zv"] = self.V(HB + 8192, [512], F32, P)
        d["qkb"] = self.V(HB + 10240, [D], BF16, P)
        d["ug"] = self.V(HB + 12288, [512], F32, P)
        d["gvb"] = self.V(HB + 14336, [512], BF16, P)
        d["xn"] = self.V(HB + 16384, [D], BF16, P)
        d["hT"] = self.V(HB + 18432, [8, P], BF16)
        d["gob"] = self.V(HB + 20480, [512], BF16, P)
        d["goT"] = self.V(HB + 21504, [4, P], BF16)
        return d

    def stage_ab(self, sq, l, slot, gfm, g_qk, g_gmn, gm_b, g_sub, lam, win, woA, woB, QT, after_last_proj):
        kb = self.kb
        P = sq.TP
        NT = sq.NT
        isp = sq.kind == "p"
        bf = self.a_bufs(sq)
        sqr, rope, zqk, zv, qkb, ug, gvb, xn, hT, gob, goT = (bf[k] for k in
            ["sqr", "rope", "zqk", "zv", "qkb", "ug", "gvb", "xn", "hT", "gob", "goT"])
        gg = sqr[:, 0:512]
        sq2 = sqr[:, 512:1024]

        def a0(t):
            self.norm_ew(self.X[0:P, t, :], xn, sqr, P, 0)

        def a1(t):
            self.tr_evac(xn, gfm[:, 0, :], hT, P)

        def a2(t):
            cs = self.csp[:, t, :] if isp else self.css[0:P, :]
            cosv = cs[:, 0:8]
            sinv = cs[:, 8:16]
            pqk = self.ps_pair()
            for which in range(2):
                for kc in range(8):
                    kb.mm(pqk[0:P, which, :], lhsT=hT[:, kc, :], rhs=win[which][:, kc, :], start=(kc == 0), stop=(kc == 7))
            pz3 = []
            for i, cg in enumerate((2, 3, 4)):
                pz = self.ps[0:P, 4 + i, :]
                for kc in range(8):
                    kb.mm(pz, lhsT=hT[:, kc, :], rhs=win[cg][:, kc, :], start=(kc == 0), stop=(kc == 7))
                pz3.append(pz)
            pv_, pu, pg = pz3
            if t == NT - 1:
                after_last_proj()
            ss16 = self.stat[0:P, 16:32]
            rs16 = self.stat[0:P, 96:112]
            pqk2 = pqk[0:P].rearrange("p a b -> p (a b)")
            kb.act(sqr, pqk2, AF.Square)
            kb.reduce_add(ss16, sqr.rearrange("p (g d) -> p g d", g=16))
            self.rstd(rs16, ss16, 64, P)
            kb.cp("act", zv, pv_)
            kb.act(ug, pu, AF.Gelu_apprx_tanh)
            d3 = zqk.rearrange("p (g d) -> p g d", g=16)
            kb.tt("dve", d3, pqk2.rearrange("p (g d) -> p g d", g=16), rs16.unsqueeze(2).broadcast_to([P, 16, 64]), ALU.mult)
            d4 = zqk.rearrange("p (a g d) -> p a g d", a=2, g=8)
            kb.tt("dve", d4, d4, g_qk[0:P].unsqueeze(2).broadcast_to([P, 2, 8, 64]), ALU.mult)
            x1 = d3[:, :, 0:8]
            x2 = d3[:, :, 8:16]
            cb_ = cosv.unsqueeze(1).broadcast_to([P, 16, 8])
            sb_ = sinv.unsqueeze(1).broadcast_to([P, 16, 8])
            r3 = rope.rearrange("p a (g d) -> p a g d", g=16)
            kb.tt("dve", r3[:, 0], x1, cb_, ALU.mult)
            kb.tt("dve", r3[:, 1], x2, sb_, ALU.mult)
            kb.tt("dve", r3[:, 2], x2, cb_, ALU.mult)
            kb.tt("dve", r3[:, 3], x1, sb_, ALU.mult)
            kb.tt("dve", x1, r3[:, 0], r3[:, 1], ALU.subtract)
            kb.tt("dve", x2, r3[:, 2], r3[:, 3], ALU.add)
            kb.cp("act", qkb, zqk)
            zk = zqk[:, 512:1024]
            if isp:
                kb.dma("sp", self.dkp[l, sq.b, t * 128:(t + 1) * 128, :], zk)
                kb.dma("sp", self.dvp[l, sq.b, t * 128:(t + 1) * 128, :], zv)
            else:
                kb.dma("sp", self.dks[l, :, :], zk)
                kb.dma("sp", self.dvs[l, :, :], zv)
            vt = t if isp else 8
            kb.cp("dve", self.VA[0:P, vt, :, 0:128], zv.rearrange("p (h d) -> p h d", h=4))
            kb.act(gg, pg, AF.Gelu_apprx_tanh)
            kb.act(sq2, gg, AF.Square)
            ss4 = self.stat[0:P, 4:8]
            rs4 = self.stat[0:P, 8:12]
            kb.reduce_add(ss4, sq2.rearrange("p (g d) -> p g d", g=4))
            self.rstd(rs4, ss4, 128, P)
            gg3 = gg.rearrange("p (g d) -> p g d", g=4)
            kb.tt("dve", gg3, gg3, rs4.unsqueeze(2).broadcast_to([P, 4, 128]), ALU.mult)
            kb.tt("dve", gg3, gg3, g_gmn[0:P, :].unsqueeze(1).broadcast_to([P, 4, 128]), ALU.mult)
            if not isp:
                kb.dma("sp", self.gvs[l, :, :], gg)
            kb.cp("act", gvb, gg)

        def a3(t):
            ti = t % sq.TPB
            pt = self.ps_tr(8, P)
            for j in range(8):
                kb.tr(pt[:, j, :], qkb[:, j * 128:(j + 1) * 128], self.identb[0:P, 0:P])
            pgate = self.ps_mm()[0:P, :]
            for g in range(4):
                kb.mm(pgate[:, g * 128:(g + 1) * 128], lhsT=slot["wsT"][0:P, g, 0:P], rhs=gvb[:, g * 128:(g + 1) * 128])
            kb.cp("dve", QT[:, :, ti * 128:ti * 128 + P], pt[:, 0:4, :])
            kpos = t * 128 if isp else PAST
            kb.cp("act", self.KT[:, :, kpos:kpos + P], pt[:, 4:8, :])
            go3 = sq2.rearrange("p (g d) -> p g d", g=4)
            kb.tt("dve", go3, pgate.rearrange("p (g d) -> p g d", g=4),
                  gm_b[0:P, :].unsqueeze(2).broadcast_to([P, 4, 128]), ALU.add)
            kb.tt("dve", gob, sq2, ug, ALU.mult)
            if ti == sq.TPB - 1:
                self.stage_b_block(sq, l, t // sq.TPB, slot, g_sub, lam, woA, QT)

        def a4(t):
            pt2 = self.ps_tr(4, P)
            for j in range(4):
                kb.tr(pt2[:, j, :], gob[:, j * 128:(j + 1) * 128], self.identb[0:P, 0:P])
            kb.cp("act", goT, pt2)

        def a5(t):
            xrow = self.X[0:P, t, :]
            py = self.ps_pair()
            for cb in range(2):
                for kc in range(4):
                    kb.mm(py[0:P, cb, :], lhsT=goT[:, kc, :], rhs=woB[:, kc, cb * 512:(cb + 1) * 512],
                          start=(kc == 0), stop=(kc == 3))
            kb.tt("dve", xrow, py[0:P].rearrange("p a b -> p (a b)"), xrow, ALU.add)

        phases = [a0, a1, a2, a3, a4, a5]
        for i in range(NT + len(phases) - 1):
            for k in reversed(range(len(phases))):
                t = i - k
                if 0 <= t < NT:
                    phases[k](t)

    def stage_b_block(self, sq, l, blk, slot, g_sub, lam, woA, QT):
        kb = self.kb
        HB = self.HB_OFF
        isp = sq.kind == "p"
        P = sq.TP
        QB = sq.QB
        eTs = [self.V(HB + 26624 + i * 2048, [2, 512], BF16) for i in range(2)]
        OB = self.V(HB + 26624, [4, 512], BF16)
        otmp = self.V(HB + 30720, [128], F32)
        oT = self.V(HB + 31232, [4, 128], BF16)
        OF = self.V(HB, [4, 512], F32)
        osq = self.V(HB + 8192, [512], F32)
        accS = self.V(HB + 10240, [3, 480], F32)
        nqs = sq.TPB
        if isp:
            nkt = 4 * blk + 4
        else:
            nkt = 9
        rz = self.stat[0:P, 32:48]
        ssq = self.stat[0:P, 48:49]
        rso = self.stat[0:P, 49:50]
        nl = self.stat[0:P, 50:51]
        def acc(qs, m):
            r = qs * 2 + m
            return self.ps[0:P, 4 + r // 3, (r % 3) * 160:(r % 3) * 160 + 129]

        def accs(qs, m):
            r = qs * 2 + m
            return accS[0:P, r // 3, (r % 3) * 160:(r % 3) * 160 + 129]

        steps = [(h, kt) for h in range(4) for kt in range(nkt)]

        def qk_exp(i):
            h, kt = steps[i]
            KP = 128 if (isp or kt < 8) else DS
            j = kt - 4 * blk if isp else -1
            q0 = max(0, j) * 128
            kpos = kt * 128
            psS = self.ps_pair()
            for m in range(2):
                kb.mm(psS[0:KP, m, q0:QB], lhsT=self.KT[m * 64:(m + 1) * 64, h, kpos:kpos + KP],
                      rhs=QT[m * 64:(m + 1) * 64, h, q0:QB])
            eT = eTs[i % 2]
            kb.act(eT[0:KP, :, q0:QB], psS[0:KP, :, q0:QB], AF.Exp, scale=0.125)
            if j >= 0:
                kb.memset("dve", eT[64:128, :, q0:q0 + 64], 0.0)

        def pv(i):
            h, kt = steps[i]
            KP = 128 if (isp or kt < 8) else DS
            j = kt - 4 * blk if isp else -1
            eT = eTs[i % 2]
            for qs in range(max(0, j), nqs):
                last = (kt == 4 * blk + qs) if isp else (kt == nkt - 1)
                for m in range(2):
                    kb.mm(acc(qs, m), lhsT=eT[0:KP, m, qs * 128:qs * 128 + P], rhs=self.VA[0:KP, kt, h, 0:129],
                          start=(kt == 0 and (qs * 2 + m) % 3 == 0), stop=last)
            if kt == nkt - 1:
                nreg = 2 * nqs
                for bk in range((nreg + 2) // 3):
                    w_ = min(3, nreg - 3 * bk) * 160
                    kb.cp("act", accS[0:P, bk, 0:w_], self.ps[0:P, 4 + bk, 0:w_])
                for qs in range(nqs):
                    a0 = accs(qs, 0)
                    a1 = accs(qs, 1)
                    kb.recip(rz[:, 0:1], a0[:, 128:129])
                    kb.recip(rz[:, 1:2], a1[:, 128:129])
                    kb.tt("dve", nl, rz[:, 1:2], lam[0:P, 3:4], ALU.mult)
                    ot = otmp[0:P, :]
                    kb.ts("dve", ot, a0[:, 0:128], rz[:, 0:1], ALU.mult)
                    kb.stt("dve", OF[0:P, qs, h * 128:(h + 1) * 128], a1[:, 0:128], nl, ot, ALU.mult, ALU.add)

        qk_exp(0)
        for i in range(len(steps)):
            if i + 1 < len(steps):
                qk_exp(i + 1)
            pv(i)
        ss4 = self.stat[0:P, 4:8]
        rs4 = self.stat[0:P, 8:12]
        for qs in range(nqs):
            t = blk * sq.TPB + qs
            of = OF[0:P, qs, :]
            kb.act(osq[0:P, :], of, AF.Square)
            kb.reduce_add(ss4, osq[0:P, :].rearrange("p (h d) -> p h d", h=4))
            self.rstd(rs4, ss4, 128, P)
            of3 = of.rearrange("p (h d) -> p h d", h=4)
            kb.tt("dve", of3, of3, rs4.unsqueeze(2).broadcast_to([P, 4, 128]), ALU.mult)
            kb.tt("dve", OB[0:P, qs, :].rearrange("p (h d) -> p h d", h=4), of3,
                  g_sub[0:P, :].unsqueeze(1).broadcast_to([P, 4, 128]), ALU.mult)
            pt = self.ps_tr(4, P)
            for j in range(4):
                kb.tr(pt[:, j, :], OB[0:P, qs, j * 128:(j + 1) * 128], self.identb[0:P, 0:P])
            kb.cp("act", oT[:, :, 0:P], pt)
            py = self.ps_pair()
            for cb in range(2):
                for kc in range(4):
                    kb.mm(py[0:P, cb, :], lhsT=oT[:, kc, 0:P], rhs=woA[:, kc, cb * 512:(cb + 1) * 512],
                          start=(kc == 0), stop=(kc == 3))
            xrow = self.X[0:P, t, :]
            kb.tt("dve", xrow, py[0:P].rearrange("p a b -> p (a b)"), xrow, ALU.add)

    def norm_ew(self, xrow, xn, sqr, P, sc):
        kb = self.kb
        ss = self.stat[0:P, sc:sc + 1]
        rs = self.stat[0:P, sc + 1:sc + 2]
        kb.act(sqr, xrow, AF.Square, accum_out=ss)
        self.rstd(rs, ss, D, P)
        kb.ts("dve", xn, xrow, rs, ALU.mult)

    def tr_evac(self, xn, gcol, outT, P):
        kb = self.kb
        pt = self.ps_tr(8, P)
        for kc in range(8):
            kb.tr(pt[:, kc, :], xn[:, kc * 128:(kc + 1) * 128], self.identb[0:P, 0:P])
        kb.tt("dve", outT, pt, gcol.unsqueeze(2).broadcast_to([128, 8, P]), ALU.mult)

    def stage_c(self, sq, l, gfm, g_xq, wq_p, wo_p):
        kb = self.kb
        P = sq.TP
        NT = sq.NT
        o = self.KT_OFF
        xn1 = self.V(o, [D], BF16, P); o += 2048
        sqj = self.V(o, [D], F32, P); o += 4096
        h2T = self.V(o, [8, P], BF16); o += 2048
        sqq = self.V(o, [D], F32, P); o += 4096
        qcn = self.V(o, [D], BF16, P); o += 2048
        qcT = self.V(o, [8, P], BF16); o += 2048
        eT = self.V(o, [4, 2, P], BF16); o += 2048
        ocb = self.V(o, [D], BF16, P); o += 2048
        ocT = self.V(o, [8, P], BF16); o += 2048
        xn3 = self.V(o, [D], BF16, P); o += 2048
        assert o <= self.KT_OFF + 33024, o
        ss4 = self.stat[0:P, 4:8]
        rs4 = self.stat[0:P, 8:12]
        rz4 = self.stat[0:P, 68:72]
        HBv = self.V(self.HB_OFF, [8, sq.S], BF16)

        def f0(t):
            self.norm_ew(self.X[0:P, t, :], xn1, sqj, P, 0)

        def f1(t):
            self.tr_evac(xn1, gfm[:, 1, :], h2T, P)

        def f2(t):
            pq = self.ps_pair()
            for cb in range(2):
                for kc in range(8):
                    kb.mm(pq[0:P, cb, :], lhsT=h2T[:, kc, :], rhs=wq_p[cb][:, kc, :], start=(kc == 0), stop=(kc == 7))
            pq2 = pq[0:P].rearrange("p a b -> p (a b)")
            kb.act(sqq, pq2, AF.Square)
            kb.reduce_add(ss4, sqq.rearrange("p (h d) -> p h d", h=4))
            self.rstd(rs4, ss4, 256, P)
            s3 = sqq.rearrange("p (h d) -> p h d", h=4)
            kb.tt("dve", s3, pq2.rearrange("p (h d) -> p h d", h=4), rs4.unsqueeze(2).broadcast_to([P, 4, 256]), ALU.mult)
            kb.tt("dve", qcn.rearrange("p (h d) -> p h d", h=4), s3, g_xq[0:P, :].unsqueeze(1).broadcast_to([P, 4, 256]),
                  ALU.mult)

        def f3(t):
            pt = self.ps_tr(8, P)
            for j in range(8):
                kb.tr(pt[:, j, :], qcn[:, j * 128:(j + 1) * 128], self.identb[0:P, 0:P])
            kb.cp("act", qcT, pt)

        def f4(t):
            psS = self.ps_pair()
            for h in range(4):
                for mt in range(2):
                    c0 = (h * 2 + mt) * 128
                    dst = psS[:, c0 // 512, (c0 % 512):(c0 % 512) + P]
                    for dc in range(2):
                        kb.mm(dst, lhsT=self.mkT[:, h * 2 + dc, mt * 128:(mt + 1) * 128], rhs=qcT[:, h * 2 + dc, :],
                              start=(dc == 0), stop=(dc == 1))
            for half in range(2):
                kb.act(eT[:, half * 2:half * 2 + 2, :, :],
                       psS[:, half, :].rearrange("p (h m q) -> p h m q", h=2, m=2)[:, :, :, 0:P], AF.Exp, scale=1.0 / 16.0)

        def f5(t):
            for h in range(4):
                po = self.ps[0:P, 4 + (h % 2), 0:257]
                for mt in range(2):
                    kb.mm(po, lhsT=eT[:, h, mt, :], rhs=self.MVA[:, mt, h, 0:257], start=(mt == 0), stop=(mt == 1))
                kb.recip(rz4[:, h:h + 1], po[:, 256:257])
                kb.ts("dve", ocb[:, h * 256:(h + 1) * 256], po[:, 0:256], rz4[:, h:h + 1], ALU.mult)

        def f6(t):
            pt2 = self.ps_tr(8, P)
            for j in range(8):
                kb.tr(pt2[:, j, :], ocb[:, j * 128:(j + 1) * 128], self.identb[0:P, 0:P])
            kb.cp("act", ocT, pt2)

        def f7(t):
            xrow = self.X[0:P, t, :]
            py = self.ps_pair()
            for cb in range(2):
                for kc in range(8):
                    kb.mm(py[0:P, cb, :], lhsT=ocT[:, kc, :], rhs=wo_p[cb][:, kc, :], start=(kc == 0), stop=(kc == 7))
            kb.tt("dve", xrow, py[0:P].rearrange("p a b -> p (a b)"), xrow, ALU.add)
            self.norm_ew(xrow, xn3, sqj, P, 64)

        def f8(t):
            self.tr_evac(xn3, gfm[:, 2, :], HBv[:, :, t * 128:t * 128 + P], P)

        phases = [f0, f1, f2, f3, f4, f5, f6, f7, f8]
        self.tr_banks = [7, 6]
        for i in range(NT + len(phases) - 1):
            for k in reversed(range(len(phases))):
                t = i - k
                if 0 <= t < NT:
                    phases[k](t)
        self.tr_banks = [7]

    def stage_ffn(self, sq, l, conv_w, conv_b, fpan):
        kb = self.kb
        isp = sq.kind == "p"
        P = sq.TP
        NTOK = sq.QB
        HBv = self.V(self.HB_OFF, [8, sq.S], BF16)
        o = self.KT_OFF
        gss = [self.V(o + i * 2064, [516], F32) for i in range(2)]; o += 4128
        ccs = [self.V(o + i * 2048, [512], F32) for i in range(2)]; o += 4096
        scs = [self.V(o + i * 2048, [512], F32) for i in range(2)]; o += 4096
        aTs = [self.V(o + i * 4096, [4, 512], BF16) for i in range(2)]; o += 8192
        carry = self.V(o, [NFC, 2], F32); o += 176
        cst = self.V(o, [512], F32, 2); o += 2048
        assert o <= self.KT_OFF + 33024
        if isp:
            kb.memset("dve", carry, 0.0)
        else:
            kb.dma("sp", carry, self.convst_d[:, l, :, :])
        ri = 0
        ai = 0
        pending = None
        for g in range(6):
            gate, up, down, nf = fpan[g]
            for blk in range(sq.NBLK):
                aT = aTs[ai % 2]
                ai += 1
                for fl in range(nf):
                    fc = g * 4 + fl
                    pg = self.ps_mm()
                    for kc in range(8):
                        kb.mm(pg[:, 0:NTOK], lhsT=gate[:, kc, fl * 128:(fl + 1) * 128],
                              rhs=HBv[:, kc, blk * 512:blk * 512 + NTOK], start=(kc == 0), stop=(kc == 7))
                    pu = self.ps_mm()
                    for kc in range(8):
                        kb.mm(pu[:, 0:NTOK], lhsT=up[:, kc, fl * 128:(fl + 1) * 128],
                              rhs=HBv[:, kc, blk * 512:blk * 512 + NTOK], start=(kc == 0), stop=(kc == 7))
                    gs = gss[ri % 2]
                    cc = ccs[ri % 2]
                    sc = scs[ri % 2]
                    ri += 1
                    kb.cp("act", gs[:, 2:2 + NTOK], pg[:, 0:NTOK])
                    kb.cp("dve", gs[:, 0:2], carry[:, fc, :])
                    kb.cp("dve", carry[:, fc, :], gs[:, NTOK:NTOK + 2])
                    kb.act(cc[:, 0:NTOK], pg[:, 0:NTOK], AF.Identity, bias=conv_b[:, fc:fc + 1], scale=conv_w[:, fc, 2:3])
                    kb.stt("dve", cc[:, 0:NTOK], gs[:, 1:1 + NTOK], conv_w[:, fc, 1:2], cc[:, 0:NTOK], ALU.mult, ALU.add)
                    kb.stt("dve", cc[:, 0:NTOK], gs[:, 0:NTOK], conv_w[:, fc, 0:1], cc[:, 0:NTOK], ALU.mult, ALU.add)
                    kb.act(sc[:, 0:NTOK], cc[:, 0:NTOK], AF.Silu)
                    kb.tt("dve", aT[:, fl, 0:NTOK], sc[:, 0:NTOK], pu[:, 0:NTOK], ALU.mult)
                def do_down(blk=blk, aT=aT, down=down, nf=nf):
                    for ti in range(sq.TPB):
                        t = blk * sq.TPB + ti
                        py = self.ps[0:P, 4 + 2 * (t % 2):6 + 2 * (t % 2), :]
                        for cb in range(2):
                            for fl in range(nf):
                                kb.mm(py[:, cb, :], lhsT=aT[:, fl, ti * 128:ti * 128 + P],
                                      rhs=down[:, fl, cb * 512:(cb + 1) * 512], start=(fl == 0), stop=(fl == nf - 1))
                        xrow = self.X[0:P, t, :]
                        kb.tt("dve", xrow, py.rearrange("p a b -> p (a b)"), xrow, ALU.add)
                if pending is not None:
                    pending()
                pending = do_down
            if pending is not None:
                pending()
                pending = None
            pc = self.ps[0:2, 3, :]
            for fl in range(nf):
                kb.tr(pc[:, fl * 128:(fl + 1) * 128], carry[:, g * 4 + fl, :], self.identf)
            kb.cp("act", cst[:, 0:nf * 128], pc[:, 0:nf * 128])
            dst = self.fcp[l, sq.b, :, g * 512:g * 512 + nf * 128] if isp else self.fcs[l, :, g * 512:g * 512 + nf * 128]
            kb.dma("sp", dst, cst[:, 0:nf * 128])
            if g + 2 < 6:
                fpan[g + 2] = self.load_ffn_group(l, g + 2, 4 if (g % 2 == 0) else 0)


_PROG = None


def _get_prog():
    global _PROG
    if _PROG is None:
        _PROG = Prog()
    return _PROG


def _host_consts(inp):
    f32 = np.float32
    rep = np.zeros((L, 128, NREP), f32)
    fm = np.zeros((L, 128, NFM), f32)
    for l in range(L):
        v = np.concatenate([inp["da_q_norm_g"][l], inp["da_k_norm_g"][l], inp["da_subln_g"][l], inp["gm_norm_g"][l],
                            inp["xq_norm_g"][l], inp["xk_norm_g"][l], inp["lambda_q1"][l], inp["lambda_k1"][l],
                            inp["lambda_q2"][l], inp["lambda_k2"][l]]).astype(f32)
        rep[l] = np.broadcast_to(v[None, :], (128, NREP))
        for i, nm in enumerate(["norm_mix_g", "norm_x_g", "norm_ffn_g", "norm_mem_g"]):
            fm[l, :, i * 8:(i + 1) * 8] = inp[nm][l].reshape(8, 128).T
        fm[l, :, 32:36] = inp["gm_b"][l].T
        cw = inp["conv_w"][l].reshape(3, NFC, 128)
        fm[l, :, 36:102] = cw.transpose(2, 1, 0).reshape(128, NFC * 3)
        fm[l, :, 102:124] = inp["conv_b"][l].reshape(NFC, 128).T
    wst = np.ascontiguousarray(inp["gm_w_s"].transpose(0, 3, 1, 2)).reshape(L, 128, 512).astype(f32)
    ident = np.eye(128, dtype=f32)
    tril = np.triu(np.ones((128, 128), f32))
    half = 8
    inv = (np.float32(500000.0) ** (-np.arange(half, dtype=f32) / np.float32(half))).astype(f32)

    def cs(pos):
        ang = pos.astype(f32)[:, None] * inv[None, :]
        return np.concatenate([np.cos(ang), np.sin(ang)], axis=1).astype(f32)

    csp = cs(np.arange(S)).reshape(16, 128, 16).transpose(1, 0, 2)
    css = cs(PAST + np.arange(DS))
    return dict(rep=rep, fm=fm, wst=wst, ident=ident, tril=tril, csp=np.ascontiguousarray(csp), css=css)


def _in_maps(inp, cores):
    hc = _host_consts(inp)
    in_maps = []
    for c in cores:
        m = dict(hc)
        m["xp"] = np.ascontiguousarray(inp["x_prompt"][c * NB:(c + 1) * NB])
        m["xs"] = np.ascontiguousarray(inp["x_sample"][c])
        m["ck"] = np.ascontiguousarray(inp["cache_da_k"][:, c].reshape(L, PAST, 512))
        m["cv"] = np.ascontiguousarray(inp["cache_da_v"][:, c].reshape(L, PAST, 512))
        m["cmk"] = np.ascontiguousarray(inp["cache_mem_k"][:, c].reshape(L, MEM, D))
        m["cmv"] = np.ascontiguousarray(inp["cache_mem_v"][:, c].reshape(L, MEM, D))
        m["memp"] = np.ascontiguousarray(inp["mem_prompt"][c * NB:(c + 1) * NB])
        st = inp["state_ffn_conv"][:, c]
        m["convst"] = np.ascontiguousarray(st.reshape(L, 2, NFC, 128).transpose(3, 0, 2, 1))
        m["w_in"] = inp["w_in"]; m["w_out"] = inp["w_out"]
        m["wq"] = inp["wq_c"]; m["wk"] = inp["wk_c"]; m["wv"] = inp["wv_c"]; m["wo"] = inp["wo_c"]
        m["w_up"] = inp["w_up"]; m["w_down"] = inp["w_down"]
        in_maps.append(m)
    return in_maps


def kernel(**inp):
    inp = {k: np.asarray(v) for k, v in inp.items()}
    prog = _get_prog()
    n = 8
    in_maps = _in_maps(inp, range(n))
    res = run_bass_kernel_spmd(prog.nc, in_maps, core_ids=list(range(n))).results
    cat = lambda k, ax: np.concatenate([r[k] for r in res], axis=ax)
    y_p = cat("yp", 0)
    y_s = np.stack([r["ys"] for r in res], 0)
    dk_p = cat("dkp", 1).reshape(L, 32, S, 4, 2, 64)
    dv_p = cat("dvp", 1).reshape(L, 32, S, 4, 128)
    mk_p = cat("mkp", 1).reshape(L, 32, MEM, 4, 256)
    mv_p = cat("mvp", 1).reshape(L, 32, MEM, 4, 256)
    fc_p = cat("fcp", 1)
    dk_s = np.stack([r["dks"] for r in res], 1).reshape(L, 8, DS, 4, 2, 64)
    dv_s = np.stack([r["dvs"] for r in res], 1).reshape(L, 8, DS, 4, 128)
    gv_s = np.stack([r["gvs"] for r in res], 1).reshape(L, 8, DS, 4, 128)
    fc_s = np.stack([r["fcs"] for r in res], 1)
    return (y_p, y_s, dk_p, dv_p, mk_p, mv_p, fc_p, dk_s, dv_s, gv_s, fc_s)
```
